# Optimizing a Trainium2 kernel written in Bass

```python
import math
import jax, jax.numpy as jnp
from jax import lax
import numpy as np

D_MODEL = 1024
BATCH = 8
SEQ = 2048
DEPTH = 4

HEAD_DIM = 64
N_HEADS = D_MODEL // HEAD_DIM
FOX_HEADS = N_HEADS // 2
NSA_HEADS = N_HEADS - FOX_HEADS
NSA_KV_GROUPS = 2
NSA_GQA = NSA_HEADS // NSA_KV_GROUPS
SB_HEADS = N_HEADS
CMP_BLOCK = 32
CMP_STRIDE = 16
SLC_BLOCK = 64
SLC_TOPK = 8
WINDOW = 256
Q_BLOCK = 128
N_BUCKETS = 32
MAX_DISTANCE = 128
XA_HEADS = 4
XA_HEAD_DIM = 128
XA_WIDTH = XA_HEADS * XA_HEAD_DIM
MEM_LEN = 256
D_FF = 4 * D_MODEL
EPS = 1e-6
NEG = -1e30
FORCED_SCORE = 1e4
FORGET_BIAS_INIT = 4.0

FOX_W = FOX_HEADS * HEAD_DIM
NSA_W = NSA_HEADS * HEAD_DIM
KV_W = NSA_KV_GROUPS * HEAD_DIM
EVEN_SPLITS = [FOX_W, FOX_W, FOX_W, FOX_HEADS, NSA_W, KV_W, KV_W, KV_W, KV_W, KV_W, KV_W, 3 * NSA_HEADS]
EVEN_PROJ = sum(EVEN_SPLITS)
EVEN_OFFSETS = [int(o) for o in np.cumsum(EVEN_SPLITS)[:-1]]
ODD_PROJ = 3 * SB_HEADS * HEAD_DIM

kernel_name = 'hybrid_fox_nsa_stickbreaking_trunk'


def rmsnorm(x, g):
    xf = x.astype(jnp.float32)
    y = xf * lax.rsqrt(jnp.mean(xf * xf, axis=-1, keepdims=True) + EPS)
    return (y * g.astype(jnp.float32)).astype(x.dtype)


def t5_bucket(dist):
    dist = jnp.maximum(dist, 0)
    max_exact = N_BUCKETS // 2
    d_f = jnp.maximum(dist, 1).astype(jnp.float32)
    large = max_exact + (jnp.log(d_f / max_exact) / math.log(MAX_DISTANCE / max_exact)
                         * (N_BUCKETS - max_exact)).astype(jnp.int32)
    large = jnp.minimum(large, N_BUCKETS - 1)
    return jnp.where(dist < max_exact, dist, large)


def _heads(a, n):
    return a.reshape(a.shape[0], a.shape[1], n, -1)


def fox_attention(q, k, v, logf):
    B, T, H, dh = q.shape
    scale = dh ** -0.5
    F = jnp.cumsum(logf.astype(jnp.float32), axis=1).transpose(0, 2, 1)
    outs = []
    for q0 in range(0, T, Q_BLOCK):
        q1 = q0 + Q_BLOCK
        s = jnp.einsum('bqhd,bkhd->bhqk', q[:, q0:q1], k[:, :q1]).astype(jnp.float32) * scale
        s = s + F[:, :, q0:q1, None] - F[:, :, None, :q1]
        causal = np.arange(q0, q1)[:, None] >= np.arange(q1)[None, :]
        p = jax.nn.softmax(jnp.where(causal, s, NEG), axis=-1).astype(v.dtype)
        outs.append(jnp.einsum('bhqk,bkhd->bqhd', p, v[:, :q1]))
    return jnp.concatenate(outs, axis=1)


def stick_breaking_attention(q, k, v):
    B, T, H, dh = q.shape
    scale = dh ** -0.5
    outs = []
    for q0 in range(0, T, Q_BLOCK):
        q1 = q0 + Q_BLOCK
        z = jnp.einsum('bqhd,bkhd->bhqk', q[:, q0:q1], k[:, :q1]).astype(jnp.float32) * scale
        strict = np.arange(q0, q1)[:, None] > np.arange(q1)[None, :]
        log_1mb = jnp.where(strict, jax.nn.log_sigmoid(-z), 0.0)
        later = lax.cumsum(log_1mb, axis=3, reverse=True) - log_1mb
        a = jnp.where(strict, jnp.exp(jax.nn.log_sigmoid(z) + later), 0.0).astype(v.dtype)
        outs.append(jnp.einsum('bhqk,bkhd->bqhd', a, v[:, :q1]))
    return jnp.concatenate(outs, axis=1)


def _compress(x, pe, w1, w2):
    T = x.shape[1]
    n_cmp = (T - CMP_BLOCK) // CMP_STRIDE + 1
    idx = np.arange(n_cmp)[:, None] * CMP_STRIDE + np.arange(CMP_BLOCK)[None, :]
    xb = x[:, idx] + pe[None, None, :, None, :]
    h = jax.nn.gelu(jnp.einsum('bclgd,lde->bcge', xb, w1))
    return jnp.einsum('bcge,ef->bgcf', h, w2)


def nsa_attention(q, k_cmp, v_cmp, k_slc, v_slc, k_win, v_win, gate_logits,
                  pe_k, w1_k, w2_k, pe_v, w1_v, w2_v, rel_bias):
    B, T, H, dh = q.shape
    G, R = NSA_KV_GROUPS, NSA_GQA
    scale = dh ** -0.5
    qg = q.reshape(B, T, G, R, dh).transpose(0, 2, 3, 1, 4)
    kc = _compress(k_cmp, pe_k, w1_k, w2_k)
    vc = _compress(v_cmp, pe_v, w1_v, w2_v)
    n_cmp = kc.shape[2]
    cmp_start = np.arange(n_cmp) * CMP_STRIDE
    cmp_end = cmp_start + CMP_BLOCK - 1
    n_slc = T // SLC_BLOCK
    n_top = min(SLC_TOPK, n_slc)
    slc_start = np.arange(n_slc) * SLC_BLOCK
    slc_end = slc_start + SLC_BLOCK - 1
    overlap = jnp.asarray(((cmp_start[:, None] <= slc_end[None, :]) &
                           (cmp_end[:, None] >= slc_start[None, :])).astype(np.float32))
    ks = k_slc.reshape(B, n_slc, SLC_BLOCK, G, dh).transpose(0, 3, 1, 2, 4)
    vs = v_slc.reshape(B, n_slc, SLC_BLOCK, G, dh).transpose(0, 3, 1, 2, 4)
    kw = jnp.pad(k_win.transpose(0, 2, 1, 3), ((0, 0), (0, 0), (WINDOW, 0), (0, 0)))
    vw = jnp.pad(v_win.transpose(0, 2, 1, 3), ((0, 0), (0, 0), (WINDOW, 0), (0, 0)))
    gates = jax.nn.sigmoid(gate_logits.astype(jnp.float32)).reshape(B, T, G, R, 3).transpose(0, 2, 3, 1, 4)
    tab = rel_bias.astype(jnp.float32).T.reshape(G, R, N_BUCKETS)
    wdist = np.arange(Q_BLOCK)[:, None] + WINDOW - np.arange(WINDOW + Q_BLOCK)[None, :]
    w_in_band = (wdist >= 0) & (wdist < WINDOW)
    wdist_j = jnp.asarray(wdist, dtype=jnp.int32)
    w_bias = tab[:, :, t5_bucket(wdist_j)]
    cmp_end_j = jnp.asarray(cmp_end, dtype=jnp.int32)
    blk = jnp.arange(n_slc, dtype=jnp.int32)
    b_ix = jnp.arange(B)[:, None, None, None]
    g_ix = jnp.arange(G)[None, :, None, None]
    g5 = jnp.arange(G)[None, :, None, None, None]
    r5 = jnp.arange(R)[None, None, :, None, None]

    def attend(q0):
        t = q0 + jnp.arange(Q_BLOCK, dtype=jnp.int32)
        qb = lax.dynamic_slice_in_dim(qg, q0, Q_BLOCK, axis=3)
        cdist = t[:, None] - cmp_end_j[None, :]
        cvalid = cdist >= 0
        sc = jnp.einsum('bgrqd,bgcd->bgrqc', qb, kc).astype(jnp.float32) * scale + tab[:, :, t5_bucket(cdist)]
        pc = jnp.where(cvalid, jax.nn.softmax(jnp.where(cvalid, sc, NEG), axis=-1), 0.0)
        o_cmp = jnp.einsum('bgrqc,bgcd->bgrqd', pc.astype(vc.dtype), vc)
        score = jnp.einsum('bgrqc,cn->bgqn', pc, overlap)
        cur = t // SLC_BLOCK
        forced = (blk[None, :] == 0) | (blk[None, :] == cur[:, None]) | (blk[None, :] == cur[:, None] - 1)
        future = blk[None, :] > cur[:, None]
        score = jnp.where(future, -1.0, jnp.where(forced, FORCED_SCORE, score))
        _, sel = lax.top_k(score, n_top)
        kg = ks[b_ix, g_ix, sel].reshape(B, G, Q_BLOCK, n_top * SLC_BLOCK, dh)
        vg = vs[b_ix, g_ix, sel].reshape(B, G, Q_BLOCK, n_top * SLC_BLOCK, dh)
        pos = (sel[..., None] * SLC_BLOCK + jnp.arange(SLC_BLOCK, dtype=jnp.int32)).reshape(B, G, Q_BLOCK, -1)
        sdist = t[None, None, :, None] - pos
        svalid = (sdist >= 0)[:, :, None]
        ss = (jnp.einsum('bgrqd,bgqkd->bgrqk', qb, kg).astype(jnp.float32) * scale
              + tab[g5, r5, t5_bucket(sdist)[:, :, None]])
        p_s = jax.nn.softmax(jnp.where(svalid, ss, NEG), axis=-1)
        o_slc = jnp.einsum('bgrqk,bgqkd->bgrqd', p_s.astype(vg.dtype), vg)
        kb = lax.dynamic_slice_in_dim(kw, q0, WINDOW + Q_BLOCK, axis=2)
        vb = lax.dynamic_slice_in_dim(vw, q0, WINDOW + Q_BLOCK, axis=2)
        wvalid = w_in_band & ((t[:, None] - wdist_j) >= 0)
        sw = jnp.einsum('bgrqd,bgkd->bgrqk', qb, kb).astype(jnp.float32) * scale + w_bias
        p_w = jax.nn.softmax(jnp.where(wvalid, sw, NEG), axis=-1)
        o_win = jnp.einsum('bgrqk,bgkd->bgrqd', p_w.astype(vb.dtype), vb)
        g = lax.dynamic_slice_in_dim(gates, q0, Q_BLOCK, axis=3)
        return (g[..., 0:1] * o_cmp + g[..., 1:2] * o_slc + g[..., 2:3] * o_win).astype(q.dtype)

    outs = lax.map(attend, jnp.arange(0, T, Q_BLOCK, dtype=jnp.int32))
    return outs.transpose(1, 0, 4, 2, 3, 5).reshape(B, T, H, dh)


def memory_cross_attention(h, mem_n, wq, wkv, wo):
    B, T, _ = h.shape
    q = _heads(h @ wq, XA_HEADS)
    k, v = jnp.split(mem_n @ wkv, 2, axis=-1)
    k, v = _heads(k, XA_HEADS), _heads(v, XA_HEADS)
    s = jnp.einsum('bthd,bmhd->bhtm', q, k).astype(jnp.float32) * (XA_HEAD_DIM ** -0.5)
    p = jax.nn.softmax(s, axis=-1).astype(v.dtype)
    o = jnp.einsum('bhtm,bmhd->bthd', p, v).reshape(B, T, XA_WIDTH)
    return o @ wo


def setup_inputs(seed: int = 0) -> dict:
    key = jax.random.key(seed)
    keys = iter(jax.random.split(key, 32))

    def nrm(shape, scale):
        return jax.random.normal(next(keys), shape, jnp.float32) * scale

    def gain(shape):
        return 1.0 + nrm(shape, 0.05)

    ne = (DEPTH + 1) // 2
    no = DEPTH // 2
    d = D_MODEL
    return {
        'x': nrm((BATCH, SEQ, d), 1.0),
        'mem': nrm((BATCH, MEM_LEN, d), 1.0),
        'rel_bias': nrm((N_BUCKETS, NSA_HEADS), 0.2),
        'mem_norm_g': gain((d,)),
        'norm_mix_g': gain((DEPTH, d)),
        'norm_xattn_g': gain((DEPTH, d)),
        'norm_mlp_g': gain((DEPTH, d)),
        'final_norm_g': gain((d,)),
        'w_in_even': nrm((ne, d, EVEN_PROJ), d ** -0.5),
        'b_forget': FORGET_BIAS_INIT + nrm((ne, FOX_HEADS), 0.5),
        'cmp_pe_k': nrm((ne, CMP_BLOCK, HEAD_DIM), 0.02),
        'cmp_w1_k': nrm((ne, CMP_BLOCK, HEAD_DIM, HEAD_DIM), (CMP_BLOCK * HEAD_DIM) ** -0.5),
        'cmp_w2_k': nrm((ne, HEAD_DIM, HEAD_DIM), HEAD_DIM ** -0.5),
        'cmp_pe_v': nrm((ne, CMP_BLOCK, HEAD_DIM), 0.02),
        'cmp_w1_v': nrm((ne, CMP_BLOCK, HEAD_DIM, HEAD_DIM), (CMP_BLOCK * HEAD_DIM) ** -0.5),
        'cmp_w2_v': nrm((ne, HEAD_DIM, HEAD_DIM), HEAD_DIM ** -0.5),
        'w_out_even': nrm((ne, FOX_W + NSA_W, d), (FOX_W + NSA_W) ** -0.5),
        'w_in_odd': nrm((no, d, ODD_PROJ), d ** -0.5),
        'w_out_odd': nrm((no, SB_HEADS * HEAD_DIM, d), (SB_HEADS * HEAD_DIM) ** -0.5),
        'xa_wq': nrm((DEPTH, d, XA_WIDTH), d ** -0.5),
        'xa_wkv': nrm((DEPTH, d, 2 * XA_WIDTH), d ** -0.5),
        'xa_wo': nrm((DEPTH, XA_WIDTH, d), XA_WIDTH ** -0.5),
        'mlp_w1': nrm((DEPTH, d, D_FF), d ** -0.5),
        'mlp_w2': nrm((DEPTH, D_FF, d), D_FF ** -0.5),
    }


def reference(x, mem, rel_bias, mem_norm_g, norm_mix_g, norm_xattn_g, norm_mlp_g, final_norm_g,
              w_in_even, b_forget, cmp_pe_k, cmp_w1_k, cmp_w2_k, cmp_pe_v, cmp_w1_v, cmp_w2_v,
              w_out_even, w_in_odd, w_out_odd, xa_wq, xa_wkv, xa_wo, mlp_w1, mlp_w2):
    B, T, _ = x.shape
    mem_n = rmsnorm(mem, mem_norm_g)
    for layer in range(DEPTH):
        h = rmsnorm(x, norm_mix_g[layer])
        if layer % 2 == 0:
            e = layer // 2
            (fq, fk, fv, ff, nq, kc, vc, ksl, vsl, kwn, vwn, ng) = jnp.split(h @ w_in_even[e], EVEN_OFFSETS, axis=-1)
            logf = jax.nn.log_sigmoid(ff.astype(jnp.float32) + b_forget[e].astype(jnp.float32))
            fox = fox_attention(_heads(fq, FOX_HEADS), _heads(fk, FOX_HEADS), _heads(fv, FOX_HEADS), logf)
            nsa = nsa_attention(_heads(nq, NSA_HEADS),
                                _heads(kc, NSA_KV_GROUPS), _heads(vc, NSA_KV_GROUPS),
                                _heads(ksl, NSA_KV_GROUPS), _heads(vsl, NSA_KV_GROUPS),
                                _heads(kwn, NSA_KV_GROUPS), _heads(vwn, NSA_KV_GROUPS), ng,
                                cmp_pe_k[e], cmp_w1_k[e], cmp_w2_k[e],
                                cmp_pe_v[e], cmp_w1_v[e], cmp_w2_v[e], rel_bias)
            mixed = jnp.concatenate([fox.reshape(B, T, FOX_W), nsa.reshape(B, T, NSA_W)], axis=-1) @ w_out_even[e]
        else:
            o = layer // 2
            sq, sk, sv = jnp.split(h @ w_in_odd[o], 3, axis=-1)
            sb = stick_breaking_attention(_heads(sq, SB_HEADS), _heads(sk, SB_HEADS), _heads(sv, SB_HEADS))
            mixed = sb.reshape(B, T, SB_HEADS * HEAD_DIM) @ w_out_odd[o]
        x = x + mixed
        x = x + memory_cross_attention(rmsnorm(x, norm_xattn_g[layer]), mem_n,
                                       xa_wq[layer], xa_wkv[layer], xa_wo[layer])
        hm = rmsnorm(x, norm_mlp_g[layer])
        x = x + jnp.square(jax.nn.relu(hm @ mlp_w1[layer])) @ mlp_w2[layer]
    return rmsnorm(x, final_norm_g)
```

```python
from contextlib import ExitStack
import numpy as np
import concourse.bass as bass
import concourse.mybir as mybir

F32 = mybir.dt.float32
BF16 = mybir.dt.bfloat16
AF = mybir.ActivationFunctionType
ALU = mybir.AluOpType
DT_SIZE = {F32: 4, BF16: 2}
SEM_LIMIT = 16000


class View:
    __slots__ = ("ap", "region", "box")

    def __init__(self, ap, region=None, box=None):
        self.ap = ap
        self.region = region
        self.box = box


def _ovl(a, b):
    return a[0] < b[1] and b[0] < a[1] and a[2] < b[3] and b[2] < a[3]


def _cov(a, b):
    return a[0] <= b[0] and a[1] >= b[1] and a[2] <= b[2] and a[3] >= b[3]


class Buf:
    def __init__(self, ap, name, shape, dtype, region=None, base=0):
        self.h = ap
        self.name = name
        self.shape = list(shape)
        self.dtype = dtype
        self.esz = DT_SIZE[dtype]
        self.region = region or name
        self.base = base
        st = [1] * len(shape)
        for i in range(len(shape) - 2, 0, -1):
            st[i] = st[i + 1] * shape[i + 1]
        self.strides = st

    def __getitem__(self, idx):
        if not isinstance(idx, tuple):
            idx = (idx,)
        idx = list(idx) + [slice(None)] * (len(self.shape) - len(idx))
        lo = 0
        hi = 0
        p0, p1 = 0, self.shape[0]
        for d, (ix, n) in enumerate(zip(idx, self.shape)):
            if isinstance(ix, int):
                a, b, s = ix, ix + 1, 1
            else:
                a, b, s = ix.indices(n)
                cnt = max(0, (b - a + s - 1) // s)
                b = a + (cnt - 1) * s + 1
            assert 0 <= a < b <= n, (self.name, idx, self.shape)
            if d == 0:
                p0, p1 = a, b
            else:
                lo += a * self.strides[d]
                hi += (b - 1) * self.strides[d]
        box = (p0, p1, self.base + lo * self.esz, self.base + (hi + 1) * self.esz)
        return View(self.h[tuple(idx)], self.region, box)

    def full(self):
        return self[tuple(slice(None) for _ in self.shape)]

    def sub(self, c0, c1, pattern=None, **kw):
        assert len(self.shape) == 2
        ap = self.h[:, c0:c1]
        shape = [self.shape[0], c1 - c0]
        if pattern is not None:
            ap = ap.rearrange(pattern, **kw)
            shape = list(ap.shape)
        return Buf(ap, self.name, shape, self.dtype, region=self.region, base=self.base + c0 * self.esz)


class Op:
    __slots__ = ("eng", "fn", "tl", "idx", "deps", "waits", "signal", "semkey", "semval", "dma")


class Prog:
    ENG = ("pe", "act", "dve", "pool", "sp")

    def __init__(self, nc, nslots=8):
        self.nc = nc
        self.ops = []
        self.acc = {}
        self.tl_ops = {}
        self.dma_rr = {"sp": 0, "pool": 0, "act": 0}
        self.nslots = nslots
        self.stack = ExitStack()
        self.sb_bytes = 0

    def sbuf(self, name, shape, dtype):
        h = self.stack.enter_context(self.nc.sbuf_tensor("sb_" + name, list(shape), dtype))
        n = 1
        for s in shape[1:]:
            n *= s
        self.sb_bytes += n * DT_SIZE[dtype]
        return Buf(h[tuple(slice(None) for _ in shape)], name, shape, dtype)

    def psum(self, name, shape, dtype=F32):
        h = self.stack.enter_context(self.nc.psum_tensor(name, list(shape), dtype))
        return Buf(h[tuple(slice(None) for _ in shape)], name, shape, dtype)

    def add(self, eng, fn, reads=(), writes=(), dma=False):
        op = Op()
        opid = len(self.ops)
        op.eng = eng
        op.fn = fn
        op.dma = dma
        if dma:
            slot = self.dma_rr[eng] % self.nslots
            self.dma_rr[eng] += 1
            tl = ("dma", eng, slot)
        else:
            tl = eng
        lst = self.tl_ops.setdefault(tl, [])
        op.tl = tl
        op.idx = len(lst)
        deps = set()
        if dma and lst:
            deps.add(lst[-1])
        lst.append(opid)
        for v in reads:
            if v.region is None:
                continue
            W, Rd = self.acc.setdefault(v.region, ([], {}))
            for box, o in W:
                if _ovl(box, v.box):
                    deps.add(o)
        for v in writes:
            if v.region is None:
                continue
            W, Rd = self.acc.setdefault(v.region, ([], {}))
            for box, o in W:
                if _ovl(box, v.box):
                    deps.add(o)
            for (t, box), o in Rd.items():
                if _ovl(box, v.box):
                    deps.add(o)
        for v in writes:
            if v.region is None:
                continue
            W, Rd = self.acc[v.region]
            W[:] = [(b, o) for (b, o) in W if not _cov(v.box, b)]
            for k in [k for k in Rd if _cov(v.box, k[1])]:
                del Rd[k]
            W.append((v.box, opid))
        for v in reads:
            if v.region is None:
                continue
            self.acc[v.region][1][(tl, v.box)] = opid
        op.deps = deps
        op.waits = []
        op.signal = dma
        self.ops.append(op)
        return op

    def mm(self, out, lhsT, rhs, start=True, stop=True, **kw):
        rd = [lhsT, rhs] + ([] if start else [out])
        return self.add("pe", lambda e: e.matmul(out.ap, lhsT.ap, rhs.ap, start=start, stop=stop, **kw),
                        reads=rd, writes=[out])

    def transpose(self, out, in_, ident):
        return self.add("pe", lambda e: e.transpose(out.ap, in_.ap, ident.ap), reads=[in_, ident], writes=[out])

    def act(self, out, in_, func, bias=None, scale=1.0, accum=None):
        rd = [in_]
        kw = {}
        if bias is not None:
            if isinstance(bias, View):
                rd.append(bias)
                kw["bias"] = bias.ap
            else:
                kw["bias"] = bias
        if isinstance(scale, View):
            rd.append(scale)
            kw["scale"] = scale.ap
        else:
            kw["scale"] = scale
        wr = [out]
        if accum is not None:
            kw["accum_out"] = accum.ap
            wr.append(accum)
        return self.add("act", lambda e: e.activation(out.ap, in_.ap, func, **kw), reads=rd, writes=wr)

    def tt(self, eng, out, in0, in1, op):
        return self.add(eng, lambda e: e.tensor_tensor(out.ap, in0.ap, in1.ap, op), reads=[in0, in1], writes=[out])

    def ts(self, eng, out, in0, s1, op0, s2=None, op1=None, accum=None):
        rd = [in0]
        a1 = s1
        a2 = s2
        if isinstance(s1, View):
            rd.append(s1)
            a1 = s1.ap
        if isinstance(s2, View):
            rd.append(s2)
            a2 = s2.ap
        kw = {}
        if op1 is not None:
            kw["op1"] = op1
        wr = [out]
        if accum is not None:
            kw["accum_out"] = accum.ap
            wr.append(accum)
        return self.add(eng, lambda e: e.tensor_scalar(out.ap, in0.ap, a1, a2, op0, **kw), reads=rd, writes=wr)

    def stt(self, eng, out, in0, scalar, in1, op0, op1):
        rd = [in0, in1]
        sc = scalar
        if isinstance(scalar, View):
            rd.append(scalar)
            sc = scalar.ap
        return self.add(eng, lambda e: e.scalar_tensor_tensor(out.ap, in0.ap, sc, in1.ap, op0, op1),
                        reads=rd, writes=[out])

    def copy(self, eng, out, in_):
        if eng == "act":
            return self.add("act", lambda e: e.copy(out.ap, in_.ap), reads=[in_], writes=[out])
        return self.add(eng, lambda e: e.tensor_copy(out.ap, in_.ap), reads=[in_], writes=[out])

    def recip(self, out, in_):
        return self.add("dve", lambda e: e.reciprocal(out.ap, in_.ap), reads=[in_], writes=[out])

    def memset(self, eng, out, val):
        return self.add(eng, lambda e: e.memset(out.ap, val), writes=[out])

    def dma(self, issuer, out, in_):
        return self.add(issuer, lambda e: e.dma_start(out=out.ap, in_=in_.ap), reads=[in_], writes=[out], dma=True)

    def finish(self, final_issuer="sp"):
        nc = self.nc
        ops = self.ops
        fin = Op()
        fin.eng = final_issuer
        fin.fn = None
        fin.dma = False
        fin.tl = None
        fin.idx = -1
        fin.deps = set(l[-1] for l in self.tl_ops.values() if l)
        fin.waits = []
        fin.signal = False
        ops.append(fin)

        clock = {e: {} for e in self.ENG}
        snap = {}
        for op in ops:
            ck = clock[op.eng]
            need = {}
            for d in op.deps:
                o = ops[d]
                if need.get(o.tl, -1) < o.idx:
                    need[o.tl] = o.idx
            for tl, idx in sorted(need.items(), key=lambda kv: str(kv[0])):
                if tl == "pe" and op.eng == "pe" and not op.dma:
                    continue
                if ck.get(tl, -1) >= idx:
                    continue
                op.waits.append((tl, idx))
                for k, v in snap[(tl, idx)].items():
                    if ck.get(k, -1) < v:
                        ck[k] = v
                if ck.get(tl, -1) < idx:
                    ck[tl] = idx
                ops[self.tl_ops[tl][idx]].signal = True
            if op.tl is not None:
                s = dict(ck)
                s[op.tl] = op.idx
                snap[(op.tl, op.idx)] = s
        semkeys = set()
        for tl, lst in self.tl_ops.items():
            k = 0
            for oid in lst:
                o = ops[oid]
                if isinstance(tl, tuple):
                    o.semkey = (tl, 0)
                    o.semval = 16 * (o.idx + 1)
                    semkeys.add(o.semkey)
                elif o.signal:
                    o.semkey = (tl, k // SEM_LIMIT)
                    o.semval = k % SEM_LIMIT + 1
                    semkeys.add(o.semkey)
                    k += 1
        st = self.stack
        sems = {}
        for i, key in enumerate(sorted(semkeys, key=str)):
            sems[key] = st.enter_context(nc.semaphore("s%d" % i))
        self.n_sems = len(sems)
        block = st.enter_context(nc.Block())
        eng_ops = {e: [o for o in ops if o.eng == e] for e in self.ENG}
        tl_ops = self.tl_ops

        def mk(ename):
            def body(e):
                for op in eng_ops[ename]:
                    for tl, idx in op.waits:
                        t = ops[tl_ops[tl][idx]]
                        e.wait_ge(sems[t.semkey], t.semval)
                    if op.fn is not None:
                        ins = op.fn(e)
                        if op.signal:
                            ins.then_inc(sems[op.semkey], 16 if op.dma else 1)
            return body

        block.tensor(mk("pe"))
        block.scalar(mk("act"))
        block.vector(mk("dve"))
        block.gpsimd(mk("pool"))
        block.sync(mk("sp"))
        st.close()
        return nc


from concourse.bass_utils import run_bass_kernel_spmd

T = 2048
D = 1024
NCK = 8
CW = 512
NCH = T // CW
MEM = 256
NEGM = -30000.0
EPS = 1e-6


class Ctx:
    pass


def V_(ap):
    return View(ap)


def setup(nc, dbg_out=None):
    K = Ctx()
    K.nc = nc
    P = K.P = Prog(nc)
    din = lambda n, s: nc.dram_tensor(n, list(s), F32, kind="ExternalInput").ap()
    K.d = d = {}
    for n, s in [("x", (T, D)), ("mem", (MEM, D)), ("gains", (128, 14, 8)),
                 ("w_in_even", (2, D, 2848)), ("w_out_even", (2, D, D)), ("w_in_odd", (2, D, 3072)),
                 ("w_out_odd", (2, D, D)), ("xa_wq", (4, D, 512)), ("xa_wkv", (4, D, 1024)),
                 ("xa_wo", (4, 512, D)), ("mlp_w1", (4, D, 4096)), ("mlp_w2", (4, 4096, D)),
                 ("b_forget", (2, 8)), ("cmp_w1_k", (2, 32, 64, 64)), ("cmp_w1_v", (2, 32, 64, 64)),
                 ("cmp_w2_k", (2, 64, 64)), ("cmp_w2_v", (2, 64, 64)),
                 ("cmp_peT_k", (2, 64, 32)), ("cmp_peT_v", (2, 64, 32)),
                 ("bf_rep", (2, 128, 128)), ("c_ident", (128, 128)), ("c_caus", (128, 128)), ("c_strict", (128, 128)),
                 ("c_negu", (128, 128)), ("c_usum", (128, 128)), ("c_expb", (32, 16, 128)),
                 ("c_ovl", (127, 33)), ("c_kb", (128, 64)), ("c_ab", (128, 64)), ("c_sel12", (12, 6, 128)),
                 ("t_tb", (128, 8, 2, 128)), ("t_wb", (128, 8, 384)), ("t_cb", (8, 127, T)), ("t_ch", (128, 8))]:
        d[n] = din(n, s)
    d["out"] = nc.dram_tensor("out", [T, D], F32, kind="ExternalOutput").ap()
    K.xT = P.sbuf("xT", [128, NCK, T], F32)
    K.hT = P.sbuf("hT", [128, NCK, T], BF16)
    K.KTf = P.sbuf("KT", [128, 2 * T], BF16)
    K.VTf = P.sbuf("VT", [128, 2 * T], BF16)
    K.qT = P.sbuf("qT", [128, 2, CW], BF16)
    K.oT = P.sbuf("oT", [128, 4, CW], BF16)
    K.WL = [P.sbuf("WL%d" % i, [128, 2048], BF16) for i in range(3)]
    K.WS = [P.sbuf("WS%d" % i, [128, 2048], BF16) for i in range(3)]
    K.PT = [P.sbuf("PT%d" % i, [128, CW], BF16) for i in range(4)]
    K.FW = [P.sbuf("FW%d" % i, [128, CW], F32) for i in range(6)]
    K.rr = {}
    K.gains = P.sbuf("gains", [128, 14, 8], F32)
    K.identf = P.sbuf("identf", [128, 128], F32)
    K.ident = P.sbuf("ident", [128, 128], BF16)
    K.ones = P.sbuf("ones", [128, 128], BF16)
    K.memT = P.sbuf("memT", [128, NCK, MEM], BF16)
    K.eps_col = P.sbuf("eps_col", [128, 1], F32)
    K.ps = [P.psum("ps%d" % i, [128, CW]) for i in range(8)]
    K.ps3 = [Buf(b.h.rearrange("p (a b) -> p a b", a=4), b.name, [128, 4, 128], F32) for b in K.ps]
    K.roles = {"Y": [0, 1], "O": [2, 3], "Dn": [4, 5], "M": [6, 7]}
    return K


def rot(K, key, lst):
    i = K.rr.get(key, 0)
    K.rr[key] = i + 1
    return lst[i % len(lst)]


def bank(K, role):
    return K.ps[rot(K, "ps" + role, K.roles[role])]


def pt(K):
    return rot(K, "PT", K.PT)


def fw(K):
    return rot(K, "FW", K.FW)


def ws(K):
    return rot(K, "WS", K.WS)


def load_consts(K):
    P, d = K.P, K.d
    P.dma("sp", K.gains.full(), V_(d["gains"]))
    P.dma("sp", K.identf.full(), V_(d["c_ident"]))
    P.dma("pool", K.ident.full(), V_(d["c_ident"]))
    P.memset("dve", K.ones.full(), 1.0)
    P.memset("dve", K.eps_col.full(), EPS)


def load_T(K, src_d, dst, ntok, dst_is_bf16_norm=False):
    P = K.P
    for tt in range(ntok // 128):
        for half in range(2):
            st = fw(K)
            P.dma("sp", st.full(), V_(src_d[tt * 128:(tt + 1) * 128, half * 512:(half + 1) * 512]))
            bi = rot(K, "psM", K.roles["M"])
            for q in range(4):
                P.transpose(K.ps[bi][:, q * 128:(q + 1) * 128], st[:, q * 128:(q + 1) * 128], K.identf.full())
            eng = "dve" if (tt + half) % 2 == 0 else "act"
            P.copy(eng, dst[:, half * 4:(half + 1) * 4, tt * 128:(tt + 1) * 128], K.ps3[bi].full())


def store_T(K, src, dst_d):
    P = K.P
    for tt in range(T // 128):
        for half in range(2):
            bi = rot(K, "psM", K.roles["M"])
            for q in range(4):
                P.transpose(K.ps[bi][:, q * 128:(q + 1) * 128], src[:, half * 4 + q, tt * 128:(tt + 1) * 128],
                            K.identf.full())
            st = fw(K)
            eng = "dve" if (tt + half) % 2 == 0 else "act"
            P.copy(eng, st.full(), K.ps[bi].full())
            P.dma("sp", V_(dst_d[tt * 128:(tt + 1) * 128, half * 512:(half + 1) * 512]), st.full())


def rmsnorm(K, src, gidx, dst, ntok):
    P = K.P
    W = min(CW, ntok)
    for ch in range(ntok // W):
        cs = slice(ch * W, (ch + 1) * W)
        b = bank(K, "M")
        for c in range(NCK):
            sq = pt(K)
            P.act(sq[:, :W], src[:, c, cs], AF.Square)
            P.mm(b[:, :W], K.ones.full(), sq[:, :W], start=(c == 0), stop=(c == NCK - 1))
        r = fw(K)
        P.act(r[:, :W], b[:, :W], AF.Sqrt, bias=K.eps_col[:, 0:1], scale=1.0 / D)
        P.recip(r[:, :W], r[:, :W])
        for c in range(NCK):
            P.stt("dve", dst[:, c, cs], src[:, c, cs], K.gains[:, gidx, c:c + 1], r[:, :W], ALU.mult, ALU.mult)


def wload(K, buf, dram_ap, pattern=None, **kw):
    ap = dram_ap if pattern is None else dram_ap.rearrange(pattern, **kw)
    K.P.dma("pool", buf, V_(ap))


def resid_add(K, n, cs, psv):
    K.P.tt("dve", K.xT[:, n, cs], K.xT[:, n, cs], psv, ALU.add)


def mlp_phase(K, layer):
    P, d = K.P, K.d
    rmsnorm(K, K.xT, 8 + layer, K.hT, T)
    U = [K.KTf.sub(0, 2 * T, "p (a b) -> p a b", a=2), K.VTf.sub(0, 2 * T, "p (a b) -> p a b", a=2)]
    NG = 16

    def stage1(g):
        w1 = ws(K).sub(0, 2048, "p (c n) -> p c n", c=8)
        wload(K, w1.full(), d["mlp_w1"][layer, :, g * 256:(g + 1) * 256], "(c p) n -> p c n", p=128)
        u = U[g % 2]
        for fi in range(2):
            for ch in range(NCH):
                cs = slice(ch * CW, (ch + 1) * CW)
                b = bank(K, "Y")
                for c in range(NCK):
                    P.mm(b.full(), w1[:, c, fi * 128:(fi + 1) * 128], K.hT[:, c, cs], start=(c == 0), stop=(c == NCK - 1))
                r = fw(K)
                P.act(r.full(), b.full(), AF.Relu)
                P.tt("pool", u[:, fi, cs], r.full(), r.full(), ALU.mult)

    def stage2(g):
        w2 = ws(K).sub(0, 2048, "p (a n) -> p a n", a=2)
        wload(K, w2.full(), d["mlp_w2"][layer, g * 256:(g + 1) * 256, :], "(a p) n -> p a n", p=128)
        u = U[g % 2]
        for n in range(NCK):
            for ch in range(NCH):
                cs = slice(ch * CW, (ch + 1) * CW)
                b = bank(K, "O")
                for fi in range(2):
                    P.mm(b.full(), w2[:, fi, n * 128:(n + 1) * 128], u[:, fi, cs], start=(fi == 0), stop=(fi == 1))
                resid_add(K, n, cs, b.full())

    stage1(0)
    for g in range(NG):
        if g + 1 < NG:
            stage1(g + 1)
        stage2(g)


def xattn_phase(K, layer):
    P, d = K.P, K.d
    rmsnorm(K, K.xT, 4 + layer, K.hT, T)
    kx = K.WL[0].sub(0, 1024, "p (a m) -> p a m", a=4)
    vx = K.WL[0].sub(1024, 2048, "p (a n) -> p a n", a=2)
    for a in range(4):
        w = ws(K).sub(0, 1024, "p (c n) -> p c n", c=8)
        wload(K, w.full(), d["xa_wkv"][layer, :, a * 128:(a + 1) * 128], "(c p) n -> p c n", p=128)
        b = bank(K, "M")
        for c in range(NCK):
            P.mm(b[:, :MEM], w[:, c, :], K.memT[:, c, :], start=(c == 0), stop=(c == NCK - 1))
        P.copy("dve", kx[:, a, :], b[:, :MEM])
    for hv in range(2):
        w = ws(K).sub(0, 2048, "p (c n) -> p c n", c=8)
        wload(K, w.full(), d["xa_wkv"][layer, :, 512 + hv * 256:512 + (hv + 1) * 256], "(c p) n -> p c n", p=128)
        for mt in range(2):
            b = bank(K, "M")
            for c in range(NCK):
                P.mm(b[:, :256], K.memT[:, c, mt * 128:(mt + 1) * 128], w[:, c, :], start=(c == 0), stop=(c == NCK - 1))
            P.copy("dve", vx[:, mt, hv * 256:(hv + 1) * 256], b[:, :256])
    qx = [K.KTf.sub(0, 2 * T, "p (a b) -> p a b", a=2), K.VTf.sub(0, 2 * T, "p (a b) -> p a b", a=2)]
    for hp in range(2):
        w = ws(K).sub(0, 2048, "p (c n) -> p c n", c=8)
        wload(K, w.full(), d["xa_wq"][layer, :, hp * 256:(hp + 1) * 256], "(c p) n -> p c n", p=128)
        for a in range(2):
            for ch in range(NCH):
                cs = slice(ch * CW, (ch + 1) * CW)
                b = bank(K, "M")
                for c in range(NCK):
                    P.mm(b.full(), w[:, c, a * 128:(a + 1) * 128], K.hT[:, c, cs], start=(c == 0), stop=(c == NCK - 1))
                P.copy("act" if ch % 2 else "dve", qx[hp][:, a, cs], b.full())
    wo = [K.WL[1].sub(0, 2048, "p (a n) -> p a n", a=2), K.WL[2].sub(0, 2048, "p (a n) -> p a n", a=2)]
    for i in range(2):
        wload(K, wo[i].full(), d["xa_wo"][layer, i * 256:(i + 1) * 256, :], "(a p) n -> p a n", p=128)
    sc = 128.0 ** -0.5
    for ch in range(NCH):
        cs = slice(ch * CW, (ch + 1) * CW)
        for a in range(4):
            ob = bank(K, "O")
            db = bank(K, "Dn")
            for mt in range(2):
                y = bank(K, "Y")
                P.mm(y.full(), kx[:, a, mt * 128:(mt + 1) * 128], qx[a // 2][:, a % 2, cs])
                p_ = pt(K)
                P.act(p_.full(), y.full(), AF.Exp, scale=sc)
                P.mm(ob.full(), vx[:, mt, a * 128:(a + 1) * 128], p_.full(), start=(mt == 0), stop=(mt == 1))
                P.mm(db.full(), K.ones.full(), p_.full(), start=(mt == 0), stop=(mt == 1))
            rd = fw(K)
            P.recip(rd.full(), db.full())
            P.tt("dve", K.oT[:, a, :], ob.full(), rd.full(), ALU.mult)
        for n in range(NCK):
            b = bank(K, "M")
            for a in range(4):
                P.mm(b.full(), wo[a // 2][:, a % 2, n * 128:(n + 1) * 128], K.oT[:, a, :], start=(a == 0), stop=(a == 3))
            resid_add(K, n, cs, b.full())


def prep_mem(K):
    P = K.P
    memf = Buf(K.xT.h[:, :, 0:MEM], "xT", [128, NCK, MEM], F32, region="xT")
    memf.strides = [1, T, 1]
    load_T(K, K.d["mem"], memf, MEM)
    rmsnorm(K, memf, 13, K.memT, MEM)


def setup_mixer(K):
    P = K.P
    K.caus = P.sbuf("caus", [128, 128], BF16)
    K.strict = P.sbuf("strict", [128, 128], BF16)
    K.negu = P.sbuf("negu", [128, 128], BF16)
    K.zeros = P.sbuf("zeros", [128, 128], BF16)
    K.negrow = P.sbuf("negrow", [1, 128], BF16)
    K.RS = [P.sbuf("rs%d" % i, [1, CW], BF16) for i in range(2)]
    K.ACC = [P.sbuf("acc%d" % i, [128, CW], F32) for i in range(2)]
    wload(K, K.caus.full(), K.d["c_caus"])
    wload(K, K.strict.full(), K.d["c_strict"])
    wload(K, K.negu.full(), K.d["c_negu"])
    P.memset("dve", K.zeros.full(), 0.0)
    P.memset("dve", K.negrow.full(), -1.0)


def proj_fm(K, w, ncols_tile, dst_fn, scale=None):
    P = K.P
    for ch in range(NCH):
        cs = slice(ch * CW, (ch + 1) * CW)
        b = bank(K, "M")
        for c in range(NCK):
            P.mm(b.full(), w[:, c, ncols_tile], K.hT[:, c, cs], start=(c == 0), stop=(c == NCK - 1))
        P.copy("act" if ch % 2 else "dve", dst_fn(cs), b.full())


def proj_tm(K, w, ncols, VT):
    P = K.P
    for tt in range(T // 128):
        b = bank(K, "M")
        for c in range(NCK):
            P.mm(b[:, :ncols], K.hT[:, c, tt * 128:(tt + 1) * 128], w[:, c, :], start=(c == 0), stop=(c == NCK - 1))
        P.copy("act" if tt % 2 else "dve", VT[:, tt, 0:ncols], b[:, :ncols])


def q_proj(K, wq, cs):
    P = K.P
    for jp in range(2):
        b = bank(K, "M")
        for c in range(NCK):
            P.mm(b.full(), wq[:, c, jp * 128:(jp + 1) * 128], K.hT[:, c, cs], start=(c == 0), stop=(c == NCK - 1))
        P.ts("dve", K.qT[:, jp, :], b.full(), 0.125, ALU.mult)


def out_proj(K, wout, cs):
    P = K.P
    for n in range(NCK):
        b = bank(K, "M")
        for jp in range(2):
            P.mm(b.full(), wout[:, jp, n * 128:(n + 1) * 128], K.oT[:, jp, :], start=(jp == 0), stop=(jp == 1))
        resid_add(K, n, cs, b.full())


def w_tile(K, dram_cols):
    w = ws(K).sub(0, 1024, "p (c n) -> p c n", c=8)
    wload(K, w.full(), dram_cols, "(c p) n -> p c n", p=128)
    return w


def sb_pass(K, o, pg):
    P, d = K.P, K.d
    win, wo_d = d["w_in_odd"], d["w_out_odd"]
    KT = K.KTf.sub(0, 2 * T, "p (a b) -> p a b", a=2)
    VT = K.VTf.sub(0, 2 * T, "p (a b) -> p a b", a=16)
    for jp in range(2):
        c0 = 1024 + (2 * pg + jp) * 128
        w = w_tile(K, win[o, :, c0:c0 + 128])
        proj_fm(K, w, slice(0, 128), lambda cs, jp=jp: KT[:, jp, cs])
    wv = K.WL[0].sub(0, 2048, "p (c n) -> p c n", c=8)
    wload(K, wv.full(), win[o, :, 2048 + pg * 256:2048 + (pg + 1) * 256], "(c p) n -> p c n", p=128)
    proj_tm(K, wv, 256, VT)
    wq = K.WL[1].sub(0, 2048, "p (c n) -> p c n", c=8)
    wload(K, wq.full(), win[o, :, pg * 256:(pg + 1) * 256], "(c p) n -> p c n", p=128)
    wout = K.WL[2].sub(0, 2048, "p (a n) -> p a n", a=2)
    wload(K, wout.full(), wo_d[o, pg * 256:(pg + 1) * 256, :], "(a p) n -> p a n", p=128)
    for ch in range(NCH):
        cs = slice(ch * CW, (ch + 1) * CW)
        q_proj(K, wq, cs)
        for jp in range(2):
            ob = bank(K, "O")
            P.mm(ob.full(), K.zeros.full(), K.qT[:, jp, :], start=True, stop=False)
            for half in range(2):
                hs = slice(half * 64, half * 64 + 64)
                rps = bank(K, "Dn")
                P.mm(rps[0:1, :], K.zeros[:, 0:1], K.qT[:, jp, :], start=True, stop=False)
                ktop = 4 * ch + 3
                rs = None
                for kt in range(ktop, -1, -1):
                    n0 = max(0, kt - 4 * ch) * 128
                    N = CW - n0
                    diag = kt >= 4 * ch
                    y = bank(K, "Y")
                    P.mm(y[:, :N], KT[hs, jp, kt * 128:(kt + 1) * 128], K.qT[hs, jp, n0:CW], start=True, stop=not diag)
                    if diag:
                        P.mm(y[:, 0:128], K.ident.full(), K.strict.full(), start=False, stop=True)
                    E = fw(K)
                    P.act(E[:, :N], y[:, :N], AF.Exp)
                    Lb = pt(K)
                    P.act(Lb[:, :N], E[:, :N], AF.Ln, bias=1.0)
                    P.mm(y[:, :N], K.negu.full(), Lb[:, :N], start=False, stop=(rs is None))
                    if rs is not None:
                        P.mm(y[:, :N], K.negrow.full(), rs[0:1, n0:CW], start=False, stop=True)
                    A = pt(K)
                    P.act(A[:, :N], y[:, :N], AF.Exp)
                    P.mm(ob[hs, n0:CW], VT[:, kt, (2 * jp + half) * 64:(2 * jp + half + 1) * 64], A[:, :N],
                         start=False, stop=(kt == 0))
                    if kt > 0:
                        P.mm(rps[0:1, n0:CW], K.ones[:, 0:1], Lb[:, :N], start=False, stop=(kt == 1))
                        rs = rot(K, "RS", K.RS)
                        P.copy("dve", rs.full(), rps[0:1, :])
            P.copy("dve", K.oT[:, jp, :], ob.full())
        out_proj(K, wout, cs)


def fox_prep(K, e):
    P, d = K.P, K.d
    if not hasattr(K, "Ccol"):
        K.Ccol = P.sbuf("Ccol", [128, 128], F32)
        K.Coff = P.sbuf("Coff", [128, 128], F32)
        K.LF = P.sbuf("LF", [128, 128], F32)
        K.Stot = P.sbuf("Stot", [128, 128], F32)
        K.bfrep = P.sbuf("bfrep", [128, 128], F32)
        K.usum = P.sbuf("usum", [128, 128], F32)
        K.onesf = P.sbuf("onesf", [128, 128], F32)
        K.biasF = P.sbuf("biasF", [128, 128], F32)
        P.dma("sp", K.usum.full(), V_(d["c_usum"]))
        P.memset("dve", K.onesf.full(), 1.0)
    P.dma("sp", K.bfrep.full(), V_(d["bf_rep"][e]))
    wff = ws(K).sub(0, 64, "p (c n) -> p c n", c=8)
    wload(K, wff.full(), d["w_in_even"][e, :, 1536:1544], "(c p) n -> p c n", p=128)
    b = bank(K, "M")
    for tt in range(16):
        for c in range(NCK):
            P.mm(b[:, tt * 8:(tt + 1) * 8], K.hT[:, c, tt * 128:(tt + 1) * 128], wff[:, c, :],
                 start=(c == 0), stop=(c == NCK - 1))
    z = fw(K)
    P.tt("dve", z[:, 0:128], b[:, 0:128], K.bfrep.full(), ALU.add)
    P.act(z[:, 0:128], z[:, 0:128], AF.Exp, scale=-1.0)
    P.act(K.LF.full(), z[:, 0:128], AF.Ln, bias=1.0)
    b1 = bank(K, "M")
    P.mm(b1[:, 0:128], K.usum.full(), K.LF.full())
    b2 = bank(K, "M")
    P.mm(b2[:, 0:128], K.onesf.full(), K.LF.full())
    P.copy("dve", K.Stot.full(), b2[:, 0:128])
    P.memset("dve", K.Coff[:, 0:8], 0.0)
    for tt in range(1, 16):
        P.tt("dve", K.Coff[:, tt * 8:(tt + 1) * 8], K.Coff[:, (tt - 1) * 8:tt * 8], K.Stot[:, (tt - 1) * 8:tt * 8], ALU.add)
    P.tt("dve", K.Ccol.full(), b1[:, 0:128], K.Coff.full(), ALU.add)


def fox_pass(K, e, pg):
    P, d = K.P, K.d
    win, wo_d = d["w_in_even"], d["w_out_even"]
    KT = K.KTf.sub(0, 2 * T, "p (a b) -> p a b", a=2)
    VT = K.VTf.sub(0, 2 * T, "p (a b) -> p a b", a=16)
    for jp in range(2):
        c0 = 512 + (2 * pg + jp) * 128
        w = w_tile(K, win[e, :, c0:c0 + 128])
        proj_fm(K, w, slice(0, 128), lambda cs, jp=jp: KT[:, jp, cs])
    wv = K.WL[0].sub(0, 2048, "p (c n) -> p c n", c=8)
    wload(K, wv.full(), win[e, :, 1024 + pg * 256:1024 + (pg + 1) * 256], "(c p) n -> p c n", p=128)
    proj_tm(K, wv, 256, VT)
    wq = K.WL[1].sub(0, 2048, "p (c n) -> p c n", c=8)
    wload(K, wq.full(), win[e, :, pg * 256:(pg + 1) * 256], "(c p) n -> p c n", p=128)
    wout = K.WL[2].sub(0, 2048, "p (a n) -> p a n", a=2)
    wload(K, wout.full(), wo_d[e, pg * 256:(pg + 1) * 256, :], "(a p) n -> p a n", p=128)
    for ch in range(NCH):
        cs = slice(ch * CW, (ch + 1) * CW)
        q_proj(K, wq, cs)
        ktop = 4 * ch + 3
        for kt in range(ktop + 1):
            P.tt("dve", K.biasF[:, kt * 8 + 4 * pg:kt * 8 + 4 * pg + 4], K.Ccol[:, kt * 8 + 4 * pg:kt * 8 + 4 * pg + 4],
                 K.Coff[:, 4 * ch * 8 + 4 * pg:4 * ch * 8 + 4 * pg + 4], ALU.subtract)
        for jp in range(2):
            ob = bank(K, "O")
            db = bank(K, "Dn")
            for half in range(2):
                hs = slice(half * 64, half * 64 + 64)
                h = 4 * pg + 2 * jp + half
                for kt in range(ktop + 1):
                    n0 = max(0, kt - 4 * ch) * 128
                    N = CW - n0
                    diag = kt >= 4 * ch
                    y = bank(K, "Y")
                    P.mm(y[:, :N], KT[hs, jp, kt * 128:(kt + 1) * 128], K.qT[hs, jp, n0:CW], start=True, stop=not diag)
                    if diag:
                        P.mm(y[:, 0:128], K.ident.full(), K.caus.full(), start=False, stop=True)
                    A = pt(K)
                    P.act(A[:, :N], y[:, :N], AF.Exp, bias=K.biasF[:, kt * 8 + h:kt * 8 + h + 1])
                    P.mm(ob[hs, n0:CW], VT[:, kt, (2 * jp + half) * 64:(2 * jp + half + 1) * 64], A[:, :N],
                         start=(kt == 0), stop=(kt == ktop))
                    P.mm(db[hs, n0:CW], K.ones[:, 0:64], A[:, :N], start=(kt == 0), stop=(kt == ktop))
            rd = fw(K)
            P.recip(rd.full(), db.full())
            P.tt("dve", K.oT[:, jp, :], ob.full(), rd.full(), ALU.mult)
        out_proj(K, wout, cs)


def nsa_setup(K):
    P, d = K.P, K.d
    K.expb = P.sbuf("expb", [32, 16, 128], BF16)
    K.ovl = P.sbuf("ovl", [127, 33], BF16)
    K.kb = P.sbuf("kb", [128, 64], F32)
    K.ab = P.sbuf("ab", [128, 64], F32)
    K.sel12 = P.sbuf("sel12", [12, 6, 128], BF16)
    K.chcol = P.sbuf("chcol", [128, 8], F32)
    K.tbm = P.sbuf("tbm", [128, 8, 256], BF16)
    K.wbb = P.sbuf("wbb", [128, 8, 384], BF16)
    K.w2c = P.sbuf("w2c", [128, 64], BF16)
    K.peT = P.sbuf("peT", [128, 32], BF16)
    K.cKV = P.sbuf("cKV", [128, 1], F32)
    K.xg = P.sbuf("xg", [128, 127], F32)
    K.x2 = P.sbuf("x2", [128, 127], F32)
    K.hg = P.sbuf("hg", [128, 127], BF16)
    K.kcd = P.sbuf("kcd", [128, 127], BF16)
    K.vcs = P.sbuf("vcs", [127, 64], BF16)
    K.sacc = P.sbuf("sacc", [128, 4, 32], F32)
    K.fin = P.sbuf("fin", [128, 32], F32)
    K.top8 = P.sbuf("top8", [128, 8], F32)
    K.negm = P.sbuf("negm", [128, 4, 32], F32)
    K.rden4 = P.sbuf("rden4", [128, 4], F32)
    K.NEGT = [P.sbuf("negT%d" % i, [32, CW], BF16) for i in range(2)]
    K.CB = [P.sbuf("cb%d" % i, [127, CW], BF16) for i in range(2)]
    K.Gf = P.sbuf("Gf", [12, CW], F32)
    K.Ghi = P.sbuf("Ghi", [12, CW], BF16)
    K.Glo = P.sbuf("Glo", [12, CW], BF16)
    wload(K, K.expb.full(), d["c_expb"])
    wload(K, K.ovl.full(), d["c_ovl"])
    wload(K, K.sel12.full(), d["c_sel12"])
    wload(K, K.wbb.full(), d["t_wb"])
    P.dma("sp", K.kb.full(), V_(d["c_kb"]))
    P.dma("sp", K.ab.full(), V_(d["c_ab"]))
    P.dma("sp", K.chcol.full(), V_(d["t_ch"]))
    tbf = d["t_tb"].rearrange("p h a b -> p (h a b)")
    for hp in range(4):
        tmp = fw(K)
        P.dma("sp", tmp.full(), V_(tbf[:, hp * 512:(hp + 1) * 512]))
        for i in range(2):
            h = 2 * hp + i
            P.ts("dve", K.tbm[:, h, :], tmp[:, i * 256:(i + 1) * 256], K.chcol[:, h:h + 1], ALU.subtract)


def nsa_combine(K, jp, br, ob, db, guard):
    P = K.P
    gb = bank(K, "M")
    P.mm(gb.full(), K.sel12[:, jp * 3 + br, :], K.Ghi.full(), start=True, stop=False)
    P.mm(gb.full(), K.sel12[:, jp * 3 + br, :], K.Glo.full(), start=False, stop=True)
    rd = fw(K)
    if guard:
        P.ts("dve", rd.full(), db.full(), 1e-30, ALU.max)
        P.recip(rd.full(), rd.full())
    else:
        P.recip(rd.full(), db.full())
    P.tt("dve", rd.full(), gb.full(), rd.full(), ALU.mult)
    if br == 0:
        P.tt("dve", K.ACC[jp].full(), ob.full(), rd.full(), ALU.mult)
    else:
        tmp = fw(K)
        P.tt("dve", tmp.full(), ob.full(), rd.full(), ALU.mult)
        dst = K.oT[:, jp, :] if br == 2 else K.ACC[jp].full()
        P.tt("dve", dst, K.ACC[jp].full(), tmp.full(), ALU.add)


def nsa_pass(K, e, g):
    P, d = K.P, K.d
    if not hasattr(K, "expb"):
        nsa_setup(K)
    win, wo_d = d["w_in_even"], d["w_out_even"]
    KT = K.KTf.sub(0, 2 * T, "p (a b) -> p a b", a=2)
    VT = K.VTf.sub(0, 2048, "p (a b) -> p a b", a=16)
    CR = K.VTf.sub(2048, 4096)

    def dup_tile(ca, cb_):
        w = ws(K).sub(0, 1024, "p (c n) -> p c n", c=8)
        wload(K, w[:, :, 0:64], win[e, :, ca:ca + 64], "(c p) n -> p c n", p=128)
        wload(K, w[:, :, 64:128], win[e, :, cb_:cb_ + 64], "(c p) n -> p c n", p=128)
        return w

    ksl0, kwn0, kc0, vc0 = 2312 + g * 64, 2568 + g * 64, 2056 + g * 64, 2184 + g * 64
    vsl0, vwn0 = 2440 + g * 64, 2696 + g * 64
    proj_fm(K, dup_tile(ksl0, ksl0), slice(0, 128), lambda cs: KT[:, 0, cs])
    proj_fm(K, dup_tile(kwn0, kwn0), slice(0, 128), lambda cs: KT[:, 1, cs])
    proj_fm(K, dup_tile(kc0, vc0), slice(0, 128), lambda cs: CR[:, cs])
    proj_tm(K, dup_tile(vsl0, vwn0), 128, VT)
    W1 = ws(K).sub(0, 2048, "p (l e) -> p l e", l=32)
    wload(K, W1[0:64], d["cmp_w1_k"][e], "l d e -> d l e")
    wload(K, W1[64:128], d["cmp_w1_v"][e], "l d e -> d l e")
    wload(K, K.w2c[0:64, :], d["cmp_w2_k"][e])
    wload(K, K.w2c[64:128, :], d["cmp_w2_v"][e])
    wload(K, K.peT[0:64, :], d["cmp_peT_k"][e])
    wload(K, K.peT[64:128, :], d["cmp_peT_v"][e])
    hb = bank(K, "M")
    for half in range(2):
        hs = slice(half * 64, half * 64 + 64)
        for l in range(32):
            P.mm(hb[hs, 0:127], W1[hs, l, :], CR[hs, l:l + 16 * 126 + 1:16], start=(l == 0), stop=(l == 31))
        for l in range(32):
            P.mm(hb[hs, 128:129], W1[hs, l, :], K.peT[hs, l:l + 1], start=(l == 0), stop=(l == 31))
    P.copy("dve", K.cKV.full(), hb[:, 128:129])
    P.act(K.xg.full(), hb[:, 0:127], AF.Identity, bias=K.cKV[:, 0:1])
    P.tt("dve", K.x2.full(), K.xg.full(), K.xg.full(), ALU.mult)
    P.ts("dve", K.x2.full(), K.x2.full(), 0.044715, ALU.mult, 1.0, ALU.add)
    P.tt("dve", K.x2.full(), K.x2.full(), K.xg.full(), ALU.mult)
    P.act(K.x2.full(), K.x2.full(), AF.Sigmoid, scale=1.5957691216057308)
    P.tt("dve", K.hg.full(), K.x2.full(), K.xg.full(), ALU.mult)
    kb_ = bank(K, "M")
    P.mm(kb_[0:64, 0:127], K.w2c[0:64, :], K.hg[0:64, :])
    P.mm(kb_[64:128, 0:127], K.w2c[0:64, :], K.hg[0:64, :])
    P.copy("dve", K.kcd.full(), kb_[:, 0:127])
    vb_ = bank(K, "M")
    P.mm(vb_[0:127, 0:64], K.hg[64:128, :], K.w2c[64:128, :])
    P.copy("dve", K.vcs.full(), vb_[0:127, 0:64])
    wg = K.WL[0].sub(0, 96, "p (c n) -> p c n", c=8)
    wload(K, wg.full(), win[e, :, 2824 + g * 12:2824 + (g + 1) * 12], "(c p) n -> p c n", p=128)
    wq = K.WL[1].sub(0, 2048, "p (c n) -> p c n", c=8)
    wload(K, wq.full(), win[e, :, 1544 + g * 256:1544 + (g + 1) * 256], "(c p) n -> p c n", p=128)
    wout = K.WL[2].sub(0, 2048, "p (a n) -> p a n", a=2)
    wload(K, wout.full(), wo_d[e, 512 + g * 256:512 + (g + 1) * 256, :], "(a p) n -> p a n", p=128)
    for ch in range(NCH):
        cs = slice(ch * CW, (ch + 1) * CW)
        ktop = 4 * ch + 3
        q_proj(K, wq, cs)
        gb_ = bank(K, "M")
        for c in range(NCK):
            P.mm(gb_[0:12, :], wg[:, c, :], K.hT[:, c, cs], start=(c == 0), stop=(c == NCK - 1))
        P.act(K.Gf.full(), gb_[0:12, :], AF.Sigmoid)
        P.copy("dve", K.Ghi.full(), K.Gf.full())
        P.tt("dve", K.Glo.full(), K.Gf.full(), K.Ghi.full(), ALU.subtract)
        for jp in range(2):
            ob = bank(K, "O")
            db = bank(K, "Dn")
            for half in range(2):
                hs = slice(half * 64, half * 64 + 64)
                hh = 4 * g + 2 * jp + half
                cb = rot(K, "CB", K.CB)
                wload(K, cb.full(), d["t_cb"][hh, :, cs])
                y = bank(K, "Y")
                P.mm(y[0:127, :], K.kcd[hs, :], K.qT[hs, jp, :], start=True, stop=False)
                P.mm(y[0:127, :], K.ident[0:127, 0:127], cb.full(), start=False, stop=True)
                Pc = pt(K)
                P.act(Pc[0:127, :], y[0:127, :], AF.Exp)
                P.mm(ob[hs, :], K.vcs.full(), Pc[0:127, :])
                P.mm(db[hs, :], K.ones[0:127, 0:64], Pc[0:127, :])
                s4 = bank(K, "M")
                for qi in range(4):
                    P.mm(s4[:, qi * 33:(qi + 1) * 33], Pc[0:127, qi * 128:(qi + 1) * 128], K.ovl.full())
                s4v = Buf(s4.h[:, 0:132].rearrange("p (a b) -> p a b", a=4), s4.name, [128, 4, 33], F32, region=s4.region)
                P.ts("dve", K.rden4.full(), s4v[:, :, 32], 1e-30, ALU.max)
                P.recip(K.rden4.full(), K.rden4.full())
                for qi in range(4):
                    if jp == 0 and half == 0:
                        P.ts("dve", K.sacc[:, qi, :], s4[:, qi * 33:qi * 33 + 32], K.rden4[:, qi:qi + 1], ALU.mult)
                    else:
                        P.stt("dve", K.sacc[:, qi, :], s4[:, qi * 33:qi * 33 + 32], K.rden4[:, qi:qi + 1],
                              K.sacc[:, qi, :], ALU.mult, ALU.add)
            nsa_combine(K, jp, 0, ob, db, guard=True)
        tps = bank(K, "M")
        for qi in range(4):
            lo = 32 - 2 * (4 * ch + qi)
            P.tt("dve", K.fin.full(), K.sacc[:, qi, :], K.kb[:, lo:lo + 32], ALU.mult)
            P.tt("dve", K.fin.full(), K.fin.full(), K.ab[:, lo:lo + 32], ALU.add)
            P.memset("dve", K.fin[:, 0:1], 1e4)
            P.add("dve", lambda e_: e_.max(out=K.top8.full().ap, in_=K.fin.full().ap), reads=[K.fin.full()],
                  writes=[K.top8.full()])
            P.ts("dve", K.negm[:, qi, :], K.fin.full(), K.top8[:, 7:8], ALU.is_lt, NEGM, ALU.mult)
            P.transpose(tps[0:32, qi * 128:(qi + 1) * 128], K.negm[:, qi, :], K.identf.full())
        negT = rot(K, "NEGT", K.NEGT)
        P.copy("dve", negT.full(), tps[0:32, :])
        for jp in range(2):
            ob = bank(K, "O")
            db = bank(K, "Dn")
            for half in range(2):
                hs = slice(half * 64, half * 64 + 64)
                hh = 4 * g + 2 * jp + half
                for kt in range(ktop + 1):
                    n0 = max(0, kt - 4 * ch) * 128
                    N = CW - n0
                    y = bank(K, "Y")
                    P.mm(y[:, :N], KT[hs, 0, kt * 128:(kt + 1) * 128], K.qT[hs, jp, n0:CW], start=True, stop=False)
                    if kt >= 4 * ch:
                        P.mm(y[:, 0:128], K.ident.full(), K.tbm[:, hh, 0:128], start=False, stop=False)
                    if 4 * ch <= kt + 1 <= ktop:
                        o1 = (kt + 1 - 4 * ch) * 128 - n0
                        P.mm(y[:, o1:o1 + 128], K.ident.full(), K.tbm[:, hh, 128:256], start=False, stop=False)
                    P.mm(y[:, :N], K.expb[:, kt, :], negT[:, n0:CW], start=False, stop=True)
                    A = pt(K)
                    P.act(A[:, :N], y[:, :N], AF.Exp, bias=K.chcol[:, hh:hh + 1])
                    P.mm(ob[hs, n0:CW], VT[:, kt, 0:64], A[:, :N], start=(kt == 0), stop=(kt == ktop))
                    P.mm(db[hs, n0:CW], K.ones[:, 0:64], A[:, :N], start=(kt == 0), stop=(kt == ktop))
            nsa_combine(K, jp, 1, ob, db, guard=False)
        for jp in range(2):
            ob = bank(K, "O")
            db = bank(K, "Dn")
            P.mm(ob.full(), K.zeros.full(), K.qT[:, jp, :], start=True, stop=False)
            P.mm(db.full(), K.zeros.full(), K.qT[:, jp, :], start=True, stop=False)
            for half in range(2):
                hs = slice(half * 64, half * 64 + 64)
                hh = 4 * g + 2 * jp + half
                kts = list(range(max(0, 4 * ch - 2), ktop + 1))
                for kt in kts:
                    blo, bhi = max(kt, 4 * ch), min(kt + 2, ktop)
                    c0, c1 = (blo - 4 * ch) * 128, (bhi - 4 * ch + 1) * 128
                    Nw = c1 - c0
                    y = bank(K, "Y")
                    P.mm(y[:, :Nw], KT[hs, 1, kt * 128:(kt + 1) * 128], K.qT[hs, jp, c0:c1], start=True, stop=False)
                    P.mm(y[:, :Nw], K.ident.full(), K.wbb[:, hh, (blo - kt) * 128:(bhi - kt + 1) * 128], start=False, stop=True)
                    A = pt(K)
                    P.act(A[:, :Nw], y[:, :Nw], AF.Exp)
                    last = kt == kts[-1]
                    P.mm(ob[hs, c0:c1], VT[:, kt, 64:128], A[:, :Nw], start=False, stop=last)
                    P.mm(db[hs, c0:c1], K.ones[:, 0:64], A[:, :Nw], start=False, stop=last)
            nsa_combine(K, jp, 2, ob, db, guard=False)
        out_proj(K, wout, cs)


def mixer_phase(K, layer):
    rmsnorm(K, K.xT, layer, K.hT, T)
    if layer % 2 == 1:
        for pg in range(4):
            sb_pass(K, layer // 2, pg)
    else:
        e = layer // 2
        fox_prep(K, e)
        for pg in range(2):
            fox_pass(K, e, pg)
        for g in range(2):
            nsa_pass(K, e, g)


def build(phases, final_norm=True, with_mem=True):
    nc = bass.Bass("TRN2", target_bir_lowering=False)
    K = setup(nc)
    load_consts(K)
    setup_mixer(K)
    if with_mem:
        prep_mem(K)
    load_T(K, K.d["x"], K.xT, T)
    for kind, layer in phases:
        if kind == "mixer":
            mixer_phase(K, layer)
        elif kind == "fox":
            rmsnorm(K, K.xT, layer, K.hT, T)
            fox_prep(K, layer // 2)
            for pg in range(2):
                fox_pass(K, layer // 2, pg)
        elif kind == "nsa":
            rmsnorm(K, K.xT, layer, K.hT, T)
            for g in range(2):
                nsa_pass(K, layer // 2, g)
        elif kind == "xattn":
            xattn_phase(K, layer)
        elif kind == "mlp":
            mlp_phase(K, layer)
    if final_norm:
        rmsnorm(K, K.xT, 12, K.xT, T)
    store_T(K, K.xT, K.d["out"])
    K.P.finish()
    return nc, K


def t5_bucket_np(dist):
    dist = np.maximum(dist, 0)
    d_f = np.maximum(dist, 1).astype(np.float32)
    large = 16 + (np.log(d_f / np.float32(16)) / np.float32(np.log(128 / 16)) * np.float32(16)).astype(np.int32)
    large = np.minimum(large, 31)
    return np.where(dist < 16, dist, large)


def host_consts(inp):
    f = np.float32
    c = {}
    gl = [inp["norm_mix_g"][i] for i in range(4)] + [inp["norm_xattn_g"][i] for i in range(4)] + \
         [inp["norm_mlp_g"][i] for i in range(4)] + [inp["final_norm_g"], inp["mem_norm_g"]]
    c["gains"] = np.ascontiguousarray(np.stack(gl, 0).reshape(14, 8, 128).transpose(2, 0, 1)).astype(f)
    i = np.arange(128)
    c["c_ident"] = np.eye(128, dtype=f)
    c["c_caus"] = np.where(i[None, :] >= i[:, None], 0.0, NEGM).astype(f)
    c["c_strict"] = np.where(i[None, :] > i[:, None], 0.0, NEGM).astype(f)
    c["c_negu"] = np.where(i[:, None] >= i[None, :], -1.0, 0.0).astype(f)
    c["c_usum"] = np.where(i[:, None] <= i[None, :], 1.0, 0.0).astype(f)
    n = np.arange(32)
    kt = np.arange(16)
    s_ = np.arange(128)
    c["c_expb"] = (n[:, None, None] == (kt[None, :, None] * 128 + s_[None, None, :]) // 64).astype(f)
    cc = np.arange(127)
    cs0 = cc * 16
    ce = cs0 + 31
    ss0 = n * 64
    se = ss0 + 63
    ovl = ((cs0[:, None] <= se[None, :]) & (ce[:, None] >= ss0[None, :])).astype(f)
    c["c_ovl"] = np.concatenate([ovl, np.ones((127, 1), f)], 1)
    q = np.arange(128)
    hi = (q >= 64).astype(np.int64)[:, None]
    m = np.arange(64)[None, :]
    forced = (m == 32 + hi) | (m == 31 + hi)
    future = m > 32 + hi
    c["c_kb"] = np.where(forced | future, 0.0, 1.0).astype(f)
    c["c_ab"] = np.where(forced, 1e4, np.where(future, -1.0, 0.0)).astype(f)
    sel = np.zeros((12, 6, 128), f)
    for jp in range(2):
        for half in range(2):
            for br in range(3):
                sel[(2 * jp + half) * 3 + br, jp * 3 + br, half * 64:(half + 1) * 64] = 1.0
    c["c_sel12"] = sel
    rb = inp["rel_bias"].astype(f)
    tb = np.zeros((128, 8, 2, 128), f)
    for dl in range(2):
        dist = dl * 128 + s_[None, :] - s_[:, None]
        val = rb[t5_bucket_np(dist)]
        val = np.where((dist >= 0)[:, :, None], val, NEGM)
        tb[:, :, dl, :] = val.transpose(0, 2, 1)
    c["t_tb"] = tb
    wb = np.zeros((128, 8, 384), f)
    for dl in range(3):
        dist = dl * 128 + s_[None, :] - s_[:, None]
        val = rb[t5_bucket_np(dist)]
        ok = (dist >= 0) & (dist < 256)
        val = np.where(ok[:, :, None], val, NEGM)
        wb[:, :, dl * 128:(dl + 1) * 128] = val.transpose(0, 2, 1)
    c["t_wb"] = wb
    tq = np.arange(T)
    dist = tq[None, :] - ce[:, None]
    val = rb[t5_bucket_np(dist)]
    val = np.where((dist >= 0)[:, :, None], val, NEGM)
    c["t_cb"] = np.ascontiguousarray(val.transpose(2, 0, 1)).astype(f)
    c["t_ch"] = np.ascontiguousarray(np.broadcast_to(rb[31][None, :], (128, 8))).astype(f)
    c["bf_rep"] = np.ascontiguousarray(np.broadcast_to(np.tile(inp["b_forget"].astype(f), (1, 16))[:, None, :], (2, 128, 128)))
    c["cmp_peT_k"] = np.ascontiguousarray(inp["cmp_pe_k"].transpose(0, 2, 1)).astype(f)
    c["cmp_peT_v"] = np.ascontiguousarray(inp["cmp_pe_v"].transpose(0, 2, 1)).astype(f)
    return c


PASS_KEYS = ["w_in_even", "w_out_even", "w_in_odd", "w_out_odd", "xa_wq", "xa_wkv", "xa_wo", "mlp_w1", "mlp_w2",
             "b_forget", "cmp_w1_k", "cmp_w1_v", "cmp_w2_k", "cmp_w2_v"]


def make_in_maps(inp, xs, ncores):
    c = host_consts(inp)
    base = {k: np.ascontiguousarray(inp[k], dtype=np.float32) for k in PASS_KEYS}
    base.update(c)
    maps = []
    for b in range(ncores):
        m = dict(base)
        m["x"] = np.ascontiguousarray(xs[b], dtype=np.float32)
        m["mem"] = np.ascontiguousarray(inp["mem"][b], dtype=np.float32)
        maps.append(m)
    return maps


_CACHE = {}


def kernel(**inputs):
    phases = []
    for layer in range(4):
        phases += [("mixer", layer), ("xattn", layer), ("mlp", layer)]
    if "nc" not in _CACHE:
        _CACHE["nc"] = build(phases, final_norm=True)[0]
    nc = _CACHE["nc"]
    n = 8
    maps = make_in_maps(inputs, inputs["x"], n)
    res = run_bass_kernel_spmd(nc, maps, core_ids=list(range(n)))
    return np.stack([np.asarray(r["out"], dtype=np.float32) for r in res.results], 0)
```

```python
from contextlib import ExitStack
import numpy as np
import concourse.bass as bass
import concourse.mybir as mybir

F32 = mybir.dt.float32
BF16 = mybir.dt.bfloat16
AF = mybir.ActivationFunctionType
ALU = mybir.AluOpType
DT_SIZE = {F32: 4, BF16: 2}
SEM_LIMIT = 16000


class View:
    __slots__ = ("ap", "region", "box")

    def __init__(self, ap, region=None, box=None):
        self.ap = ap
        self.region = region
        self.box = box


def _ovl(a, b):
    return a[0] < b[1] and b[0] < a[1] and a[2] < b[3] and b[2] < a[3]


def _cov(a, b):
    return a[0] <= b[0] and a[1] >= b[1] and a[2] <= b[2] and a[3] >= b[3]


class Buf:
    def __init__(self, ap, name, shape, dtype, region=None, base=0):
        self.h = ap
        self.name = name
        self.shape = list(shape)
        self.dtype = dtype
        self.esz = DT_SIZE[dtype]
        self.region = region or name
        self.base = base
        st = [1] * len(shape)
        for i in range(len(shape) - 2, 0, -1):
            st[i] = st[i + 1] * shape[i + 1]
        self.strides = st

    def __getitem__(self, idx):
        if not isinstance(idx, tuple):
            idx = (idx,)
        idx = list(idx) + [slice(None)] * (len(self.shape) - len(idx))
        lo = 0
        hi = 0
        p0, p1 = 0, self.shape[0]
        for d, (ix, n) in enumerate(zip(idx, self.shape)):
            if isinstance(ix, int):
                a, b, s = ix, ix + 1, 1
            else:
                a, b, s = ix.indices(n)
                cnt = max(0, (b - a + s - 1) // s)
                b = a + (cnt - 1) * s + 1
            assert 0 <= a < b <= n, (self.name, idx, self.shape)
            if d == 0:
                p0, p1 = a, b
            else:
                lo += a * self.strides[d]
                hi += (b - 1) * self.strides[d]
        box = (p0, p1, self.base + lo * self.esz, self.base + (hi + 1) * self.esz)
        return View(self.h[tuple(idx)], self.region, box)

    def full(self):
        return self[tuple(slice(None) for _ in self.shape)]

    def sub(self, c0, c1, pattern=None, **kw):
        assert len(self.shape) == 2
        ap = self.h[:, c0:c1]
        shape = [self.shape[0], c1 - c0]
        if pattern is not None:
            ap = ap.rearrange(pattern, **kw)
            shape = list(ap.shape)
        return Buf(ap, self.name, shape, self.dtype, region=self.region, base=self.base + c0 * self.esz)


class Op:
    __slots__ = ("eng", "fn", "tl", "idx", "deps", "waits", "signal", "semkey", "semval", "dma")


class Prog:
    ENG = ("pe", "act", "dve", "pool", "sp")

    def __init__(self, nc, nslots=8):
        self.nc = nc
        self.ops = []
        self.acc = {}
        self.tl_ops = {}
        self.dma_rr = {"sp": 0, "pool": 0, "act": 0}
        self.nslots = nslots
        self.stack = ExitStack()
        self.sb_bytes = 0

    def sbuf(self, name, shape, dtype):
        h = self.stack.enter_context(self.nc.sbuf_tensor("sb_" + name, list(shape), dtype))
        n = 1
        for s in shape[1:]:
            n *= s
        self.sb_bytes += n * DT_SIZE[dtype]
        return Buf(h[tuple(slice(None) for _ in shape)], name, shape, dtype)

    def psum(self, name, shape, dtype=F32):
        h = self.stack.enter_context(self.nc.psum_tensor(name, list(shape), dtype))
        return Buf(h[tuple(slice(None) for _ in shape)], name, shape, dtype)

    def add(self, eng, fn, reads=(), writes=(), dma=False):
        op = Op()
        opid = len(self.ops)
        op.eng = eng
        op.fn = fn
        op.dma = dma
        if dma:
            slot = self.dma_rr[eng] % self.nslots
            self.dma_rr[eng] += 1
            tl = ("dma", eng, slot)
        else:
            tl = eng
        lst = self.tl_ops.setdefault(tl, [])
        op.tl = tl
        op.idx = len(lst)
        deps = set()
        if dma and lst:
            deps.add(lst[-1])
        lst.append(opid)
        for v in reads:
            if v.region is None:
                continue
            W, Rd = self.acc.setdefault(v.region, ([], {}))
            for box, o in W:
                if _ovl(box, v.box):
                    deps.add(o)
        for v in writes:
            if v.region is None:
                continue
            W, Rd = self.acc.setdefault(v.region, ([], {}))
            for box, o in W:
                if _ovl(box, v.box):
                    deps.add(o)
            for (t, box), o in Rd.items():
                if _ovl(box, v.box):
                    deps.add(o)
        for v in writes:
            if v.region is None:
                continue
            W, Rd = self.acc[v.region]
            W[:] = [(b, o) for (b, o) in W if not _cov(v.box, b)]
            for k in [k for k in Rd if _cov(v.box, k[1])]:
                del Rd[k]
            W.append((v.box, opid))
        for v in reads:
            if v.region is None:
                continue
            self.acc[v.region][1][(tl, v.box)] = opid
        op.deps = deps
        op.waits = []
        op.signal = dma
        self.ops.append(op)
        return op

    def mm(self, out, lhsT, rhs, start=True, stop=True, **kw):
        rd = [lhsT, rhs] + ([] if start else [out])
        return self.add("pe", lambda e: e.matmul(out.ap, lhsT.ap, rhs.ap, start=start, stop=stop, **kw),
                        reads=rd, writes=[out])

    def transpose(self, out, in_, ident):
        return self.add("pe", lambda e: e.transpose(out.ap, in_.ap, ident.ap), reads=[in_, ident], writes=[out])

    def act(self, out, in_, func, bias=None, scale=1.0, accum=None):
        rd = [in_]
        kw = {}
        if bias is not None:
            if isinstance(bias, View):
                rd.append(bias)
                kw["bias"] = bias.ap
            else:
                kw["bias"] = bias
        if isinstance(scale, View):
            rd.append(scale)
            kw["scale"] = scale.ap
        else:
            kw["scale"] = scale
        wr = [out]
        if accum is not None:
            kw["accum_out"] = accum.ap
            wr.append(accum)
        return self.add("act", lambda e: e.activation(out.ap, in_.ap, func, **kw), reads=rd, writes=wr)

    def tt(self, eng, out, in0, in1, op):
        return self.add(eng, lambda e: e.tensor_tensor(out.ap, in0.ap, in1.ap, op), reads=[in0, in1], writes=[out])

    def ts(self, eng, out, in0, s1, op0, s2=None, op1=None, accum=None):
        rd = [in0]
        a1 = s1
        a2 = s2
        if isinstance(s1, View):
            rd.append(s1)
            a1 = s1.ap
        if isinstance(s2, View):
            rd.append(s2)
            a2 = s2.ap
        kw = {}
        if op1 is not None:
            kw["op1"] = op1
        wr = [out]
        if accum is not None:
            kw["accum_out"] = accum.ap
            wr.append(accum)
        return self.add(eng, lambda e: e.tensor_scalar(out.ap, in0.ap, a1, a2, op0, **kw), reads=rd, writes=wr)

    def stt(self, eng, out, in0, scalar, in1, op0, op1):
        rd = [in0, in1]
        sc = scalar
        if isinstance(scalar, View):
            rd.append(scalar)
            sc = scalar.ap
        return self.add(eng, lambda e: e.scalar_tensor_tensor(out.ap, in0.ap, sc, in1.ap, op0, op1),
                        reads=rd, writes=[out])

    def copy(self, eng, out, in_):
        if eng == "act":
            return self.add("act", lambda e: e.copy(out.ap, in_.ap), reads=[in_], writes=[out])
        return self.add(eng, lambda e: e.tensor_copy(out.ap, in_.ap), reads=[in_], writes=[out])

    def recip(self, out, in_):
        return self.add("dve", lambda e: e.reciprocal(out.ap, in_.ap), reads=[in_], writes=[out])

    def memset(self, eng, out, val):
        return self.add(eng, lambda e: e.memset(out.ap, val), writes=[out])

    def lock(self, op, lockview):
        W, Rd = self.acc.setdefault(lockview.region, ([], {}))
        for box, o in W:
            op.deps.add(o)
        W[:] = [(lockview.box, self.ops.index(op) if False else len(self.ops) - 1)]

    def dma(self, issuer, out, in_):
        return self.add(issuer, lambda e: e.dma_start(out=out.ap, in_=in_.ap), reads=[in_], writes=[out], dma=True)

    def finish(self, final_issuer="sp"):
        nc = self.nc
        ops = self.ops
        fin = Op()
        fin.eng = final_issuer
        fin.fn = None
        fin.dma = False
        fin.tl = None
        fin.idx = -1
        fin.deps = set(l[-1] for l in self.tl_ops.values() if l)
        fin.waits = []
        fin.signal = False
        ops.append(fin)

        clock = {e: {} for e in self.ENG}
        snap = {}
        for op in ops:
            ck = clock[op.eng]
            need = {}
            for d in op.deps:
                o = ops[d]
                if need.get(o.tl, -1) < o.idx:
                    need[o.tl] = o.idx
            for tl, idx in sorted(need.items(), key=lambda kv: str(kv[0])):
                if tl == "pe" and op.eng == "pe" and not op.dma:
                    continue
                if ck.get(tl, -1) >= idx:
                    continue
                op.waits.append((tl, idx))
                for k, v in snap[(tl, idx)].items():
                    if ck.get(k, -1) < v:
                        ck[k] = v
                if ck.get(tl, -1) < idx:
                    ck[tl] = idx
                ops[self.tl_ops[tl][idx]].signal = True
            if op.tl is not None:
                s = dict(ck)
                s[op.tl] = op.idx
                snap[(op.tl, op.idx)] = s
        semkeys = set()
        for tl, lst in self.tl_ops.items():
            k = 0
            for oid in lst:
                o = ops[oid]
                if isinstance(tl, tuple):
                    o.semkey = (tl, 0)
                    o.semval = 16 * (o.idx + 1)
                    semkeys.add(o.semkey)
                elif o.signal:
                    o.semkey = (tl, k // SEM_LIMIT)
                    o.semval = k % SEM_LIMIT + 1
                    semkeys.add(o.semkey)
                    k += 1
        st = self.stack
        sems = {}
        for i, key in enumerate(sorted(semkeys, key=str)):
            sems[key] = st.enter_context(nc.semaphore("s%d" % i))
        self.n_sems = len(sems)
        block = st.enter_context(nc.Block())
        eng_ops = {e: [o for o in ops if o.eng == e] for e in self.ENG}
        tl_ops = self.tl_ops

        def mk(ename):
            def body(e):
                for op in eng_ops[ename]:
                    for tl, idx in op.waits:
                        t = ops[tl_ops[tl][idx]]
                        e.wait_ge(sems[t.semkey], t.semval)
                    if op.fn is not None:
                        ins = op.fn(e)
                        if op.signal:
                            ins.then_inc(sems[op.semkey], 16 if op.dma else 1)
            return body

        block.tensor(mk("pe"))
        block.scalar(mk("act"))
        block.vector(mk("dve"))
        block.gpsimd(mk("pool"))
        block.sync(mk("sp"))
        st.close()
        return nc


from concourse.bass_utils import run_bass_kernel_spmd

T = 2048
D = 1024
NCK = 8
CW = 512
NCH = T // CW
MEM = 256
NEGM = -30000.0
EPS = 1e-6


class Ctx:
    pass


def V_(ap):
    return View(ap)


def setup(nc, dbg_out=None):
    K = Ctx()
    K.nc = nc
    P = K.P = Prog(nc)
    din = lambda n, s: nc.dram_tensor(n, list(s), F32, kind="ExternalInput").ap()
    K.d = d = {}
    for n, s in [("x", (T, D)), ("mem", (MEM, D)), ("gains", (128, 14, 8)),
                 ("w_in_even", (2, D, 2848)), ("w_out_even", (2, D, D)), ("w_in_odd", (2, D, 3072)),
                 ("w_out_odd", (2, D, D)), ("xa_wq", (4, D, 512)), ("xa_wkv", (4, D, 1024)),
                 ("xa_wo", (4, 512, D)), ("mlp_w1", (4, D, 4096)), ("mlp_w2", (4, 4096, D)),
                 ("b_forget", (2, 8)), ("cmp_w1_k", (2, 32, 64, 64)), ("cmp_w1_v", (2, 32, 64, 64)),
                 ("cmp_w2_k", (2, 64, 64)), ("cmp_w2_v", (2, 64, 64)),
                 ("cmp_peT_k", (2, 64, 32)), ("cmp_peT_v", (2, 64, 32)),
                 ("bf_rep", (2, 128, 128)), ("c_ident", (128, 128)), ("c_caus", (128, 128)), ("c_strict", (128, 128)),
                 ("c_negu", (128, 128)), ("c_usum", (128, 128)), ("c_expb", (32, 16, 128)),
                 ("c_ovl", (127, 33)), ("c_kb", (128, 64)), ("c_ab", (128, 64)), ("c_sel12", (12, 6, 128)),
                 ("t_tb", (128, 8, 2, 128)), ("t_wb", (128, 8, 384)), ("t_cb", (8, 127, T)), ("t_ch", (128, 8))]:
        d[n] = din(n, s)
    d["out"] = nc.dram_tensor("out", [T, D], F32, kind="ExternalOutput").ap()
    K.xT = P.sbuf("xT", [128, NCK, T], F32)
    K.hT = P.sbuf("hT", [128, NCK, T], BF16)
    K.KTf = P.sbuf("KT", [128, 2 * T], BF16)
    K.VTf = P.sbuf("VT", [128, 2 * T], BF16)
    K.qT = P.sbuf("qT", [128, 2, CW], BF16)
    K.oT = P.sbuf("oT", [128, 4, CW], BF16)
    K.WL = [P.sbuf("WL%d" % i, [128, 2048], BF16) for i in range(3)]
    K.WS = [P.sbuf("WS%d" % i, [128, 2048], BF16) for i in range(3)]
    K.PT = [P.sbuf("PT%d" % i, [128, CW], BF16) for i in range(8)]
    K.FW = [P.sbuf("FW%d" % i, [128, CW], F32) for i in range(6)]
    K.rr = {}
    K.gains = P.sbuf("gains", [128, 14, 8], F32)
    K.identf = P.sbuf("identf", [128, 128], F32)
    K.ident = P.sbuf("ident", [128, 128], BF16)
    K.ones = P.sbuf("ones", [128, 128], BF16)
    K.memT = P.sbuf("memT", [128, NCK, MEM], BF16)
    K.eps_col = P.sbuf("eps_col", [128, 1], F32)
    K.ps = [P.psum("ps%d" % i, [128, CW]) for i in range(8)]
    K.ps3 = [Buf(b.h.rearrange("p (a b) -> p a b", a=4), b.name, [128, 4, 128], F32) for b in K.ps]
    K.roles = {"Y": [0, 1], "O": [2, 3], "Dn": [4, 5], "M": [6, 7]}
    return K


def set_roles(K, **kw):
    K.roles = dict(kw)


def run_interleaved(gens):
    gens = list(gens)
    while gens:
        for g_ in list(gens):
            try:
                next(g_)
            except StopIteration:
                gens.remove(g_)


ROLES_PROJ = dict(Y=[0, 1], O=[2, 3], Dn=[4, 5], M=[6, 7])
ROLES_ATT = dict(Y=[0, 1, 2], O=[3, 4], Dn=[5, 6], M=[7])
ROLES_SB = dict(Y=[0, 1, 2, 3], O=[4, 5], Dn=[6, 7], M=[7, 6])


def rot(K, key, lst):
    i = K.rr.get(key, 0)
    K.rr[key] = i + 1
    return lst[i % len(lst)]


def bank(K, role):
    return K.ps[rot(K, "ps" + role, K.roles[role])]


def pt(K):
    return rot(K, "PT", K.PT)


def fw(K):
    return rot(K, "FW", K.FW)


def ws(K):
    return rot(K, "WS", K.WS)


def load_consts(K):
    P, d = K.P, K.d
    P.dma("sp", K.gains.full(), V_(d["gains"]))
    P.dma("sp", K.identf.full(), V_(d["c_ident"]))
    P.dma("pool", K.ident.full(), V_(d["c_ident"]))
    P.memset("dve", K.ones.full(), 1.0)
    P.memset("dve", K.eps_col.full(), EPS)


def load_T(K, src_d, dst, ntok, dst_is_bf16_norm=False):
    P = K.P
    for tt in range(ntok // 128):
        for half in range(2):
            st = fw(K)
            P.dma("sp", st.full(), V_(src_d[tt * 128:(tt + 1) * 128, half * 512:(half + 1) * 512]))
            bi = rot(K, "psM", K.roles["M"])
            for q in range(4):
                P.transpose(K.ps[bi][:, q * 128:(q + 1) * 128], st[:, q * 128:(q + 1) * 128], K.identf.full())
            eng = "dve" if (tt + half) % 2 == 0 else "act"
            P.copy(eng, dst[:, half * 4:(half + 1) * 4, tt * 128:(tt + 1) * 128], K.ps3[bi].full())


def store_T(K, src, dst_d):
    P = K.P
    for tt in range(T // 128):
        for half in range(2):
            bi = rot(K, "psM", K.roles["M"])
            for q in range(4):
                P.transpose(K.ps[bi][:, q * 128:(q + 1) * 128], src[:, half * 4 + q, tt * 128:(tt + 1) * 128],
                            K.identf.full())
            st = fw(K)
            eng = "dve" if (tt + half) % 2 == 0 else "act"
            P.copy(eng, st.full(), K.ps[bi].full())
            P.dma("sp", V_(dst_d[tt * 128:(tt + 1) * 128, half * 512:(half + 1) * 512]), st.full())


def rmsnorm(K, src, gidx, dst, ntok):
    P = K.P
    W = min(CW, ntok)
    for ch in range(ntok // W):
        cs = slice(ch * W, (ch + 1) * W)
        b = bank(K, "M")
        for c in range(NCK):
            sq = pt(K)
            P.act(sq[:, :W], src[:, c, cs], AF.Square)
            P.mm(b[:, :W], K.ones.full(), sq[:, :W], start=(c == 0), stop=(c == NCK - 1))
        r = fw(K)
        P.act(r[:, :W], b[:, :W], AF.Sqrt, bias=K.eps_col[:, 0:1], scale=1.0 / D)
        P.recip(r[:, :W], r[:, :W])
        for c in range(NCK):
            P.stt("dve", dst[:, c, cs], src[:, c, cs], K.gains[:, gidx, c:c + 1], r[:, :W], ALU.mult, ALU.mult)


def wload(K, buf, dram_ap, pattern=None, **kw):
    ap = dram_ap if pattern is None else dram_ap.rearrange(pattern, **kw)
    K.P.dma("pool", buf, V_(ap))


def resid_add(K, n, cs, psv):
    K.P.tt("dve", K.xT[:, n, cs], K.xT[:, n, cs], psv, ALU.add)


def mlp_phase(K, layer):
    P, d = K.P, K.d
    set_roles(K, **ROLES_PROJ)
    rmsnorm(K, K.xT, 8 + layer, K.hT, T)
    set_roles(K, Y=[0, 1, 2, 3], O=[4, 5, 6, 7], Dn=[4, 5], M=[6, 7])
    U = [K.KTf.sub(0, 2 * T, "p (a b) -> p a b", a=2), K.VTf.sub(0, 2 * T, "p (a b) -> p a b", a=2)]
    NG = 16

    def stage1(g):
        w1 = ws(K).sub(0, 2048, "p (c n) -> p c n", c=8)
        wload(K, w1.full(), d["mlp_w1"][layer, :, g * 256:(g + 1) * 256], "(c p) n -> p c n", p=128)
        u = U[g % 2]
        for fi in range(2):
            for ch in range(NCH):
                cs = slice(ch * CW, (ch + 1) * CW)
                b = bank(K, "Y")
                for c in range(NCK):
                    P.mm(b.full(), w1[:, c, fi * 128:(fi + 1) * 128], K.hT[:, c, cs], start=(c == 0), stop=(c == NCK - 1))
                r = fw(K)
                P.act(r.full(), b.full(), AF.Relu)
                P.tt("pool", u[:, fi, cs], r.full(), r.full(), ALU.mult)

    def stage2(g):
        w2 = ws(K).sub(0, 2048, "p (a n) -> p a n", a=2)
        wload(K, w2.full(), d["mlp_w2"][layer, g * 256:(g + 1) * 256, :], "(a p) n -> p a n", p=128)
        u = U[g % 2]
        for n in range(NCK):
            for ch in range(NCH):
                cs = slice(ch * CW, (ch + 1) * CW)
                b = bank(K, "O")
                for fi in range(2):
                    P.mm(b.full(), w2[:, fi, n * 128:(n + 1) * 128], u[:, fi, cs], start=(fi == 0), stop=(fi == 1))
                resid_add(K, n, cs, b.full())

    stage1(0)
    for g in range(NG):
        if g + 1 < NG:
            stage1(g + 1)
        stage2(g)
    set_roles(K, **ROLES_PROJ)


def xattn_phase(K, layer):
    P, d = K.P, K.d
    rmsnorm(K, K.xT, 4 + layer, K.hT, T)
    kx = K.WL[0].sub(0, 1024, "p (a m) -> p a m", a=4)
    vx = K.WL[0].sub(1024, 2048, "p (a n) -> p a n", a=2)
    for a in range(4):
        w = ws(K).sub(0, 1024, "p (c n) -> p c n", c=8)
        wload(K, w.full(), d["xa_wkv"][layer, :, a * 128:(a + 1) * 128], "(c p) n -> p c n", p=128)
        b = bank(K, "M")
        for c in range(NCK):
            P.mm(b[:, :MEM], w[:, c, :], K.memT[:, c, :], start=(c == 0), stop=(c == NCK - 1))
        P.copy("dve", kx[:, a, :], b[:, :MEM])
    for hv in range(2):
        w = ws(K).sub(0, 2048, "p (c n) -> p c n", c=8)
        wload(K, w.full(), d["xa_wkv"][layer, :, 512 + hv * 256:512 + (hv + 1) * 256], "(c p) n -> p c n", p=128)
        for mt in range(2):
            b = bank(K, "M")
            for c in range(NCK):
                P.mm(b[:, :256], K.memT[:, c, mt * 128:(mt + 1) * 128], w[:, c, :], start=(c == 0), stop=(c == NCK - 1))
            P.copy("dve", vx[:, mt, hv * 256:(hv + 1) * 256], b[:, :256])
    qx = [K.KTf.sub(0, 2 * T, "p (a b) -> p a b", a=2), K.VTf.sub(0, 2 * T, "p (a b) -> p a b", a=2)]
    for hp in range(2):
        w = ws(K).sub(0, 2048, "p (c n) -> p c n", c=8)
        wload(K, w.full(), d["xa_wq"][layer, :, hp * 256:(hp + 1) * 256], "(c p) n -> p c n", p=128)
        for a in range(2):
            for ch in range(NCH):
                cs = slice(ch * CW, (ch + 1) * CW)
                b = bank(K, "M")
                for c in range(NCK):
                    P.mm(b.full(), w[:, c, a * 128:(a + 1) * 128], K.hT[:, c, cs], start=(c == 0), stop=(c == NCK - 1))
                P.copy("act" if ch % 2 else "dve", qx[hp][:, a, cs], b.full())
    wo = [K.WL[1].sub(0, 2048, "p (a n) -> p a n", a=2), K.WL[2].sub(0, 2048, "p (a n) -> p a n", a=2)]
    for i in range(2):
        wload(K, wo[i].full(), d["xa_wo"][layer, i * 256:(i + 1) * 256, :], "(a p) n -> p a n", p=128)
    sc = 128.0 ** -0.5
    for ch in range(NCH):
        cs = slice(ch * CW, (ch + 1) * CW)
        for a in range(4):
            ob = bank(K, "O")
            db = bank(K, "Dn")
            for mt in range(2):
                y = bank(K, "Y")
                P.mm(y.full(), kx[:, a, mt * 128:(mt + 1) * 128], qx[a // 2][:, a % 2, cs])
                p_ = pt(K)
                P.act(p_.full(), y.full(), AF.Exp, scale=sc)
                P.mm(ob.full(), vx[:, mt, a * 128:(a + 1) * 128], p_.full(), start=(mt == 0), stop=(mt == 1))
                P.mm(db.full(), K.ones.full(), p_.full(), start=(mt == 0), stop=(mt == 1))
            rd = fw(K)
            P.recip(rd.full(), db.full())
            P.tt("dve", K.oT[:, a, :], ob.full(), rd.full(), ALU.mult)
        for n in range(NCK):
            b = bank(K, "M")
            for a in range(4):
                P.mm(b.full(), wo[a // 2][:, a % 2, n * 128:(n + 1) * 128], K.oT[:, a, :], start=(a == 0), stop=(a == 3))
            resid_add(K, n, cs, b.full())


def prep_mem(K):
    P = K.P
    memf = Buf(K.xT.h[:, :, 0:MEM], "xT", [128, NCK, MEM], F32, region="xT")
    memf.strides = [1, T, 1]
    load_T(K, K.d["mem"], memf, MEM)
    rmsnorm(K, memf, 13, K.memT, MEM)


def setup_mixer(K):
    P = K.P
    K.caus = P.sbuf("caus", [128, 128], BF16)
    K.strict = P.sbuf("strict", [128, 128], BF16)
    K.negu = P.sbuf("negu", [128, 128], BF16)
    K.zeros = P.sbuf("zeros", [128, 128], BF16)
    K.negrow = P.sbuf("negrow", [64, 128], BF16)
    K.RS = [P.sbuf("rs%d" % i, [64, CW], BF16) for i in range(2)]
    K.ACC = [P.sbuf("acc%d" % i, [128, CW], F32) for i in range(2)]
    wload(K, K.caus.full(), K.d["c_caus"])
    wload(K, K.strict.full(), K.d["c_strict"])
    wload(K, K.negu.full(), K.d["c_negu"])
    P.memset("dve", K.zeros.full(), 0.0)
    P.memset("dve", K.negrow.full(), -1.0)


def proj_fm(K, w, ncols_tile, dst_fn, scale=None):
    P = K.P
    for ch in range(NCH):
        cs = slice(ch * CW, (ch + 1) * CW)
        b = bank(K, "M")
        for c in range(NCK):
            P.mm(b.full(), w[:, c, ncols_tile], K.hT[:, c, cs], start=(c == 0), stop=(c == NCK - 1))
        P.copy("act" if ch % 2 else "dve", dst_fn(cs), b.full())


def proj_tm(K, w, ncols, VT):
    P = K.P
    for tt in range(T // 128):
        b = bank(K, "M")
        for c in range(NCK):
            P.mm(b[:, :ncols], K.hT[:, c, tt * 128:(tt + 1) * 128], w[:, c, :], start=(c == 0), stop=(c == NCK - 1))
        P.copy("act" if tt % 2 else "dve", VT[:, tt, 0:ncols], b[:, :ncols])


def q_proj(K, wq, cs):
    P = K.P
    for jp in range(2):
        b = bank(K, "M")
        for c in range(NCK):
            P.mm(b.full(), wq[:, c, jp * 128:(jp + 1) * 128], K.hT[:, c, cs], start=(c == 0), stop=(c == NCK - 1))
        P.ts("dve", K.qT[:, jp, :], b.full(), 0.125, ALU.mult)


def out_proj(K, wout, cs):
    P = K.P
    for n in range(NCK):
        b = bank(K, "M")
        for jp in range(2):
            P.mm(b.full(), wout[:, jp, n * 128:(n + 1) * 128], K.oT[:, jp, :], start=(jp == 0), stop=(jp == 1))
        resid_add(K, n, cs, b.full())


def w_tile(K, dram_cols):
    w = ws(K).sub(0, 1024, "p (c n) -> p c n", c=8)
    wload(K, w.full(), dram_cols, "(c p) n -> p c n", p=128)
    return w


def sb_pass(K, o, pg):
    P, d = K.P, K.d
    win, wo_d = d["w_in_odd"], d["w_out_odd"]
    KT = K.KTf.sub(0, 2 * T, "p (a b) -> p a b", a=2)
    VT = K.VTf.sub(0, 2 * T, "p (a b) -> p a b", a=16)
    for jp in range(2):
        c0 = 1024 + (2 * pg + jp) * 128
        w = w_tile(K, win[o, :, c0:c0 + 128])
        proj_fm(K, w, slice(0, 128), lambda cs, jp=jp: KT[:, jp, cs])
    wv = K.WL[0].sub(0, 2048, "p (c n) -> p c n", c=8)
    wload(K, wv.full(), win[o, :, 2048 + pg * 256:2048 + (pg + 1) * 256], "(c p) n -> p c n", p=128)
    proj_tm(K, wv, 256, VT)
    wq = K.WL[1].sub(0, 2048, "p (c n) -> p c n", c=8)
    wload(K, wq.full(), win[o, :, pg * 256:(pg + 1) * 256], "(c p) n -> p c n", p=128)
    wout = K.WL[2].sub(0, 2048, "p (a n) -> p a n", a=2)
    wload(K, wout.full(), wo_d[o, pg * 256:(pg + 1) * 256, :], "(a p) n -> p a n", p=128)
    for ch in range(NCH):
        cs = slice(ch * CW, (ch + 1) * CW)
        set_roles(K, **ROLES_PROJ)
        q_proj(K, wq, cs)
        set_roles(K, **ROLES_SB)
        obs = [bank(K, "O"), bank(K, "O")]
        rpb = [K.ps[6], K.ps[7]]
        for jp in range(2):
            P.mm(obs[jp].full(), K.zeros.full(), K.qT[:, jp, :], start=True, stop=False)
            for half in range(2):
                P.mm(rpb[jp][half * 32:half * 32 + 1, :], K.zeros[:, 0:1], K.qT[:, jp, :], start=True, stop=False)

        def stream(jp, half):
            lock = View(None, "lock_rps%d" % jp, (0, 1, 0, 1))
            hs = slice(half * 64, half * 64 + 64)
            rr_ = slice(half * 32, half * 32 + 1)
            ob = obs[jp]
            rps = rpb[jp]
            rs = K.RS[jp]
            ktop = 4 * ch + 3
            have_r = False
            for kt in range(ktop, -1, -1):
                n0 = max(0, kt - 4 * ch) * 128
                N = CW - n0
                diag = kt >= 4 * ch
                y = bank(K, "Y")
                P.mm(y[:, :N], KT[hs, jp, kt * 128:(kt + 1) * 128], K.qT[hs, jp, n0:CW], start=True, stop=not diag)
                if diag:
                    P.mm(y[:, 0:128], K.ident.full(), K.strict.full(), start=False, stop=True)
                E = fw(K)
                P.act(E[:, :N], y[:, :N], AF.Exp)
                Lb = pt(K)
                P.act(Lb[:, :N], E[:, :N], AF.Ln, bias=1.0)
                yield
                P.mm(y[:, :N], K.negu.full(), Lb[:, :N], start=False, stop=not have_r)
                if have_r:
                    P.mm(y[:, :N], K.negrow[rr_, :], rs[rr_, n0:CW], start=False, stop=True)
                A = pt(K)
                P.act(A[:, :N], y[:, :N], AF.Exp)
                yield
                P.mm(ob[hs, n0:CW], VT[:, kt, (2 * jp + half) * 64:(2 * jp + half + 1) * 64], A[:, :N],
                     start=False, stop=(kt == 0))
                if kt > 0:
                    o_ = P.mm(rps[rr_, n0:CW], K.ones[:, 0:1], Lb[:, :N], start=False, stop=(kt == 1))
                    P.lock(o_, lock)
                    o_ = P.copy("dve", rs[rr_, :], rps[rr_, :])
                    P.lock(o_, lock)
                    have_r = True
                yield

        run_interleaved([stream(jp, half) for jp in range(2) for half in range(2)])
        for jp in range(2):
            P.copy("dve", K.oT[:, jp, :], obs[jp].full())
        set_roles(K, **ROLES_PROJ)
        out_proj(K, wout, cs)


def fox_prep(K, e):
    P, d = K.P, K.d
    if not hasattr(K, "Ccol"):
        K.Ccol = P.sbuf("Ccol", [128, 128], F32)
        K.Coff = P.sbuf("Coff", [128, 128], F32)
        K.LF = P.sbuf("LF", [128, 128], F32)
        K.Stot = P.sbuf("Stot", [128, 128], F32)
        K.bfrep = P.sbuf("bfrep", [128, 128], F32)
        K.usum = P.sbuf("usum", [128, 128], F32)
        K.onesf = P.sbuf("onesf", [128, 128], F32)
        K.biasF = P.sbuf("biasF", [128, 128], F32)
        P.dma("sp", K.usum.full(), V_(d["c_usum"]))
        P.memset("dve", K.onesf.full(), 1.0)
    P.dma("sp", K.bfrep.full(), V_(d["bf_rep"][e]))
    wff = ws(K).sub(0, 64, "p (c n) -> p c n", c=8)
    wload(K, wff.full(), d["w_in_even"][e, :, 1536:1544], "(c p) n -> p c n", p=128)
    b = bank(K, "M")
    for tt in range(16):
        for c in range(NCK):
            P.mm(b[:, tt * 8:(tt + 1) * 8], K.hT[:, c, tt * 128:(tt + 1) * 128], wff[:, c, :],
                 start=(c == 0), stop=(c == NCK - 1))
    z = fw(K)
    P.tt("dve", z[:, 0:128], b[:, 0:128], K.bfrep.full(), ALU.add)
    P.act(z[:, 0:128], z[:, 0:128], AF.Exp, scale=-1.0)
    P.act(K.LF.full(), z[:, 0:128], AF.Ln, bias=1.0)
    b1 = bank(K, "M")
    P.mm(b1[:, 0:128], K.usum.full(), K.LF.full())
    b2 = bank(K, "M")
    P.mm(b2[:, 0:128], K.onesf.full(), K.LF.full())
    P.copy("dve", K.Stot.full(), b2[:, 0:128])
    P.memset("dve", K.Coff[:, 0:8], 0.0)
    for tt in range(1, 16):
        P.tt("dve", K.Coff[:, tt * 8:(tt + 1) * 8], K.Coff[:, (tt - 1) * 8:tt * 8], K.Stot[:, (tt - 1) * 8:tt * 8], ALU.add)
    P.tt("dve", K.Ccol.full(), b1[:, 0:128], K.Coff.full(), ALU.add)


def fox_pass(K, e, pg):
    P, d = K.P, K.d
    win, wo_d = d["w_in_even"], d["w_out_even"]
    KT = K.KTf.sub(0, 2 * T, "p (a b) -> p a b", a=2)
    VT = K.VTf.sub(0, 2 * T, "p (a b) -> p a b", a=16)
    for jp in range(2):
        c0 = 512 + (2 * pg + jp) * 128
        w = w_tile(K, win[e, :, c0:c0 + 128])
        proj_fm(K, w, slice(0, 128), lambda cs, jp=jp: KT[:, jp, cs])
    wv = K.WL[0].sub(0, 2048, "p (c n) -> p c n", c=8)
    wload(K, wv.full(), win[e, :, 1024 + pg * 256:1024 + (pg + 1) * 256], "(c p) n -> p c n", p=128)
    proj_tm(K, wv, 256, VT)
    wq = K.WL[1].sub(0, 2048, "p (c n) -> p c n", c=8)
    wload(K, wq.full(), win[e, :, pg * 256:(pg + 1) * 256], "(c p) n -> p c n", p=128)
    wout = K.WL[2].sub(0, 2048, "p (a n) -> p a n", a=2)
    wload(K, wout.full(), wo_d[e, pg * 256:(pg + 1) * 256, :], "(a p) n -> p a n", p=128)
    for ch in range(NCH):
        cs = slice(ch * CW, (ch + 1) * CW)
        set_roles(K, **ROLES_PROJ)
        q_proj(K, wq, cs)
        ktop = 4 * ch + 3
        for kt in range(ktop + 1):
            P.tt("dve", K.biasF[:, kt * 8 + 4 * pg:kt * 8 + 4 * pg + 4], K.Ccol[:, kt * 8 + 4 * pg:kt * 8 + 4 * pg + 4],
                 K.Coff[:, 4 * ch * 8 + 4 * pg:4 * ch * 8 + 4 * pg + 4], ALU.subtract)
        set_roles(K, **ROLES_ATT)
        obs = [bank(K, "O"), bank(K, "O")]
        dbs = [bank(K, "Dn"), bank(K, "Dn")]

        def stream(jp, half):
            hs = slice(half * 64, half * 64 + 64)
            h = 4 * pg + 2 * jp + half
            ob, db = obs[jp], dbs[jp]
            for kt in range(ktop + 1):
                n0 = max(0, kt - 4 * ch) * 128
                N = CW - n0
                diag = kt >= 4 * ch
                y = bank(K, "Y")
                P.mm(y[:, :N], KT[hs, jp, kt * 128:(kt + 1) * 128], K.qT[hs, jp, n0:CW], start=True, stop=not diag)
                if diag:
                    P.mm(y[:, 0:128], K.ident.full(), K.caus.full(), start=False, stop=True)
                A = pt(K)
                P.act(A[:, :N], y[:, :N], AF.Exp, bias=K.biasF[:, kt * 8 + h:kt * 8 + h + 1])
                yield
                P.mm(ob[hs, n0:CW], VT[:, kt, (2 * jp + half) * 64:(2 * jp + half + 1) * 64], A[:, :N],
                     start=(kt == 0), stop=(kt == ktop))
                P.mm(db[hs, n0:CW], K.ones[:, 0:64], A[:, :N], start=(kt == 0), stop=(kt == ktop))
                yield

        run_interleaved([stream(jp, half) for jp in range(2) for half in range(2)])
        for jp in range(2):
            rd = fw(K)
            P.recip(rd.full(), dbs[jp].full())
            P.tt("dve", K.oT[:, jp, :], obs[jp].full(), rd.full(), ALU.mult)
        set_roles(K, **ROLES_PROJ)
        out_proj(K, wout, cs)


def nsa_setup(K):
    P, d = K.P, K.d
    K.expb = P.sbuf("expb", [32, 16, 128], BF16)
    K.ovl = P.sbuf("ovl", [127, 33], BF16)
    K.kb = P.sbuf("kb", [128, 64], F32)
    K.ab = P.sbuf("ab", [128, 64], F32)
    K.sel12 = P.sbuf("sel12", [12, 6, 128], BF16)
    K.chcol = P.sbuf("chcol", [128, 8], F32)
    K.tbm = P.sbuf("tbm", [128, 8, 256], BF16)
    K.wbb = P.sbuf("wbb", [128, 8, 384], BF16)
    K.w2c = P.sbuf("w2c", [128, 64], BF16)
    K.peT = P.sbuf("peT", [128, 32], BF16)
    K.cKV = P.sbuf("cKV", [128, 1], F32)
    K.xg = P.sbuf("xg", [128, 127], F32)
    K.x2 = P.sbuf("x2", [128, 127], F32)
    K.hg = P.sbuf("hg", [128, 127], BF16)
    K.kcd = P.sbuf("kcd", [128, 127], BF16)
    K.vcs = P.sbuf("vcs", [127, 64], BF16)
    K.sacc = P.sbuf("sacc", [128, 4, 32], F32)
    K.fin = P.sbuf("fin", [128, 32], F32)
    K.top8 = P.sbuf("top8", [128, 8], F32)
    K.negm = P.sbuf("negm", [128, 4, 32], F32)
    K.rden4 = P.sbuf("rden4", [128, 4], F32)
    K.NEGT = [P.sbuf("negT%d" % i, [32, CW], BF16) for i in range(2)]
    K.CB = [P.sbuf("cb%d" % i, [127, CW], BF16) for i in range(2)]
    K.Gf = P.sbuf("Gf", [12, CW], F32)
    K.Ghi = P.sbuf("Ghi", [12, CW], BF16)
    K.Glo = P.sbuf("Glo", [12, CW], BF16)
    wload(K, K.expb.full(), d["c_expb"])
    wload(K, K.ovl.full(), d["c_ovl"])
    wload(K, K.sel12.full(), d["c_sel12"])
    wload(K, K.wbb.full(), d["t_wb"])
    P.dma("sp", K.kb.full(), V_(d["c_kb"]))
    P.dma("sp", K.ab.full(), V_(d["c_ab"]))
    P.dma("sp", K.chcol.full(), V_(d["t_ch"]))
    tbf = d["t_tb"].rearrange("p h a b -> p (h a b)")
    for hp in range(4):
        tmp = fw(K)
        P.dma("sp", tmp.full(), V_(tbf[:, hp * 512:(hp + 1) * 512]))
        for i in range(2):
            h = 2 * hp + i
            P.ts("dve", K.tbm[:, h, :], tmp[:, i * 256:(i + 1) * 256], K.chcol[:, h:h + 1], ALU.subtract)


def nsa_combine(K, jp, br, ob, db, guard):
    P = K.P
    gb = bank(K, "M")
    P.mm(gb.full(), K.sel12[:, jp * 3 + br, :], K.Ghi.full(), start=True, stop=False)
    P.mm(gb.full(), K.sel12[:, jp * 3 + br, :], K.Glo.full(), start=False, stop=True)
    rd = fw(K)
    if guard:
        P.ts("dve", rd.full(), db.full(), 1e-30, ALU.max)
        P.recip(rd.full(), rd.full())
    else:
        P.recip(rd.full(), db.full())
    P.tt("dve", rd.full(), gb.full(), rd.full(), ALU.mult)
    if br == 0:
        P.tt("dve", K.ACC[jp].full(), ob.full(), rd.full(), ALU.mult)
    else:
        tmp = fw(K)
        P.tt("dve", tmp.full(), ob.full(), rd.full(), ALU.mult)
        dst = K.oT[:, jp, :] if br == 2 else K.ACC[jp].full()
        P.tt("dve", dst, K.ACC[jp].full(), tmp.full(), ALU.add)


def nsa_pass(K, e, g):
    P, d = K.P, K.d
    if not hasattr(K, "expb"):
        nsa_setup(K)
    win, wo_d = d["w_in_even"], d["w_out_even"]
    KT = K.KTf.sub(0, 2 * T, "p (a b) -> p a b", a=2)
    VT = K.VTf.sub(0, 2048, "p (a b) -> p a b", a=16)
    CR = K.VTf.sub(2048, 4096)

    def dup_tile(ca, cb_):
        w = ws(K).sub(0, 1024, "p (c n) -> p c n", c=8)
        wload(K, w[:, :, 0:64], win[e, :, ca:ca + 64], "(c p) n -> p c n", p=128)
        wload(K, w[:, :, 64:128], win[e, :, cb_:cb_ + 64], "(c p) n -> p c n", p=128)
        return w

    ksl0, kwn0, kc0, vc0 = 2312 + g * 64, 2568 + g * 64, 2056 + g * 64, 2184 + g * 64
    vsl0, vwn0 = 2440 + g * 64, 2696 + g * 64
    proj_fm(K, dup_tile(ksl0, ksl0), slice(0, 128), lambda cs: KT[:, 0, cs])
    proj_fm(K, dup_tile(kwn0, kwn0), slice(0, 128), lambda cs: KT[:, 1, cs])
    proj_fm(K, dup_tile(kc0, vc0), slice(0, 128), lambda cs: CR[:, cs])
    proj_tm(K, dup_tile(vsl0, vwn0), 128, VT)
    W1 = ws(K).sub(0, 2048, "p (l e) -> p l e", l=32)
    wload(K, W1[0:64], d["cmp_w1_k"][e], "l d e -> d l e")
    wload(K, W1[64:128], d["cmp_w1_v"][e], "l d e -> d l e")
    wload(K, K.w2c[0:64, :], d["cmp_w2_k"][e])
    wload(K, K.w2c[64:128, :], d["cmp_w2_v"][e])
    wload(K, K.peT[0:64, :], d["cmp_peT_k"][e])
    wload(K, K.peT[64:128, :], d["cmp_peT_v"][e])
    hb = bank(K, "M")
    for half in range(2):
        hs = slice(half * 64, half * 64 + 64)
        for l in range(32):
            P.mm(hb[hs, 0:127], W1[hs, l, :], CR[hs, l:l + 16 * 126 + 1:16], start=(l == 0), stop=(l == 31))
        for l in range(32):
            P.mm(hb[hs, 128:129], W1[hs, l, :], K.peT[hs, l:l + 1], start=(l == 0), stop=(l == 31))
    P.copy("dve", K.cKV.full(), hb[:, 128:129])
    P.act(K.xg.full(), hb[:, 0:127], AF.Identity, bias=K.cKV[:, 0:1])
    P.tt("dve", K.x2.full(), K.xg.full(), K.xg.full(), ALU.mult)
    P.ts("dve", K.x2.full(), K.x2.full(), 0.044715, ALU.mult, 1.0, ALU.add)
    P.tt("dve", K.x2.full(), K.x2.full(), K.xg.full(), ALU.mult)
    P.act(K.x2.full(), K.x2.full(), AF.Sigmoid, scale=1.5957691216057308)
    P.tt("dve", K.hg.full(), K.x2.full(), K.xg.full(), ALU.mult)
    kb_ = bank(K, "M")
    P.mm(kb_[0:64, 0:127], K.w2c[0:64, :], K.hg[0:64, :])
    P.mm(kb_[64:128, 0:127], K.w2c[0:64, :], K.hg[0:64, :])
    P.copy("dve", K.kcd.full(), kb_[:, 0:127])
    vb_ = bank(K, "M")
    P.mm(vb_[0:127, 0:64], K.hg[64:128, :], K.w2c[64:128, :])
    P.copy("dve", K.vcs.full(), vb_[0:127, 0:64])
    wg = K.WL[0].sub(0, 96, "p (c n) -> p c n", c=8)
    wload(K, wg.full(), win[e, :, 2824 + g * 12:2824 + (g + 1) * 12], "(c p) n -> p c n", p=128)
    wq = K.WL[1].sub(0, 2048, "p (c n) -> p c n", c=8)
    wload(K, wq.full(), win[e, :, 1544 + g * 256:1544 + (g + 1) * 256], "(c p) n -> p c n", p=128)
    wout = K.WL[2].sub(0, 2048, "p (a n) -> p a n", a=2)
    wload(K, wout.full(), wo_d[e, 512 + g * 256:512 + (g + 1) * 256, :], "(a p) n -> p a n", p=128)
    for ch in range(NCH):
        cs = slice(ch * CW, (ch + 1) * CW)
        ktop = 4 * ch + 3
        set_roles(K, **ROLES_PROJ)
        q_proj(K, wq, cs)
        gb_ = bank(K, "M")
        for c in range(NCK):
            P.mm(gb_[0:12, :], wg[:, c, :], K.hT[:, c, cs], start=(c == 0), stop=(c == NCK - 1))
        P.act(K.Gf.full(), gb_[0:12, :], AF.Sigmoid)
        P.copy("dve", K.Ghi.full(), K.Gf.full())
        P.tt("dve", K.Glo.full(), K.Gf.full(), K.Ghi.full(), ALU.subtract)
        set_roles(K, **ROLES_ATT)
        obs = [bank(K, "O"), bank(K, "O")]
        dbs = [bank(K, "Dn"), bank(K, "Dn")]
        for jp in range(2):
            for half in range(2):
                hs = slice(half * 64, half * 64 + 64)
                hh = 4 * g + 2 * jp + half
                ob, db = obs[jp], dbs[jp]
                cb = rot(K, "CB", K.CB)
                wload(K, cb.full(), d["t_cb"][hh, :, cs])
                y = bank(K, "Y")
                P.mm(y[0:127, :], K.kcd[hs, :], K.qT[hs, jp, :], start=True, stop=False)
                P.mm(y[0:127, :], K.ident[0:127, 0:127], cb.full(), start=False, stop=True)
                Pc = pt(K)
                P.act(Pc[0:127, :], y[0:127, :], AF.Exp)
                P.mm(ob[hs, :], K.vcs.full(), Pc[0:127, :])
                P.mm(db[hs, :], K.ones[0:127, 0:64], Pc[0:127, :])
                s4 = bank(K, "M")
                for qi in range(4):
                    P.mm(s4[:, qi * 33:(qi + 1) * 33], Pc[0:127, qi * 128:(qi + 1) * 128], K.ovl.full())
                s4v = Buf(s4.h[:, 0:132].rearrange("p (a b) -> p a b", a=4), s4.name, [128, 4, 33], F32, region=s4.region)
                P.ts("dve", K.rden4.full(), s4v[:, :, 32], 1e-30, ALU.max)
                P.recip(K.rden4.full(), K.rden4.full())
                for qi in range(4):
                    if jp == 0 and half == 0:
                        P.ts("dve", K.sacc[:, qi, :], s4[:, qi * 33:qi * 33 + 32], K.rden4[:, qi:qi + 1], ALU.mult)
                    else:
                        P.stt("dve", K.sacc[:, qi, :], s4[:, qi * 33:qi * 33 + 32], K.rden4[:, qi:qi + 1],
                              K.sacc[:, qi, :], ALU.mult, ALU.add)
        for jp in range(2):
            nsa_combine(K, jp, 0, obs[jp], dbs[jp], guard=True)
        tps = bank(K, "M")
        for qi in range(4):
            lo = 32 - 2 * (4 * ch + qi)
            P.tt("dve", K.fin.full(), K.sacc[:, qi, :], K.kb[:, lo:lo + 32], ALU.mult)
            P.tt("dve", K.fin.full(), K.fin.full(), K.ab[:, lo:lo + 32], ALU.add)
            P.memset("dve", K.fin[:, 0:1], 1e4)
            P.add("dve", lambda e_: e_.max(out=K.top8.full().ap, in_=K.fin.full().ap), reads=[K.fin.full()],
                  writes=[K.top8.full()])
            P.ts("dve", K.negm[:, qi, :], K.fin.full(), K.top8[:, 7:8], ALU.is_lt, NEGM, ALU.mult)
            P.transpose(tps[0:32, qi * 128:(qi + 1) * 128], K.negm[:, qi, :], K.identf.full())
        negT = rot(K, "NEGT", K.NEGT)
        P.copy("dve", negT.full(), tps[0:32, :])
        obs = [bank(K, "O"), bank(K, "O")]
        dbs = [bank(K, "Dn"), bank(K, "Dn")]

        def slc_stream(jp, half):
            hs = slice(half * 64, half * 64 + 64)
            hh = 4 * g + 2 * jp + half
            ob, db = obs[jp], dbs[jp]
            for kt in range(ktop + 1):
                n0 = max(0, kt - 4 * ch) * 128
                N = CW - n0
                y = bank(K, "Y")
                P.mm(y[:, :N], KT[hs, 0, kt * 128:(kt + 1) * 128], K.qT[hs, jp, n0:CW], start=True, stop=False)
                if kt >= 4 * ch:
                    P.mm(y[:, 0:128], K.ident.full(), K.tbm[:, hh, 0:128], start=False, stop=False)
                if 4 * ch <= kt + 1 <= ktop:
                    o1 = (kt + 1 - 4 * ch) * 128 - n0
                    P.mm(y[:, o1:o1 + 128], K.ident.full(), K.tbm[:, hh, 128:256], start=False, stop=False)
                P.mm(y[:, :N], K.expb[:, kt, :], negT[:, n0:CW], start=False, stop=True)
                A = pt(K)
                P.act(A[:, :N], y[:, :N], AF.Exp, bias=K.chcol[:, hh:hh + 1])
                yield
                P.mm(ob[hs, n0:CW], VT[:, kt, 0:64], A[:, :N], start=(kt == 0), stop=(kt == ktop))
                P.mm(db[hs, n0:CW], K.ones[:, 0:64], A[:, :N], start=(kt == 0), stop=(kt == ktop))
                yield

        run_interleaved([slc_stream(jp, half) for jp in range(2) for half in range(2)])
        for jp in range(2):
            nsa_combine(K, jp, 1, obs[jp], dbs[jp], guard=False)
        obs = [bank(K, "O"), bank(K, "O")]
        dbs = [bank(K, "Dn"), bank(K, "Dn")]
        for jp in range(2):
            P.mm(obs[jp].full(), K.zeros.full(), K.qT[:, jp, :], start=True, stop=False)
            P.mm(dbs[jp].full(), K.zeros.full(), K.qT[:, jp, :], start=True, stop=False)

        def win_stream(jp, half):
            hs = slice(half * 64, half * 64 + 64)
            hh = 4 * g + 2 * jp + half
            ob, db = obs[jp], dbs[jp]
            kts = list(range(max(0, 4 * ch - 2), ktop + 1))
            for kt in kts:
                blo, bhi = max(kt, 4 * ch), min(kt + 2, ktop)
                c0, c1 = (blo - 4 * ch) * 128, (bhi - 4 * ch + 1) * 128
                Nw = c1 - c0
                y = bank(K, "Y")
                P.mm(y[:, :Nw], KT[hs, 1, kt * 128:(kt + 1) * 128], K.qT[hs, jp, c0:c1], start=True, stop=False)
                P.mm(y[:, :Nw], K.ident.full(), K.wbb[:, hh, (blo - kt) * 128:(bhi - kt + 1) * 128], start=False, stop=True)
                A = pt(K)
                P.act(A[:, :Nw], y[:, :Nw], AF.Exp)
                yield
                last = kt == kts[-1]
                P.mm(ob[hs, c0:c1], VT[:, kt, 64:128], A[:, :Nw], start=False, stop=last)
                P.mm(db[hs, c0:c1], K.ones[:, 0:64], A[:, :Nw], start=False, stop=last)
                yield

        run_interleaved([win_stream(jp, half) for jp in range(2) for half in range(2)])
        for jp in range(2):
            nsa_combine(K, jp, 2, obs[jp], dbs[jp], guard=False)
        set_roles(K, **ROLES_PROJ)
        out_proj(K, wout, cs)


def mixer_phase(K, layer):
    rmsnorm(K, K.xT, layer, K.hT, T)
    if layer % 2 == 1:
        for pg in range(4):
            sb_pass(K, layer // 2, pg)
    else:
        e = layer // 2
        fox_prep(K, e)
        for pg in range(2):
            fox_pass(K, e, pg)
        for g in range(2):
            nsa_pass(K, e, g)


def build(phases, final_norm=True, with_mem=True):
    nc = bass.Bass("TRN2", target_bir_lowering=False)
    K = setup(nc)
    load_consts(K)
    setup_mixer(K)
    if with_mem:
        prep_mem(K)
    load_T(K, K.d["x"], K.xT, T)
    for kind, layer in phases:
        if kind == "mixer":
            mixer_phase(K, layer)
        elif kind == "fox":
            rmsnorm(K, K.xT, layer, K.hT, T)
            fox_prep(K, layer // 2)
            for pg in range(2):
                fox_pass(K, layer // 2, pg)
        elif kind == "nsa":
            rmsnorm(K, K.xT, layer, K.hT, T)
            for g in range(2):
                nsa_pass(K, layer // 2, g)
        elif kind == "xattn":
            xattn_phase(K, layer)
        elif kind == "mlp":
            mlp_phase(K, layer)
    if final_norm:
        rmsnorm(K, K.xT, 12, K.xT, T)
    store_T(K, K.xT, K.d["out"])
    K.P.finish()
    return nc, K


def t5_bucket_np(dist):
    dist = np.maximum(dist, 0)
    d_f = np.maximum(dist, 1).astype(np.float32)
    large = 16 + (np.log(d_f / np.float32(16)) / np.float32(np.log(128 / 16)) * np.float32(16)).astype(np.int32)
    large = np.minimum(large, 31)
    return np.where(dist < 16, dist, large)


def host_consts(inp):
    f = np.float32
    c = {}
    gl = [inp["norm_mix_g"][i] for i in range(4)] + [inp["norm_xattn_g"][i] for i in range(4)] + \
         [inp["norm_mlp_g"][i] for i in range(4)] + [inp["final_norm_g"], inp["mem_norm_g"]]
    c["gains"] = np.ascontiguousarray(np.stack(gl, 0).reshape(14, 8, 128).transpose(2, 0, 1)).astype(f)
    i = np.arange(128)
    c["c_ident"] = np.eye(128, dtype=f)
    c["c_caus"] = np.where(i[None, :] >= i[:, None], 0.0, NEGM).astype(f)
    c["c_strict"] = np.where(i[None, :] > i[:, None], 0.0, NEGM).astype(f)
    c["c_negu"] = np.where(i[:, None] >= i[None, :], -1.0, 0.0).astype(f)
    c["c_usum"] = np.where(i[:, None] <= i[None, :], 1.0, 0.0).astype(f)
    n = np.arange(32)
    kt = np.arange(16)
    s_ = np.arange(128)
    c["c_expb"] = (n[:, None, None] == (kt[None, :, None] * 128 + s_[None, None, :]) // 64).astype(f)
    cc = np.arange(127)
    cs0 = cc * 16
    ce = cs0 + 31
    ss0 = n * 64
    se = ss0 + 63
    ovl = ((cs0[:, None] <= se[None, :]) & (ce[:, None] >= ss0[None, :])).astype(f)
    c["c_ovl"] = np.concatenate([ovl, np.ones((127, 1), f)], 1)
    q = np.arange(128)
    hi = (q >= 64).astype(np.int64)[:, None]
    m = np.arange(64)[None, :]
    forced = (m == 32 + hi) | (m == 31 + hi)
    future = m > 32 + hi
    c["c_kb"] = np.where(forced | future, 0.0, 1.0).astype(f)
    c["c_ab"] = np.where(forced, 1e4, np.where(future, -1.0, 0.0)).astype(f)
    sel = np.zeros((12, 6, 128), f)
    for jp in range(2):
        for half in range(2):
            for br in range(3):
                sel[(2 * jp + half) * 3 + br, jp * 3 + br, half * 64:(half + 1) * 64] = 1.0
    c["c_sel12"] = sel
    rb = inp["rel_bias"].astype(f)
    tb = np.zeros((128, 8, 2, 128), f)
    for dl in range(2):
        dist = dl * 128 + s_[None, :] - s_[:, None]
        val = rb[t5_bucket_np(dist)]
        val = np.where((dist >= 0)[:, :, None], val, NEGM)
        tb[:, :, dl, :] = val.transpose(0, 2, 1)
    c["t_tb"] = tb
    wb = np.zeros((128, 8, 384), f)
    for dl in range(3):
        dist = dl * 128 + s_[None, :] - s_[:, None]
        val = rb[t5_bucket_np(dist)]
        ok = (dist >= 0) & (dist < 256)
        val = np.where(ok[:, :, None], val, NEGM)
        wb[:, :, dl * 128:(dl + 1) * 128] = val.transpose(0, 2, 1)
    c["t_wb"] = wb
    tq = np.arange(T)
    dist = tq[None, :] - ce[:, None]
    val = rb[t5_bucket_np(dist)]
    val = np.where((dist >= 0)[:, :, None], val, NEGM)
    c["t_cb"] = np.ascontiguousarray(val.transpose(2, 0, 1)).astype(f)
    c["t_ch"] = np.ascontiguousarray(np.broadcast_to(rb[31][None, :], (128, 8))).astype(f)
    c["bf_rep"] = np.ascontiguousarray(np.broadcast_to(np.tile(inp["b_forget"].astype(f), (1, 16))[:, None, :], (2, 128, 128)))
    c["cmp_peT_k"] = np.ascontiguousarray(inp["cmp_pe_k"].transpose(0, 2, 1)).astype(f)
    c["cmp_peT_v"] = np.ascontiguousarray(inp["cmp_pe_v"].transpose(0, 2, 1)).astype(f)
    return c


PASS_KEYS = ["w_in_even", "w_out_even", "w_in_odd", "w_out_odd", "xa_wq", "xa_wkv", "xa_wo", "mlp_w1", "mlp_w2",
             "b_forget", "cmp_w1_k", "cmp_w1_v", "cmp_w2_k", "cmp_w2_v"]


def make_in_maps(inp, xs, ncores):
    c = host_consts(inp)
    base = {k: np.ascontiguousarray(inp[k], dtype=np.float32) for k in PASS_KEYS}
    base.update(c)
    maps = []
    for b in range(ncores):
        m = dict(base)
        m["x"] = np.ascontiguousarray(xs[b], dtype=np.float32)
        m["mem"] = np.ascontiguousarray(inp["mem"][b], dtype=np.float32)
        maps.append(m)
    return maps


_CACHE = {}


def kernel(**inputs):
    phases = []
    for layer in range(4):
        phases += [("mixer", layer), ("xattn", layer), ("mlp", layer)]
    if "nc" not in _CACHE:
        _CACHE["nc"] = build(phases, final_norm=True)[0]
    nc = _CACHE["nc"]
    n = 8
    maps = make_in_maps(inputs, inputs["x"], n)
    res = run_bass_kernel_spmd(nc, maps, core_ids=list(range(n)))
    return np.stack([np.asarray(r["out"], dtype=np.float32) for r in res.results], 0)
```

```python
from contextlib import ExitStack
import numpy as np
import concourse.bass as bass
import concourse.mybir as mybir

F32 = mybir.dt.float32
BF16 = mybir.dt.bfloat16
AF = mybir.ActivationFunctionType
ALU = mybir.AluOpType
DT_SIZE = {F32: 4, BF16: 2}
SEM_LIMIT = 16000


class View:
    __slots__ = ("ap", "region", "box")

    def __init__(self, ap, region=None, box=None):
        self.ap = ap
        self.region = region
        self.box = box


def _ovl(a, b):
    return a[0] < b[1] and b[0] < a[1] and a[2] < b[3] and b[2] < a[3]


def _cov(a, b):
    return a[0] <= b[0] and a[1] >= b[1] and a[2] <= b[2] and a[3] >= b[3]


class Buf:
    def __init__(self, ap, name, shape, dtype, region=None, base=0):
        self.h = ap
        self.name = name
        self.shape = list(shape)
        self.dtype = dtype
        self.esz = DT_SIZE[dtype]
        self.region = region or name
        self.base = base
        st = [1] * len(shape)
        for i in range(len(shape) - 2, 0, -1):
            st[i] = st[i + 1] * shape[i + 1]
        self.strides = st

    def __getitem__(self, idx):
        if not isinstance(idx, tuple):
            idx = (idx,)
        idx = list(idx) + [slice(None)] * (len(self.shape) - len(idx))
        lo = 0
        hi = 0
        p0, p1 = 0, self.shape[0]
        for d, (ix, n) in enumerate(zip(idx, self.shape)):
            if isinstance(ix, int):
                a, b, s = ix, ix + 1, 1
            else:
                a, b, s = ix.indices(n)
                cnt = max(0, (b - a + s - 1) // s)
                b = a + (cnt - 1) * s + 1
            assert 0 <= a < b <= n, (self.name, idx, self.shape)
            if d == 0:
                p0, p1 = a, b
            else:
                lo += a * self.strides[d]
                hi += (b - 1) * self.strides[d]
        box = (p0, p1, self.base + lo * self.esz, self.base + (hi + 1) * self.esz)
        return View(self.h[tuple(idx)], self.region, box)

    def full(self):
        return self[tuple(slice(None) for _ in self.shape)]

    def sub(self, c0, c1, pattern=None, **kw):
        assert len(self.shape) == 2
        ap = self.h[:, c0:c1]
        shape = [self.shape[0], c1 - c0]
        if pattern is not None:
            ap = ap.rearrange(pattern, **kw)
            shape = list(ap.shape)
        return Buf(ap, self.name, shape, self.dtype, region=self.region, base=self.base + c0 * self.esz)


class Op:
    __slots__ = ("eng", "fn", "tl", "idx", "deps", "waits", "signal", "semkey", "semval", "dma")


class Prog:
    ENG = ("pe", "act", "dve", "pool", "sp")

    def __init__(self, nc, nslots=8):
        self.nc = nc
        self.ops = []
        self.acc = {}
        self.tl_ops = {}
        self.dma_rr = {"sp": 0, "pool": 0, "act": 0}
        self.nslots = nslots
        self.stack = ExitStack()
        self.sb_bytes = 0

    def sbuf(self, name, shape, dtype):
        h = self.stack.enter_context(self.nc.sbuf_tensor("sb_" + name, list(shape), dtype))
        n = 1
        for s in shape[1:]:
            n *= s
        self.sb_bytes += n * DT_SIZE[dtype]
        return Buf(h[tuple(slice(None) for _ in shape)], name, shape, dtype)

    def psum(self, name, shape, dtype=F32):
        h = self.stack.enter_context(self.nc.psum_tensor(name, list(shape), dtype))
        return Buf(h[tuple(slice(None) for _ in shape)], name, shape, dtype)

    def add(self, eng, fn, reads=(), writes=(), dma=False):
        op = Op()
        opid = len(self.ops)
        op.eng = eng
        op.fn = fn
        op.dma = dma
        if dma:
            slot = self.dma_rr[eng] % self.nslots
            self.dma_rr[eng] += 1
            tl = ("dma", eng, slot)
        else:
            tl = eng
        lst = self.tl_ops.setdefault(tl, [])
        op.tl = tl
        op.idx = len(lst)
        deps = set()
        if dma and lst:
            deps.add(lst[-1])
        lst.append(opid)
        for v in reads:
            if v.region is None:
                continue
            W, Rd = self.acc.setdefault(v.region, ([], {}))
            for box, o in W:
                if _ovl(box, v.box):
                    deps.add(o)
        for v in writes:
            if v.region is None:
                continue
            W, Rd = self.acc.setdefault(v.region, ([], {}))
            for box, o in W:
                if _ovl(box, v.box):
                    deps.add(o)
            for (t, box), o in Rd.items():
                if _ovl(box, v.box):
                    deps.add(o)
        for v in writes:
            if v.region is None:
                continue
            W, Rd = self.acc[v.region]
            W[:] = [(b, o) for (b, o) in W if not _cov(v.box, b)]
            for k in [k for k in Rd if _cov(v.box, k[1])]:
                del Rd[k]
            W.append((v.box, opid))
        for v in reads:
            if v.region is None:
                continue
            self.acc[v.region][1][(tl, v.box)] = opid
        op.deps = deps
        op.waits = []
        op.signal = dma
        self.ops.append(op)
        return op

    def mm(self, out, lhsT, rhs, start=True, stop=True, **kw):
        rd = [lhsT, rhs] + ([] if start else [out])
        return self.add("pe", lambda e: e.matmul(out.ap, lhsT.ap, rhs.ap, start=start, stop=stop, **kw),
                        reads=rd, writes=[out])

    def transpose(self, out, in_, ident):
        return self.add("pe", lambda e: e.transpose(out.ap, in_.ap, ident.ap), reads=[in_, ident], writes=[out])

    def act(self, out, in_, func, bias=None, scale=1.0, accum=None):
        rd = [in_]
        kw = {}
        if bias is not None:
            if isinstance(bias, View):
                rd.append(bias)
                kw["bias"] = bias.ap
            else:
                kw["bias"] = bias
        if isinstance(scale, View):
            rd.append(scale)
            kw["scale"] = scale.ap
        else:
            kw["scale"] = scale
        wr = [out]
        if accum is not None:
            kw["accum_out"] = accum.ap
            wr.append(accum)
        return self.add("act", lambda e: e.activation(out.ap, in_.ap, func, **kw), reads=rd, writes=wr)

    def tt(self, eng, out, in0, in1, op):
        return self.add(eng, lambda e: e.tensor_tensor(out.ap, in0.ap, in1.ap, op), reads=[in0, in1], writes=[out])

    def ts(self, eng, out, in0, s1, op0, s2=None, op1=None, accum=None):
        rd = [in0]
        a1 = s1
        a2 = s2
        if isinstance(s1, View):
            rd.append(s1)
            a1 = s1.ap
        if isinstance(s2, View):
            rd.append(s2)
            a2 = s2.ap
        kw = {}
        if op1 is not None:
            kw["op1"] = op1
        wr = [out]
        if accum is not None:
            kw["accum_out"] = accum.ap
            wr.append(accum)
        return self.add(eng, lambda e: e.tensor_scalar(out.ap, in0.ap, a1, a2, op0, **kw), reads=rd, writes=wr)

    def stt(self, eng, out, in0, scalar, in1, op0, op1):
        rd = [in0, in1]
        sc = scalar
        if isinstance(scalar, View):
            rd.append(scalar)
            sc = scalar.ap
        return self.add(eng, lambda e: e.scalar_tensor_tensor(out.ap, in0.ap, sc, in1.ap, op0, op1),
                        reads=rd, writes=[out])

    def copy(self, eng, out, in_):
        if eng == "act":
            return self.add("act", lambda e: e.copy(out.ap, in_.ap), reads=[in_], writes=[out])
        return self.add(eng, lambda e: e.tensor_copy(out.ap, in_.ap), reads=[in_], writes=[out])

    def recip(self, out, in_):
        return self.add("dve", lambda e: e.reciprocal(out.ap, in_.ap), reads=[in_], writes=[out])

    def memset(self, eng, out, val):
        return self.add(eng, lambda e: e.memset(out.ap, val), writes=[out])

    def lock(self, op, lockview):
        W, Rd = self.acc.setdefault(lockview.region, ([], {}))
        for box, o in W:
            op.deps.add(o)
        W[:] = [(lockview.box, self.ops.index(op) if False else len(self.ops) - 1)]

    def dma(self, issuer, out, in_):
        return self.add(issuer, lambda e: e.dma_start(out=out.ap, in_=in_.ap), reads=[in_], writes=[out], dma=True)

    def finish(self, final_issuer="sp"):
        nc = self.nc
        ops = self.ops
        fin = Op()
        fin.eng = final_issuer
        fin.fn = None
        fin.dma = False
        fin.tl = None
        fin.idx = -1
        fin.deps = set(l[-1] for l in self.tl_ops.values() if l)
        fin.waits = []
        fin.signal = False
        ops.append(fin)

        clock = {e: {} for e in self.ENG}
        snap = {}
        for op in ops:
            ck = clock[op.eng]
            need = {}
            for d in op.deps:
                o = ops[d]
                if need.get(o.tl, -1) < o.idx:
                    need[o.tl] = o.idx
            for tl, idx in sorted(need.items(), key=lambda kv: str(kv[0])):
                if tl == "pe" and op.eng == "pe" and not op.dma:
                    continue
                if ck.get(tl, -1) >= idx:
                    continue
                op.waits.append((tl, idx))
                for k, v in snap[(tl, idx)].items():
                    if ck.get(k, -1) < v:
                        ck[k] = v
                if ck.get(tl, -1) < idx:
                    ck[tl] = idx
                ops[self.tl_ops[tl][idx]].signal = True
            if op.tl is not None:
                s = dict(ck)
                s[op.tl] = op.idx
                snap[(op.tl, op.idx)] = s
        semkeys = set()
        for tl, lst in self.tl_ops.items():
            k = 0
            for oid in lst:
                o = ops[oid]
                if isinstance(tl, tuple):
                    o.semkey = (tl, 0)
                    o.semval = 16 * (o.idx + 1)
                    semkeys.add(o.semkey)
                elif o.signal:
                    o.semkey = (tl, k // SEM_LIMIT)
                    o.semval = k % SEM_LIMIT + 1
                    semkeys.add(o.semkey)
                    k += 1
        st = self.stack
        sems = {}
        for i, key in enumerate(sorted(semkeys, key=str)):
            sems[key] = st.enter_context(nc.semaphore("s%d" % i))
        self.n_sems = len(sems)
        block = st.enter_context(nc.Block())
        eng_ops = {e: [o for o in ops if o.eng == e] for e in self.ENG}
        tl_ops = self.tl_ops

        def mk(ename):
            def body(e):
                for op in eng_ops[ename]:
                    for tl, idx in op.waits:
                        t = ops[tl_ops[tl][idx]]
                        e.wait_ge(sems[t.semkey], t.semval)
                    if op.fn is not None:
                        ins = op.fn(e)
                        if op.signal:
                            ins.then_inc(sems[op.semkey], 16 if op.dma else 1)
            return body

        block.tensor(mk("pe"))
        block.scalar(mk("act"))
        block.vector(mk("dve"))
        block.gpsimd(mk("pool"))
        block.sync(mk("sp"))
        st.close()
        return nc


from concourse.bass_utils import run_bass_kernel_spmd

T = 2048
D = 1024
NCK = 8
CW = 512
NCH = T // CW
MEM = 256
NEGM = -30000.0
EPS = 1e-6


class Ctx:
    pass


def V_(ap):
    return View(ap)


def setup(nc, dbg_out=None):
    K = Ctx()
    K.nc = nc
    P = K.P = Prog(nc)
    din = lambda n, s: nc.dram_tensor(n, list(s), F32, kind="ExternalInput").ap()
    K.d = d = {}
    for n, s in [("x", (T, D)), ("mem", (MEM, D)), ("gains", (128, 14, 8)),
                 ("w_in_even", (2, D, 2848)), ("w_out_even", (2, D, D)), ("w_in_odd", (2, D, 3072)),
                 ("w_out_odd", (2, D, D)), ("xa_wq", (4, D, 512)), ("xa_wkv", (4, D, 1024)),
                 ("xa_wo", (4, 512, D)), ("mlp_w1", (4, D, 4096)), ("mlp_w2", (4, 4096, D)),
                 ("b_forget", (2, 8)), ("cmp_w1_k", (2, 32, 64, 64)), ("cmp_w1_v", (2, 32, 64, 64)),
                 ("cmp_w2_k", (2, 64, 64)), ("cmp_w2_v", (2, 64, 64)),
                 ("cmp_peT_k", (2, 64, 32)), ("cmp_peT_v", (2, 64, 32)),
                 ("bf_rep", (2, 128, 128)), ("c_ident", (128, 128)), ("c_caus", (128, 128)), ("c_strict", (128, 128)),
                 ("c_negu", (128, 128)), ("c_usum", (128, 128)), ("c_expb", (32, 16, 128)),
                 ("c_ovl", (127, 33)), ("c_kb", (128, 64)), ("c_ab", (128, 64)), ("c_sel12", (12, 6, 128)),
                 ("t_tb", (128, 8, 2, 128)), ("t_wb", (128, 8, 384)), ("t_cb", (8, 127, T)), ("t_ch", (128, 8))]:
        d[n] = din(n, s)
    d["out"] = nc.dram_tensor("out", [T, D], F32, kind="ExternalOutput").ap()
    K.xT = P.sbuf("xT", [128, NCK, T], F32)
    K.hT = P.sbuf("hT", [128, NCK, T], BF16)
    K.KTf = P.sbuf("KT", [128, 2 * T], BF16)
    K.VTf = P.sbuf("VT", [128, 2 * T], BF16)
    K.qT = P.sbuf("qT", [128, 2, CW], BF16)
    K.oT = P.sbuf("oT", [128, 4, CW], BF16)
    K.WL = [P.sbuf("WL%d" % i, [128, 2048], BF16) for i in range(3)]
    K.WS = [P.sbuf("WS%d" % i, [128, 2048], BF16) for i in range(3)]
    K.PT = [P.sbuf("PT%d" % i, [128, CW], BF16) for i in range(8)]
    K.FW = [P.sbuf("FW%d" % i, [128, CW], F32) for i in range(6)]
    K.rr = {}
    K.gains = P.sbuf("gains", [128, 14, 8], F32)
    K.identf = P.sbuf("identf", [128, 128], F32)
    K.ident = P.sbuf("ident", [128, 128], BF16)
    K.ones = P.sbuf("ones", [128, 128], BF16)
    K.memT = P.sbuf("memT", [128, NCK, MEM], BF16)
    K.eps_col = P.sbuf("eps_col", [128, 1], F32)
    K.ps = [P.psum("ps%d" % i, [128, CW]) for i in range(8)]
    K.ps3 = [Buf(b.h.rearrange("p (a b) -> p a b", a=4), b.name, [128, 4, 128], F32) for b in K.ps]
    K.roles = {"Y": [0, 1], "O": [2, 3], "Dn": [4, 5], "M": [6, 7]}
    return K


def set_roles(K, **kw):
    K.roles = dict(kw)


def run_interleaved(gens, offsets=None):
    gens = list(gens)
    if offsets:
        for g_, o in zip(list(gens), offsets):
            for _ in range(o):
                try:
                    next(g_)
                except StopIteration:
                    break
    while gens:
        for g_ in list(gens):
            try:
                next(g_)
            except StopIteration:
                gens.remove(g_)


ROLES_PROJ = dict(Y=[0, 1], O=[2, 3], Dn=[4, 5], M=[6, 7])
ROLES_ATT = dict(Y=[0, 1, 2], O=[3, 4], Dn=[5, 6], M=[7])
ROLES_SB = dict(Y=[0, 1, 2, 3], O=[4, 5], Dn=[6, 7], M=[7, 6])


def rot(K, key, lst):
    i = K.rr.get(key, 0)
    K.rr[key] = i + 1
    return lst[i % len(lst)]


def bank(K, role):
    return K.ps[rot(K, "ps" + role, K.roles[role])]


def pt(K):
    return rot(K, "PT", K.PT)


def fw(K):
    return rot(K, "FW", K.FW)


def ws(K):
    return rot(K, "WS", K.WS)


def load_consts(K):
    P, d = K.P, K.d
    P.dma("sp", K.gains.full(), V_(d["gains"]))
    P.dma("sp", K.identf.full(), V_(d["c_ident"]))
    P.dma("pool", K.ident.full(), V_(d["c_ident"]))
    P.memset("dve", K.ones.full(), 1.0)
    P.memset("dve", K.eps_col.full(), EPS)


def load_T(K, src_d, dst, ntok, dst_is_bf16_norm=False):
    P = K.P
    for tt in range(ntok // 128):
        for half in range(2):
            st = fw(K)
            P.dma("sp", st.full(), V_(src_d[tt * 128:(tt + 1) * 128, half * 512:(half + 1) * 512]))
            bi = rot(K, "psM", K.roles["M"])
            for q in range(4):
                P.transpose(K.ps[bi][:, q * 128:(q + 1) * 128], st[:, q * 128:(q + 1) * 128], K.identf.full())
            eng = "dve" if (tt + half) % 2 == 0 else "act"
            P.copy(eng, dst[:, half * 4:(half + 1) * 4, tt * 128:(tt + 1) * 128], K.ps3[bi].full())


def store_T(K, src, dst_d):
    P = K.P
    for tt in range(T // 128):
        for half in range(2):
            bi = rot(K, "psM", K.roles["M"])
            for q in range(4):
                P.transpose(K.ps[bi][:, q * 128:(q + 1) * 128], src[:, half * 4 + q, tt * 128:(tt + 1) * 128],
                            K.identf.full())
            st = fw(K)
            eng = "dve" if (tt + half) % 2 == 0 else "act"
            P.copy(eng, st.full(), K.ps[bi].full())
            P.dma("sp", V_(dst_d[tt * 128:(tt + 1) * 128, half * 512:(half + 1) * 512]), st.full())


def rmsnorm(K, src, gidx, dst, ntok):
    P = K.P
    W = min(CW, ntok)
    for ch in range(ntok // W):
        cs = slice(ch * W, (ch + 1) * W)
        b = bank(K, "M")
        for c in range(NCK):
            sq = pt(K)
            P.act(sq[:, :W], src[:, c, cs], AF.Square)
            P.mm(b[:, :W], K.ones.full(), sq[:, :W], start=(c == 0), stop=(c == NCK - 1))
        r = fw(K)
        P.act(r[:, :W], b[:, :W], AF.Sqrt, bias=K.eps_col[:, 0:1], scale=1.0 / D)
        P.recip(r[:, :W], r[:, :W])
        for c in range(NCK):
            P.stt("dve", dst[:, c, cs], src[:, c, cs], K.gains[:, gidx, c:c + 1], r[:, :W], ALU.mult, ALU.mult)


def wload(K, buf, dram_ap, pattern=None, **kw):
    ap = dram_ap if pattern is None else dram_ap.rearrange(pattern, **kw)
    K.P.dma("pool", buf, V_(ap))


def resid_add(K, n, cs, psv):
    K.P.tt("dve", K.xT[:, n, cs], K.xT[:, n, cs], psv, ALU.add)


def mlp_phase(K, layer):
    P, d = K.P, K.d
    set_roles(K, **ROLES_PROJ)
    rmsnorm(K, K.xT, 8 + layer, K.hT, T)
    set_roles(K, Y=[0, 1, 2, 3], O=[4, 5, 6, 7], Dn=[4, 5], M=[6, 7])
    U = [K.KTf.sub(0, 2 * T, "p (a b) -> p a b", a=2), K.VTf.sub(0, 2 * T, "p (a b) -> p a b", a=2)]
    NG = 16

    def stage1(g):
        w1 = ws(K).sub(0, 2048, "p (c n) -> p c n", c=8)
        wload(K, w1.full(), d["mlp_w1"][layer, :, g * 256:(g + 1) * 256], "(c p) n -> p c n", p=128)
        u = U[g % 2]
        for fi in range(2):
            for ch in range(NCH):
                cs = slice(ch * CW, (ch + 1) * CW)
                b = bank(K, "Y")
                for c in range(NCK):
                    P.mm(b.full(), w1[:, c, fi * 128:(fi + 1) * 128], K.hT[:, c, cs], start=(c == 0), stop=(c == NCK - 1))
                r = fw(K)
                P.act(r.full(), b.full(), AF.Relu)
                P.tt("pool", u[:, fi, cs], r.full(), r.full(), ALU.mult)

    def stage2(g):
        w2 = ws(K).sub(0, 2048, "p (a n) -> p a n", a=2)
        wload(K, w2.full(), d["mlp_w2"][layer, g * 256:(g + 1) * 256, :], "(a p) n -> p a n", p=128)
        u = U[g % 2]
        for n in range(NCK):
            for ch in range(NCH):
                cs = slice(ch * CW, (ch + 1) * CW)
                b = bank(K, "O")
                for fi in range(2):
                    P.mm(b.full(), w2[:, fi, n * 128:(n + 1) * 128], u[:, fi, cs], start=(fi == 0), stop=(fi == 1))
                resid_add(K, n, cs, b.full())

    stage1(0)
    for g in range(NG):
        if g + 1 < NG:
            stage1(g + 1)
        stage2(g)
    set_roles(K, **ROLES_PROJ)


def xattn_phase(K, layer):
    P, d = K.P, K.d
    rmsnorm(K, K.xT, 4 + layer, K.hT, T)
    kx = K.WL[0].sub(0, 1024, "p (a m) -> p a m", a=4)
    vx = K.WL[0].sub(1024, 2048, "p (a n) -> p a n", a=2)
    for a in range(4):
        w = ws(K).sub(0, 1024, "p (c n) -> p c n", c=8)
        wload(K, w.full(), d["xa_wkv"][layer, :, a * 128:(a + 1) * 128], "(c p) n -> p c n", p=128)
        b = bank(K, "M")
        for c in range(NCK):
            P.mm(b[:, :MEM], w[:, c, :], K.memT[:, c, :], start=(c == 0), stop=(c == NCK - 1))
        P.copy("dve", kx[:, a, :], b[:, :MEM])
    for hv in range(2):
        w = ws(K).sub(0, 2048, "p (c n) -> p c n", c=8)
        wload(K, w.full(), d["xa_wkv"][layer, :, 512 + hv * 256:512 + (hv + 1) * 256], "(c p) n -> p c n", p=128)
        for mt in range(2):
            b = bank(K, "M")
            for c in range(NCK):
                P.mm(b[:, :256], K.memT[:, c, mt * 128:(mt + 1) * 128], w[:, c, :], start=(c == 0), stop=(c == NCK - 1))
            P.copy("dve", vx[:, mt, hv * 256:(hv + 1) * 256], b[:, :256])
    qx = [K.KTf.sub(0, 2 * T, "p (a b) -> p a b", a=2), K.VTf.sub(0, 2 * T, "p (a b) -> p a b", a=2)]
    for hp in range(2):
        w = ws(K).sub(0, 2048, "p (c n) -> p c n", c=8)
        wload(K, w.full(), d["xa_wq"][layer, :, hp * 256:(hp + 1) * 256], "(c p) n -> p c n", p=128)
        for a in range(2):
            for ch in range(NCH):
                cs = slice(ch * CW, (ch + 1) * CW)
                b = bank(K, "M")
                for c in range(NCK):
                    P.mm(b.full(), w[:, c, a * 128:(a + 1) * 128], K.hT[:, c, cs], start=(c == 0), stop=(c == NCK - 1))
                P.copy("act" if ch % 2 else "dve", qx[hp][:, a, cs], b.full())
    wo = [K.WL[1].sub(0, 2048, "p (a n) -> p a n", a=2), K.WL[2].sub(0, 2048, "p (a n) -> p a n", a=2)]
    for i in range(2):
        wload(K, wo[i].full(), d["xa_wo"][layer, i * 256:(i + 1) * 256, :], "(a p) n -> p a n", p=128)
    sc = 128.0 ** -0.5
    for ch in range(NCH):
        cs = slice(ch * CW, (ch + 1) * CW)
        for a in range(4):
            ob = bank(K, "O")
            db = bank(K, "Dn")
            for mt in range(2):
                y = bank(K, "Y")
                P.mm(y.full(), kx[:, a, mt * 128:(mt + 1) * 128], qx[a // 2][:, a % 2, cs])
                p_ = pt(K)
                P.act(p_.full(), y.full(), AF.Exp, scale=sc)
                P.mm(ob.full(), vx[:, mt, a * 128:(a + 1) * 128], p_.full(), start=(mt == 0), stop=(mt == 1))
                P.mm(db.full(), K.ones.full(), p_.full(), start=(mt == 0), stop=(mt == 1))
            rd = fw(K)
            P.recip(rd.full(), db.full())
            P.tt("dve", K.oT[:, a, :], ob.full(), rd.full(), ALU.mult)
        for n in range(NCK):
            b = bank(K, "M")
            for a in range(4):
                P.mm(b.full(), wo[a // 2][:, a % 2, n * 128:(n + 1) * 128], K.oT[:, a, :], start=(a == 0), stop=(a == 3))
            resid_add(K, n, cs, b.full())


def prep_mem(K):
    P = K.P
    memf = Buf(K.xT.h[:, :, 0:MEM], "xT", [128, NCK, MEM], F32, region="xT")
    memf.strides = [1, T, 1]
    load_T(K, K.d["mem"], memf, MEM)
    rmsnorm(K, memf, 13, K.memT, MEM)


def setup_mixer(K):
    P = K.P
    K.caus = P.sbuf("caus", [128, 128], BF16)
    K.strict = P.sbuf("strict", [128, 128], BF16)
    K.negu = P.sbuf("negu", [128, 128], BF16)
    K.zeros = P.sbuf("zeros", [128, 128], BF16)
    K.negones = P.sbuf("negones", [128, 128], BF16)
    K.LACC = [P.sbuf("lacc%d" % i, [128, CW], BF16) for i in range(4)]
    K.ACC = [P.sbuf("acc%d" % i, [128, CW], F32) for i in range(2)]
    wload(K, K.caus.full(), K.d["c_caus"])
    wload(K, K.strict.full(), K.d["c_strict"])
    wload(K, K.negu.full(), K.d["c_negu"])
    P.memset("dve", K.zeros.full(), 0.0)
    P.memset("dve", K.negones.full(), -1.0)


def proj_fm(K, w, ncols_tile, dst_fn, scale=None):
    P = K.P
    for ch in range(NCH):
        cs = slice(ch * CW, (ch + 1) * CW)
        b = bank(K, "M")
        for c in range(NCK):
            P.mm(b.full(), w[:, c, ncols_tile], K.hT[:, c, cs], start=(c == 0), stop=(c == NCK - 1))
        P.copy("act" if ch % 2 else "dve", dst_fn(cs), b.full())


def proj_tm(K, w, ncols, VT):
    P = K.P
    for tt in range(T // 128):
        b = bank(K, "M")
        for c in range(NCK):
            P.mm(b[:, :ncols], K.hT[:, c, tt * 128:(tt + 1) * 128], w[:, c, :], start=(c == 0), stop=(c == NCK - 1))
        P.copy("act" if tt % 2 else "dve", VT[:, tt, 0:ncols], b[:, :ncols])


def q_proj(K, wq, cs):
    P = K.P
    for jp in range(2):
        b = bank(K, "M")
        for c in range(NCK):
            P.mm(b.full(), wq[:, c, jp * 128:(jp + 1) * 128], K.hT[:, c, cs], start=(c == 0), stop=(c == NCK - 1))
        P.ts("dve", K.qT[:, jp, :], b.full(), 0.125, ALU.mult)


def out_proj(K, wout, cs):
    P = K.P
    for n in range(NCK):
        b = bank(K, "M")
        for jp in range(2):
            P.mm(b.full(), wout[:, jp, n * 128:(n + 1) * 128], K.oT[:, jp, :], start=(jp == 0), stop=(jp == 1))
        resid_add(K, n, cs, b.full())


def w_tile(K, dram_cols):
    w = ws(K).sub(0, 1024, "p (c n) -> p c n", c=8)
    wload(K, w.full(), dram_cols, "(c p) n -> p c n", p=128)
    return w


def sb_pass(K, o, pg):
    P, d = K.P, K.d
    win, wo_d = d["w_in_odd"], d["w_out_odd"]
    KT = K.KTf.sub(0, 2 * T, "p (a b) -> p a b", a=2)
    VT = K.VTf.sub(0, 2 * T, "p (a b) -> p a b", a=16)
    for jp in range(2):
        c0 = 1024 + (2 * pg + jp) * 128
        w = w_tile(K, win[o, :, c0:c0 + 128])
        proj_fm(K, w, slice(0, 128), lambda cs, jp=jp: KT[:, jp, cs])
    wv = K.WL[0].sub(0, 2048, "p (c n) -> p c n", c=8)
    wload(K, wv.full(), win[o, :, 2048 + pg * 256:2048 + (pg + 1) * 256], "(c p) n -> p c n", p=128)
    proj_tm(K, wv, 256, VT)
    wq = K.WL[1].sub(0, 2048, "p (c n) -> p c n", c=8)
    wload(K, wq.full(), win[o, :, pg * 256:(pg + 1) * 256], "(c p) n -> p c n", p=128)
    wout = K.WL[2].sub(0, 2048, "p (a n) -> p a n", a=2)
    wload(K, wout.full(), wo_d[o, pg * 256:(pg + 1) * 256, :], "(a p) n -> p a n", p=128)
    for ch in range(NCH):
        cs = slice(ch * CW, (ch + 1) * CW)
        set_roles(K, **ROLES_PROJ)
        q_proj(K, wq, cs)
        set_roles(K, **ROLES_SB)
        obs = [bank(K, "O"), bank(K, "O")]
        for jp in range(2):
            P.mm(obs[jp].full(), K.zeros.full(), K.qT[:, jp, :], start=True, stop=False)

        def stream(jp, half):
            hs = slice(half * 64, half * 64 + 64)
            ob = obs[jp]
            la = K.LACC[2 * jp + half]
            P.memset("pool", la.full(), 0.0)
            ktop = 4 * ch + 3
            first = True
            for kt in range(ktop, -1, -1):
                n0 = max(0, kt - 4 * ch) * 128
                N = CW - n0
                diag = kt >= 4 * ch
                y = K.ps[2 * jp + half]
                P.mm(y[:, :N], KT[hs, jp, kt * 128:(kt + 1) * 128], K.qT[hs, jp, n0:CW], start=True, stop=not diag)
                if diag:
                    P.mm(y[:, 0:128], K.ident.full(), K.strict.full(), start=False, stop=True)
                E = fw(K)
                P.act(E[:, :N], y[:, :N], AF.Exp)
                Lb = pt(K)
                P.act(Lb[:, :N], E[:, :N], AF.Ln, bias=1.0)
                yield
                P.mm(y[:, :N], K.negu.full(), Lb[:, :N], start=False, stop=first)
                if not first:
                    P.mm(y[:, :N], K.negones.full(), la[:, n0:CW], start=False, stop=True)
                A = pt(K)
                P.act(A[:, :N], y[:, :N], AF.Exp)
                yield
                P.mm(ob[hs, n0:CW], VT[:, kt, (2 * jp + half) * 64:(2 * jp + half + 1) * 64], A[:, :N],
                     start=False, stop=(kt == 0))
                if kt > 0:
                    P.tt("pool", la[:, n0:CW], la[:, n0:CW], Lb[:, :N], ALU.add)
                first = False
                yield

        run_interleaved([stream(jp, half) for jp in range(2) for half in range(2)], offsets=[0, 1, 2, 1])
        for jp in range(2):
            P.copy("dve", K.oT[:, jp, :], obs[jp].full())
        set_roles(K, **ROLES_PROJ)
        out_proj(K, wout, cs)


def fox_prep(K, e):
    P, d = K.P, K.d
    if not hasattr(K, "Ccol"):
        K.Ccol = P.sbuf("Ccol", [128, 128], F32)
        K.Coff = P.sbuf("Coff", [128, 128], F32)
        K.LF = P.sbuf("LF", [128, 128], F32)
        K.Stot = P.sbuf("Stot", [128, 128], F32)
        K.bfrep = P.sbuf("bfrep", [128, 128], F32)
        K.usum = P.sbuf("usum", [128, 128], F32)
        K.onesf = P.sbuf("onesf", [128, 128], F32)
        K.biasF = P.sbuf("biasF", [128, 128], F32)
        P.dma("sp", K.usum.full(), V_(d["c_usum"]))
        P.memset("dve", K.onesf.full(), 1.0)
    P.dma("sp", K.bfrep.full(), V_(d["bf_rep"][e]))
    wff = ws(K).sub(0, 64, "p (c n) -> p c n", c=8)
    wload(K, wff.full(), d["w_in_even"][e, :, 1536:1544], "(c p) n -> p c n", p=128)
    b = bank(K, "M")
    for tt in range(16):
        for c in range(NCK):
            P.mm(b[:, tt * 8:(tt + 1) * 8], K.hT[:, c, tt * 128:(tt + 1) * 128], wff[:, c, :],
                 start=(c == 0), stop=(c == NCK - 1))
    z = fw(K)
    P.tt("dve", z[:, 0:128], b[:, 0:128], K.bfrep.full(), ALU.add)
    P.act(z[:, 0:128], z[:, 0:128], AF.Exp, scale=-1.0)
    P.act(K.LF.full(), z[:, 0:128], AF.Ln, bias=1.0)
    b1 = bank(K, "M")
    P.mm(b1[:, 0:128], K.usum.full(), K.LF.full())
    b2 = bank(K, "M")
    P.mm(b2[:, 0:128], K.onesf.full(), K.LF.full())
    P.copy("dve", K.Stot.full(), b2[:, 0:128])
    P.memset("dve", K.Coff[:, 0:8], 0.0)
    for tt in range(1, 16):
        P.tt("dve", K.Coff[:, tt * 8:(tt + 1) * 8], K.Coff[:, (tt - 1) * 8:tt * 8], K.Stot[:, (tt - 1) * 8:tt * 8], ALU.add)
    P.tt("dve", K.Ccol.full(), b1[:, 0:128], K.Coff.full(), ALU.add)


def fox_pass(K, e, pg):
    P, d = K.P, K.d
    win, wo_d = d["w_in_even"], d["w_out_even"]
    KT = K.KTf.sub(0, 2 * T, "p (a b) -> p a b", a=2)
    VT = K.VTf.sub(0, 2 * T, "p (a b) -> p a b", a=16)
    for jp in range(2):
        c0 = 512 + (2 * pg + jp) * 128
        w = w_tile(K, win[e, :, c0:c0 + 128])
        proj_fm(K, w, slice(0, 128), lambda cs, jp=jp: KT[:, jp, cs])
    wv = K.WL[0].sub(0, 2048, "p (c n) -> p c n", c=8)
    wload(K, wv.full(), win[e, :, 1024 + pg * 256:1024 + (pg + 1) * 256], "(c p) n -> p c n", p=128)
    proj_tm(K, wv, 256, VT)
    wq = K.WL[1].sub(0, 2048, "p (c n) -> p c n", c=8)
    wload(K, wq.full(), win[e, :, pg * 256:(pg + 1) * 256], "(c p) n -> p c n", p=128)
    wout = K.WL[2].sub(0, 2048, "p (a n) -> p a n", a=2)
    wload(K, wout.full(), wo_d[e, pg * 256:(pg + 1) * 256, :], "(a p) n -> p a n", p=128)
    for ch in range(NCH):
        cs = slice(ch * CW, (ch + 1) * CW)
        set_roles(K, **ROLES_PROJ)
        q_proj(K, wq, cs)
        ktop = 4 * ch + 3
        for kt in range(ktop + 1):
            P.tt("dve", K.biasF[:, kt * 8 + 4 * pg:kt * 8 + 4 * pg + 4], K.Ccol[:, kt * 8 + 4 * pg:kt * 8 + 4 * pg + 4],
                 K.Coff[:, 4 * ch * 8 + 4 * pg:4 * ch * 8 + 4 * pg + 4], ALU.subtract)
        set_roles(K, **ROLES_ATT)
        obs = [bank(K, "O"), bank(K, "O")]
        dbs = [bank(K, "Dn"), bank(K, "Dn")]

        def stream(jp, half):
            hs = slice(half * 64, half * 64 + 64)
            h = 4 * pg + 2 * jp + half
            ob, db = obs[jp], dbs[jp]
            for kt in range(ktop + 1):
                n0 = max(0, kt - 4 * ch) * 128
                N = CW - n0
                diag = kt >= 4 * ch
                y = bank(K, "Y")
                P.mm(y[:, :N], KT[hs, jp, kt * 128:(kt + 1) * 128], K.qT[hs, jp, n0:CW], start=True, stop=not diag)
                if diag:
                    P.mm(y[:, 0:128], K.ident.full(), K.caus.full(), start=False, stop=True)
                A = pt(K)
                P.act(A[:, :N], y[:, :N], AF.Exp, bias=K.biasF[:, kt * 8 + h:kt * 8 + h + 1])
                yield
                P.mm(ob[hs, n0:CW], VT[:, kt, (2 * jp + half) * 64:(2 * jp + half + 1) * 64], A[:, :N],
                     start=(kt == 0), stop=(kt == ktop))
                P.mm(db[hs, n0:CW], K.ones[:, 0:64], A[:, :N], start=(kt == 0), stop=(kt == ktop))
                yield

        run_interleaved([stream(jp, half) for jp in range(2) for half in range(2)], offsets=[0, 1, 0, 1])
        for jp in range(2):
            rd = fw(K)
            P.recip(rd.full(), dbs[jp].full())
            P.tt("dve", K.oT[:, jp, :], obs[jp].full(), rd.full(), ALU.mult)
        set_roles(K, **ROLES_PROJ)
        out_proj(K, wout, cs)


def nsa_setup(K):
    P, d = K.P, K.d
    K.expb = P.sbuf("expb", [32, 16, 128], BF16)
    K.ovl = P.sbuf("ovl", [127, 33], BF16)
    K.kb = P.sbuf("kb", [128, 64], F32)
    K.ab = P.sbuf("ab", [128, 64], F32)
    K.sel12 = P.sbuf("sel12", [12, 6, 128], BF16)
    K.chcol = P.sbuf("chcol", [128, 8], F32)
    K.tbm = P.sbuf("tbm", [128, 8, 256], BF16)
    K.wbb = P.sbuf("wbb", [128, 8, 384], BF16)
    K.w2c = P.sbuf("w2c", [128, 64], BF16)
    K.peT = P.sbuf("peT", [128, 32], BF16)
    K.cKV = P.sbuf("cKV", [128, 1], F32)
    K.xg = P.sbuf("xg", [128, 127], F32)
    K.x2 = P.sbuf("x2", [128, 127], F32)
    K.hg = P.sbuf("hg", [128, 127], BF16)
    K.kcd = P.sbuf("kcd", [128, 127], BF16)
    K.vcs = P.sbuf("vcs", [127, 64], BF16)
    K.sacc = P.sbuf("sacc", [128, 4, 32], F32)
    K.fin = P.sbuf("fin", [128, 32], F32)
    K.top8 = P.sbuf("top8", [128, 8], F32)
    K.negm = P.sbuf("negm", [128, 4, 32], F32)
    K.rden4 = P.sbuf("rden4", [128, 4], F32)
    K.NEGT = [P.sbuf("negT%d" % i, [32, CW], BF16) for i in range(1)]
    K.CB = [P.sbuf("cb%d" % i, [127, CW], BF16) for i in range(2)]
    K.Gf = P.sbuf("Gf", [12, CW], F32)
    K.Ghi = P.sbuf("Ghi", [12, CW], BF16)
    K.Glo = P.sbuf("Glo", [12, CW], BF16)
    wload(K, K.expb.full(), d["c_expb"])
    wload(K, K.ovl.full(), d["c_ovl"])
    wload(K, K.sel12.full(), d["c_sel12"])
    wload(K, K.wbb.full(), d["t_wb"])
    P.dma("sp", K.kb.full(), V_(d["c_kb"]))
    P.dma("sp", K.ab.full(), V_(d["c_ab"]))
    P.dma("sp", K.chcol.full(), V_(d["t_ch"]))
    tbf = d["t_tb"].rearrange("p h a b -> p (h a b)")
    for hp in range(4):
        tmp = fw(K)
        P.dma("sp", tmp.full(), V_(tbf[:, hp * 512:(hp + 1) * 512]))
        for i in range(2):
            h = 2 * hp + i
            P.ts("dve", K.tbm[:, h, :], tmp[:, i * 256:(i + 1) * 256], K.chcol[:, h:h + 1], ALU.subtract)


def nsa_combine(K, jp, br, ob, db, guard):
    P = K.P
    gb = bank(K, "M")
    P.mm(gb.full(), K.sel12[:, jp * 3 + br, :], K.Ghi.full(), start=True, stop=False)
    P.mm(gb.full(), K.sel12[:, jp * 3 + br, :], K.Glo.full(), start=False, stop=True)
    rd = fw(K)
    if guard:
        P.ts("dve", rd.full(), db.full(), 1e-30, ALU.max)
        P.recip(rd.full(), rd.full())
    else:
        P.recip(rd.full(), db.full())
    P.tt("dve", rd.full(), gb.full(), rd.full(), ALU.mult)
    if br == 0:
        P.tt("dve", K.ACC[jp].full(), ob.full(), rd.full(), ALU.mult)
    else:
        tmp = fw(K)
        P.tt("dve", tmp.full(), ob.full(), rd.full(), ALU.mult)
        dst = K.oT[:, jp, :] if br == 2 else K.ACC[jp].full()
        P.tt("dve", dst, K.ACC[jp].full(), tmp.full(), ALU.add)


def nsa_pass(K, e, g):
    P, d = K.P, K.d
    if not hasattr(K, "expb"):
        nsa_setup(K)
    win, wo_d = d["w_in_even"], d["w_out_even"]
    KT = K.KTf.sub(0, 2 * T, "p (a b) -> p a b", a=2)
    VT = K.VTf.sub(0, 2048, "p (a b) -> p a b", a=16)
    CR = K.VTf.sub(2048, 4096)

    def dup_tile(ca, cb_):
        w = ws(K).sub(0, 1024, "p (c n) -> p c n", c=8)
        wload(K, w[:, :, 0:64], win[e, :, ca:ca + 64], "(c p) n -> p c n", p=128)
        wload(K, w[:, :, 64:128], win[e, :, cb_:cb_ + 64], "(c p) n -> p c n", p=128)
        return w

    ksl0, kwn0, kc0, vc0 = 2312 + g * 64, 2568 + g * 64, 2056 + g * 64, 2184 + g * 64
    vsl0, vwn0 = 2440 + g * 64, 2696 + g * 64
    proj_fm(K, dup_tile(ksl0, ksl0), slice(0, 128), lambda cs: KT[:, 0, cs])
    proj_fm(K, dup_tile(kwn0, kwn0), slice(0, 128), lambda cs: KT[:, 1, cs])
    proj_fm(K, dup_tile(kc0, vc0), slice(0, 128), lambda cs: CR[:, cs])
    proj_tm(K, dup_tile(vsl0, vwn0), 128, VT)
    W1 = ws(K).sub(0, 2048, "p (l e) -> p l e", l=32)
    wload(K, W1[0:64], d["cmp_w1_k"][e], "l d e -> d l e")
    wload(K, W1[64:128], d["cmp_w1_v"][e], "l d e -> d l e")
    wload(K, K.w2c[0:64, :], d["cmp_w2_k"][e])
    wload(K, K.w2c[64:128, :], d["cmp_w2_v"][e])
    wload(K, K.peT[0:64, :], d["cmp_peT_k"][e])
    wload(K, K.peT[64:128, :], d["cmp_peT_v"][e])
    hb = bank(K, "M")
    for half in range(2):
        hs = slice(half * 64, half * 64 + 64)
        for l in range(32):
            P.mm(hb[hs, 0:127], W1[hs, l, :], CR[hs, l:l + 16 * 126 + 1:16], start=(l == 0), stop=(l == 31))
        for l in range(32):
            P.mm(hb[hs, 128:129], W1[hs, l, :], K.peT[hs, l:l + 1], start=(l == 0), stop=(l == 31))
    P.copy("dve", K.cKV.full(), hb[:, 128:129])
    P.act(K.xg.full(), hb[:, 0:127], AF.Identity, bias=K.cKV[:, 0:1])
    P.tt("dve", K.x2.full(), K.xg.full(), K.xg.full(), ALU.mult)
    P.ts("dve", K.x2.full(), K.x2.full(), 0.044715, ALU.mult, 1.0, ALU.add)
    P.tt("dve", K.x2.full(), K.x2.full(), K.xg.full(), ALU.mult)
    P.act(K.x2.full(), K.x2.full(), AF.Sigmoid, scale=1.5957691216057308)
    P.tt("dve", K.hg.full(), K.x2.full(), K.xg.full(), ALU.mult)
    kb_ = bank(K, "M")
    P.mm(kb_[0:64, 0:127], K.w2c[0:64, :], K.hg[0:64, :])
    P.mm(kb_[64:128, 0:127], K.w2c[0:64, :], K.hg[0:64, :])
    P.copy("dve", K.kcd.full(), kb_[:, 0:127])
    vb_ = bank(K, "M")
    P.mm(vb_[0:127, 0:64], K.hg[64:128, :], K.w2c[64:128, :])
    P.copy("dve", K.vcs.full(), vb_[0:127, 0:64])
    wg = K.WL[0].sub(0, 96, "p (c n) -> p c n", c=8)
    wload(K, wg.full(), win[e, :, 2824 + g * 12:2824 + (g + 1) * 12], "(c p) n -> p c n", p=128)
    wq = K.WL[1].sub(0, 2048, "p (c n) -> p c n", c=8)
    wload(K, wq.full(), win[e, :, 1544 + g * 256:1544 + (g + 1) * 256], "(c p) n -> p c n", p=128)
    wout = K.WL[2].sub(0, 2048, "p (a n) -> p a n", a=2)
    wload(K, wout.full(), wo_d[e, 512 + g * 256:512 + (g + 1) * 256, :], "(a p) n -> p a n", p=128)
    for ch in range(NCH):
        cs = slice(ch * CW, (ch + 1) * CW)
        ktop = 4 * ch + 3
        set_roles(K, **ROLES_PROJ)
        q_proj(K, wq, cs)
        gb_ = bank(K, "M")
        for c in range(NCK):
            P.mm(gb_[0:12, :], wg[:, c, :], K.hT[:, c, cs], start=(c == 0), stop=(c == NCK - 1))
        P.act(K.Gf.full(), gb_[0:12, :], AF.Sigmoid)
        P.copy("dve", K.Ghi.full(), K.Gf.full())
        P.tt("dve", K.Glo.full(), K.Gf.full(), K.Ghi.full(), ALU.subtract)
        set_roles(K, **ROLES_ATT)
        obs = [bank(K, "O"), bank(K, "O")]
        dbs = [bank(K, "Dn"), bank(K, "Dn")]
        for jp in range(2):
            for half in range(2):
                hs = slice(half * 64, half * 64 + 64)
                hh = 4 * g + 2 * jp + half
                ob, db = obs[jp], dbs[jp]
                cb = rot(K, "CB", K.CB)
                wload(K, cb.full(), d["t_cb"][hh, :, cs])
                y = bank(K, "Y")
                P.mm(y[0:127, :], K.kcd[hs, :], K.qT[hs, jp, :], start=True, stop=False)
                P.mm(y[0:127, :], K.ident[0:127, 0:127], cb.full(), start=False, stop=True)
                Pc = pt(K)
                P.act(Pc[0:127, :], y[0:127, :], AF.Exp)
                P.mm(ob[hs, :], K.vcs.full(), Pc[0:127, :])
                P.mm(db[hs, :], K.ones[0:127, 0:64], Pc[0:127, :])
                s4 = bank(K, "M")
                for qi in range(4):
                    P.mm(s4[:, qi * 33:(qi + 1) * 33], Pc[0:127, qi * 128:(qi + 1) * 128], K.ovl.full())
                s4v = Buf(s4.h[:, 0:132].rearrange("p (a b) -> p a b", a=4), s4.name, [128, 4, 33], F32, region=s4.region)
                P.ts("dve", K.rden4.full(), s4v[:, :, 32], 1e-30, ALU.max)
                P.recip(K.rden4.full(), K.rden4.full())
                for qi in range(4):
                    if jp == 0 and half == 0:
                        P.ts("dve", K.sacc[:, qi, :], s4[:, qi * 33:qi * 33 + 32], K.rden4[:, qi:qi + 1], ALU.mult)
                    else:
                        P.stt("dve", K.sacc[:, qi, :], s4[:, qi * 33:qi * 33 + 32], K.rden4[:, qi:qi + 1],
                              K.sacc[:, qi, :], ALU.mult, ALU.add)
        for jp in range(2):
            nsa_combine(K, jp, 0, obs[jp], dbs[jp], guard=True)
        tps = bank(K, "M")
        for qi in range(4):
            lo = 32 - 2 * (4 * ch + qi)
            P.tt("dve", K.fin.full(), K.sacc[:, qi, :], K.kb[:, lo:lo + 32], ALU.mult)
            P.tt("dve", K.fin.full(), K.fin.full(), K.ab[:, lo:lo + 32], ALU.add)
            P.memset("dve", K.fin[:, 0:1], 1e4)
            P.add("dve", lambda e_: e_.max(out=K.top8.full().ap, in_=K.fin.full().ap), reads=[K.fin.full()],
                  writes=[K.top8.full()])
            P.ts("dve", K.negm[:, qi, :], K.fin.full(), K.top8[:, 7:8], ALU.is_lt, NEGM, ALU.mult)
            P.transpose(tps[0:32, qi * 128:(qi + 1) * 128], K.negm[:, qi, :], K.identf.full())
        negT = rot(K, "NEGT", K.NEGT)
        P.copy("dve", negT.full(), tps[0:32, :])
        obs = [bank(K, "O"), bank(K, "O")]
        dbs = [bank(K, "Dn"), bank(K, "Dn")]

        def slc_stream(jp, half):
            hs = slice(half * 64, half * 64 + 64)
            hh = 4 * g + 2 * jp + half
            ob, db = obs[jp], dbs[jp]
            for kt in range(ktop + 1):
                n0 = max(0, kt - 4 * ch) * 128
                N = CW - n0
                y = bank(K, "Y")
                P.mm(y[:, :N], KT[hs, 0, kt * 128:(kt + 1) * 128], K.qT[hs, jp, n0:CW], start=True, stop=False)
                if kt >= 4 * ch:
                    P.mm(y[:, 0:128], K.ident.full(), K.tbm[:, hh, 0:128], start=False, stop=False)
                if 4 * ch <= kt + 1 <= ktop:
                    o1 = (kt + 1 - 4 * ch) * 128 - n0
                    P.mm(y[:, o1:o1 + 128], K.ident.full(), K.tbm[:, hh, 128:256], start=False, stop=False)
                P.mm(y[:, :N], K.expb[:, kt, :], negT[:, n0:CW], start=False, stop=True)
                A = pt(K)
                P.act(A[:, :N], y[:, :N], AF.Exp, bias=K.chcol[:, hh:hh + 1])
                yield
                P.mm(ob[hs, n0:CW], VT[:, kt, 0:64], A[:, :N], start=(kt == 0), stop=(kt == ktop))
                P.mm(db[hs, n0:CW], K.ones[:, 0:64], A[:, :N], start=(kt == 0), stop=(kt == ktop))
                yield

        run_interleaved([slc_stream(jp, half) for jp in range(2) for half in range(2)])
        for jp in range(2):
            nsa_combine(K, jp, 1, obs[jp], dbs[jp], guard=False)
        obs = [bank(K, "O"), bank(K, "O")]
        dbs = [bank(K, "Dn"), bank(K, "Dn")]
        for jp in range(2):
            P.mm(obs[jp].full(), K.zeros.full(), K.qT[:, jp, :], start=True, stop=False)
            P.mm(dbs[jp].full(), K.zeros.full(), K.qT[:, jp, :], start=True, stop=False)

        def win_stream(jp, half):
            hs = slice(half * 64, half * 64 + 64)
            hh = 4 * g + 2 * jp + half
            ob, db = obs[jp], dbs[jp]
            kts = list(range(max(0, 4 * ch - 2), ktop + 1))
            for kt in kts:
                blo, bhi = max(kt, 4 * ch), min(kt + 2, ktop)
                c0, c1 = (blo - 4 * ch) * 128, (bhi - 4 * ch + 1) * 128
                Nw = c1 - c0
                y = bank(K, "Y")
                P.mm(y[:, :Nw], KT[hs, 1, kt * 128:(kt + 1) * 128], K.qT[hs, jp, c0:c1], start=True, stop=False)
                P.mm(y[:, :Nw], K.ident.full(), K.wbb[:, hh, (blo - kt) * 128:(bhi - kt + 1) * 128], start=False, stop=True)
                A = pt(K)
                P.act(A[:, :Nw], y[:, :Nw], AF.Exp)
                yield
                last = kt == kts[-1]
                P.mm(ob[hs, c0:c1], VT[:, kt, 64:128], A[:, :Nw], start=False, stop=last)
                P.mm(db[hs, c0:c1], K.ones[:, 0:64], A[:, :Nw], start=False, stop=last)
                yield

        run_interleaved([win_stream(jp, half) for jp in range(2) for half in range(2)])
        for jp in range(2):
            nsa_combine(K, jp, 2, obs[jp], dbs[jp], guard=False)
        set_roles(K, **ROLES_PROJ)
        out_proj(K, wout, cs)


def mixer_phase(K, layer):
    rmsnorm(K, K.xT, layer, K.hT, T)
    if layer % 2 == 1:
        for pg in range(4):
            sb_pass(K, layer // 2, pg)
    else:
        e = layer // 2
        fox_prep(K, e)
        for pg in range(2):
            fox_pass(K, e, pg)
        for g in range(2):
            nsa_pass(K, e, g)


def build(phases, final_norm=True, with_mem=True):
    nc = bass.Bass("TRN2", target_bir_lowering=False)
    K = setup(nc)
    load_consts(K)
    setup_mixer(K)
    if with_mem:
        prep_mem(K)
    load_T(K, K.d["x"], K.xT, T)
    for kind, layer in phases:
        if kind == "mixer":
            mixer_phase(K, layer)
        elif kind == "fox":
            rmsnorm(K, K.xT, layer, K.hT, T)
            fox_prep(K, layer // 2)
            for pg in range(2):
                fox_pass(K, layer // 2, pg)
        elif kind == "nsa":
            rmsnorm(K, K.xT, layer, K.hT, T)
            for g in range(2):
                nsa_pass(K, layer // 2, g)
        elif kind == "xattn":
            xattn_phase(K, layer)
        elif kind == "mlp":
            mlp_phase(K, layer)
    if final_norm:
        rmsnorm(K, K.xT, 12, K.xT, T)
    store_T(K, K.xT, K.d["out"])
    K.P.finish()
    return nc, K


def t5_bucket_np(dist):
    dist = np.maximum(dist, 0)
    d_f = np.maximum(dist, 1).astype(np.float32)
    large = 16 + (np.log(d_f / np.float32(16)) / np.float32(np.log(128 / 16)) * np.float32(16)).astype(np.int32)
    large = np.minimum(large, 31)
    return np.where(dist < 16, dist, large)


def host_consts(inp):
    f = np.float32
    c = {}
    gl = [inp["norm_mix_g"][i] for i in range(4)] + [inp["norm_xattn_g"][i] for i in range(4)] + \
         [inp["norm_mlp_g"][i] for i in range(4)] + [inp["final_norm_g"], inp["mem_norm_g"]]
    c["gains"] = np.ascontiguousarray(np.stack(gl, 0).reshape(14, 8, 128).transpose(2, 0, 1)).astype(f)
    i = np.arange(128)
    c["c_ident"] = np.eye(128, dtype=f)
    c["c_caus"] = np.where(i[None, :] >= i[:, None], 0.0, NEGM).astype(f)
    c["c_strict"] = np.where(i[None, :] > i[:, None], 0.0, NEGM).astype(f)
    c["c_negu"] = np.where(i[:, None] >= i[None, :], -1.0, 0.0).astype(f)
    c["c_usum"] = np.where(i[:, None] <= i[None, :], 1.0, 0.0).astype(f)
    n = np.arange(32)
    kt = np.arange(16)
    s_ = np.arange(128)
    c["c_expb"] = (n[:, None, None] == (kt[None, :, None] * 128 + s_[None, None, :]) // 64).astype(f)
    cc = np.arange(127)
    cs0 = cc * 16
    ce = cs0 + 31
    ss0 = n * 64
    se = ss0 + 63
    ovl = ((cs0[:, None] <= se[None, :]) & (ce[:, None] >= ss0[None, :])).astype(f)
    c["c_ovl"] = np.concatenate([ovl, np.ones((127, 1), f)], 1)
    q = np.arange(128)
    hi = (q >= 64).astype(np.int64)[:, None]
    m = np.arange(64)[None, :]
    forced = (m == 32 + hi) | (m == 31 + hi)
    future = m > 32 + hi
    c["c_kb"] = np.where(forced | future, 0.0, 1.0).astype(f)
    c["c_ab"] = np.where(forced, 1e4, np.where(future, -1.0, 0.0)).astype(f)
    sel = np.zeros((12, 6, 128), f)
    for jp in range(2):
        for half in range(2):
            for br in range(3):
                sel[(2 * jp + half) * 3 + br, jp * 3 + br, half * 64:(half + 1) * 64] = 1.0
    c["c_sel12"] = sel
    rb = inp["rel_bias"].astype(f)
    tb = np.zeros((128, 8, 2, 128), f)
    for dl in range(2):
        dist = dl * 128 + s_[None, :] - s_[:, None]
        val = rb[t5_bucket_np(dist)]
        val = np.where((dist >= 0)[:, :, None], val, NEGM)
        tb[:, :, dl, :] = val.transpose(0, 2, 1)
    c["t_tb"] = tb
    wb = np.zeros((128, 8, 384), f)
    for dl in range(3):
        dist = dl * 128 + s_[None, :] - s_[:, None]
        val = rb[t5_bucket_np(dist)]
        ok = (dist >= 0) & (dist < 256)
        val = np.where(ok[:, :, None], val, NEGM)
        wb[:, :, dl * 128:(dl + 1) * 128] = val.transpose(0, 2, 1)
    c["t_wb"] = wb
    tq = np.arange(T)
    dist = tq[None, :] - ce[:, None]
    val = rb[t5_bucket_np(dist)]
    val = np.where((dist >= 0)[:, :, None], val, NEGM)
    c["t_cb"] = np.ascontiguousarray(val.transpose(2, 0, 1)).astype(f)
    c["t_ch"] = np.ascontiguousarray(np.broadcast_to(rb[31][None, :], (128, 8))).astype(f)
    c["bf_rep"] = np.ascontiguousarray(np.broadcast_to(np.tile(inp["b_forget"].astype(f), (1, 16))[:, None, :], (2, 128, 128)))
    c["cmp_peT_k"] = np.ascontiguousarray(inp["cmp_pe_k"].transpose(0, 2, 1)).astype(f)
    c["cmp_peT_v"] = np.ascontiguousarray(inp["cmp_pe_v"].transpose(0, 2, 1)).astype(f)
    return c


PASS_KEYS = ["w_in_even", "w_out_even", "w_in_odd", "w_out_odd", "xa_wq", "xa_wkv", "xa_wo", "mlp_w1", "mlp_w2",
             "b_forget", "cmp_w1_k", "cmp_w1_v", "cmp_w2_k", "cmp_w2_v"]


def make_in_maps(inp, xs, ncores):
    c = host_consts(inp)
    base = {k: np.ascontiguousarray(inp[k], dtype=np.float32) for k in PASS_KEYS}
    base.update(c)
    maps = []
    for b in range(ncores):
        m = dict(base)
        m["x"] = np.ascontiguousarray(xs[b], dtype=np.float32)
        m["mem"] = np.ascontiguousarray(inp["mem"][b], dtype=np.float32)
        maps.append(m)
    return maps


_CACHE = {}


def kernel(**inputs):
    phases = []
    for layer in range(4):
        phases += [("mixer", layer), ("xattn", layer), ("mlp", layer)]
    if "nc" not in _CACHE:
        _CACHE["nc"] = build(phases, final_norm=True)[0]
    nc = _CACHE["nc"]
    n = 8
    maps = make_in_maps(inputs, inputs["x"], n)
    res = run_bass_kernel_spmd(nc, maps, core_ids=list(range(n)))
    return np.stack([np.asarray(r["out"], dtype=np.float32) for r in res.results], 0)
```

```python
from contextlib import ExitStack
import numpy as np
import concourse.bass as bass
import concourse.mybir as mybir

F32 = mybir.dt.float32
BF16 = mybir.dt.bfloat16
AF = mybir.ActivationFunctionType
ALU = mybir.AluOpType
DT_SIZE = {F32: 4, BF16: 2}
SEM_LIMIT = 16000


class View:
    __slots__ = ("ap", "region", "box")

    def __init__(self, ap, region=None, box=None):
        self.ap = ap
        self.region = region
        self.box = box


def _ovl(a, b):
    return a[0] < b[1] and b[0] < a[1] and a[2] < b[3] and b[2] < a[3]


def _cov(a, b):
    return a[0] <= b[0] and a[1] >= b[1] and a[2] <= b[2] and a[3] >= b[3]


class Buf:
    def __init__(self, ap, name, shape, dtype, region=None, base=0):
        self.h = ap
        self.name = name
        self.shape = list(shape)
        self.dtype = dtype
        self.esz = DT_SIZE[dtype]
        self.region = region or name
        self.base = base
        st = [1] * len(shape)
        for i in range(len(shape) - 2, 0, -1):
            st[i] = st[i + 1] * shape[i + 1]
        self.strides = st

    def __getitem__(self, idx):
        if not isinstance(idx, tuple):
            idx = (idx,)
        idx = list(idx) + [slice(None)] * (len(self.shape) - len(idx))
        lo = 0
        hi = 0
        p0, p1 = 0, self.shape[0]
        for d, (ix, n) in enumerate(zip(idx, self.shape)):
            if isinstance(ix, int):
                a, b, s = ix, ix + 1, 1
            else:
                a, b, s = ix.indices(n)
                cnt = max(0, (b - a + s - 1) // s)
                b = a + (cnt - 1) * s + 1
            assert 0 <= a < b <= n, (self.name, idx, self.shape)
            if d == 0:
                p0, p1 = a, b
            else:
                lo += a * self.strides[d]
                hi += (b - 1) * self.strides[d]
        box = (p0, p1, self.base + lo * self.esz, self.base + (hi + 1) * self.esz)
        return View(self.h[tuple(idx)], self.region, box)

    def full(self):
        return self[tuple(slice(None) for _ in self.shape)]

    def sub(self, c0, c1, pattern=None, **kw):
        assert len(self.shape) == 2
        ap = self.h[:, c0:c1]
        shape = [self.shape[0], c1 - c0]
        if pattern is not None:
            ap = ap.rearrange(pattern, **kw)
            shape = list(ap.shape)
        return Buf(ap, self.name, shape, self.dtype, region=self.region, base=self.base + c0 * self.esz)


class Op:
    __slots__ = ("eng", "fn", "tl", "idx", "deps", "waits", "signal", "semkey", "semval", "dma")


class Prog:
    ENG = ("pe", "act", "dve", "pool", "sp")

    def __init__(self, nc, nslots=8):
        self.nc = nc
        self.ops = []
        self.acc = {}
        self.tl_ops = {}
        self.dma_rr = {"sp": 0, "pool": 0, "act": 0}
        self.nslots = nslots
        self.stack = ExitStack()
        self.sb_bytes = 0

    def sbuf(self, name, shape, dtype):
        h = self.stack.enter_context(self.nc.sbuf_tensor("sb_" + name, list(shape), dtype))
        n = 1
        for s in shape[1:]:
            n *= s
        self.sb_bytes += n * DT_SIZE[dtype]
        return Buf(h[tuple(slice(None) for _ in shape)], name, shape, dtype)

    def psum(self, name, shape, dtype=F32):
        h = self.stack.enter_context(self.nc.psum_tensor(name, list(shape), dtype))
        return Buf(h[tuple(slice(None) for _ in shape)], name, shape, dtype)

    def add(self, eng, fn, reads=(), writes=(), dma=False):
        op = Op()
        opid = len(self.ops)
        op.eng = eng
        op.fn = fn
        op.dma = dma
        if dma:
            slot = self.dma_rr[eng] % self.nslots
            self.dma_rr[eng] += 1
            tl = ("dma", eng, slot)
        else:
            tl = eng
        lst = self.tl_ops.setdefault(tl, [])
        op.tl = tl
        op.idx = len(lst)
        deps = set()
        if dma and lst:
            deps.add(lst[-1])
        lst.append(opid)
        for v in reads:
            if v.region is None:
                continue
            W, Rd = self.acc.setdefault(v.region, ([], {}))
            for box, o in W:
                if _ovl(box, v.box):
                    deps.add(o)
        for v in writes:
            if v.region is None:
                continue
            W, Rd = self.acc.setdefault(v.region, ([], {}))
            for box, o in W:
                if _ovl(box, v.box):
                    deps.add(o)
            for (t, box), o in Rd.items():
                if _ovl(box, v.box):
                    deps.add(o)
        for v in writes:
            if v.region is None:
                continue
            W, Rd = self.acc[v.region]
            W[:] = [(b, o) for (b, o) in W if not _cov(v.box, b)]
            for k in [k for k in Rd if _cov(v.box, k[1])]:
                del Rd[k]
            W.append((v.box, opid))
        for v in reads:
            if v.region is None:
                continue
            self.acc[v.region][1][(tl, v.box)] = opid
        op.deps = deps
        op.waits = []
        op.signal = dma
        self.ops.append(op)
        return op

    def mm(self, out, lhsT, rhs, start=True, stop=True, **kw):
        rd = [lhsT, rhs] + ([] if start else [out])
        return self.add("pe", lambda e: e.matmul(out.ap, lhsT.ap, rhs.ap, start=start, stop=stop, **kw),
                        reads=rd, writes=[out])

    def transpose(self, out, in_, ident):
        return self.add("pe", lambda e: e.transpose(out.ap, in_.ap, ident.ap), reads=[in_, ident], writes=[out])

    def act(self, out, in_, func, bias=None, scale=1.0, accum=None):
        rd = [in_]
        kw = {}
        if bias is not None:
            if isinstance(bias, View):
                rd.append(bias)
                kw["bias"] = bias.ap
            else:
                kw["bias"] = bias
        if isinstance(scale, View):
            rd.append(scale)
            kw["scale"] = scale.ap
        else:
            kw["scale"] = scale
        wr = [out]
        if accum is not None:
            kw["accum_out"] = accum.ap
            wr.append(accum)
        return self.add("act", lambda e: e.activation(out.ap, in_.ap, func, **kw), reads=rd, writes=wr)

    def tt(self, eng, out, in0, in1, op):
        return self.add(eng, lambda e: e.tensor_tensor(out.ap, in0.ap, in1.ap, op), reads=[in0, in1], writes=[out])

    def ts(self, eng, out, in0, s1, op0, s2=None, op1=None, accum=None):
        rd = [in0]
        a1 = s1
        a2 = s2
        if isinstance(s1, View):
            rd.append(s1)
            a1 = s1.ap
        if isinstance(s2, View):
            rd.append(s2)
            a2 = s2.ap
        kw = {}
        if op1 is not None:
            kw["op1"] = op1
        wr = [out]
        if accum is not None:
            kw["accum_out"] = accum.ap
            wr.append(accum)
        return self.add(eng, lambda e: e.tensor_scalar(out.ap, in0.ap, a1, a2, op0, **kw), reads=rd, writes=wr)

    def stt(self, eng, out, in0, scalar, in1, op0, op1):
        rd = [in0, in1]
        sc = scalar
        if isinstance(scalar, View):
            rd.append(scalar)
            sc = scalar.ap
        return self.add(eng, lambda e: e.scalar_tensor_tensor(out.ap, in0.ap, sc, in1.ap, op0, op1),
                        reads=rd, writes=[out])

    def copy(self, eng, out, in_):
        if eng == "act":
            return self.add("act", lambda e: e.copy(out.ap, in_.ap), reads=[in_], writes=[out])
        return self.add(eng, lambda e: e.tensor_copy(out.ap, in_.ap), reads=[in_], writes=[out])

    def recip(self, out, in_):
        return self.add("dve", lambda e: e.reciprocal(out.ap, in_.ap), reads=[in_], writes=[out])

    def memset(self, eng, out, val):
        return self.add(eng, lambda e: e.memset(out.ap, val), writes=[out])

    def lock(self, op, lockview):
        W, Rd = self.acc.setdefault(lockview.region, ([], {}))
        for box, o in W:
            op.deps.add(o)
        W[:] = [(lockview.box, self.ops.index(op) if False else len(self.ops) - 1)]

    def dma(self, issuer, out, in_):
        return self.add(issuer, lambda e: e.dma_start(out=out.ap, in_=in_.ap), reads=[in_], writes=[out], dma=True)

    def finish(self, final_issuer="sp"):
        nc = self.nc
        ops = self.ops
        fin = Op()
        fin.eng = final_issuer
        fin.fn = None
        fin.dma = False
        fin.tl = None
        fin.idx = -1
        fin.deps = set(l[-1] for l in self.tl_ops.values() if l)
        fin.waits = []
        fin.signal = False
        ops.append(fin)

        clock = {e: {} for e in self.ENG}
        snap = {}
        for op in ops:
            ck = clock[op.eng]
            need = {}
            for d in op.deps:
                o = ops[d]
                if need.get(o.tl, -1) < o.idx:
                    need[o.tl] = o.idx
            for tl, idx in sorted(need.items(), key=lambda kv: str(kv[0])):
                if tl == "pe" and op.eng == "pe" and not op.dma:
                    continue
                if ck.get(tl, -1) >= idx:
                    continue
                op.waits.append((tl, idx))
                for k, v in snap[(tl, idx)].items():
                    if ck.get(k, -1) < v:
                        ck[k] = v
                if ck.get(tl, -1) < idx:
                    ck[tl] = idx
                ops[self.tl_ops[tl][idx]].signal = True
            if op.tl is not None:
                s = dict(ck)
                s[op.tl] = op.idx
                snap[(op.tl, op.idx)] = s
        semkeys = set()
        for tl, lst in self.tl_ops.items():
            k = 0
            for oid in lst:
                o = ops[oid]
                if isinstance(tl, tuple):
                    o.semkey = (tl, 0)
                    o.semval = 16 * (o.idx + 1)
                    semkeys.add(o.semkey)
                elif o.signal:
                    o.semkey = (tl, k // SEM_LIMIT)
                    o.semval = k % SEM_LIMIT + 1
                    semkeys.add(o.semkey)
                    k += 1
        st = self.stack
        sems = {}
        for i, key in enumerate(sorted(semkeys, key=str)):
            sems[key] = st.enter_context(nc.semaphore("s%d" % i))
        self.n_sems = len(sems)
        block = st.enter_context(nc.Block())
        eng_ops = {e: [o for o in ops if o.eng == e] for e in self.ENG}
        tl_ops = self.tl_ops

        def mk(ename):
            def body(e):
                for op in eng_ops[ename]:
                    for tl, idx in op.waits:
                        t = ops[tl_ops[tl][idx]]
                        e.wait_ge(sems[t.semkey], t.semval)
                    if op.fn is not None:
                        ins = op.fn(e)
                        if op.signal:
                            ins.then_inc(sems[op.semkey], 16 if op.dma else 1)
            return body

        block.tensor(mk("pe"))
        block.scalar(mk("act"))
        block.vector(mk("dve"))
        block.gpsimd(mk("pool"))
        block.sync(mk("sp"))
        st.close()
        return nc


from concourse.bass_utils import run_bass_kernel_spmd

T = 2048
D = 1024
NCK = 8
CW = 512
NCH = T // CW
MEM = 256
NEGM = -30000.0
EPS = 1e-6


class Ctx:
    pass


def V_(ap):
    return View(ap)


def setup(nc, dbg_out=None):
    K = Ctx()
    K.nc = nc
    P = K.P = Prog(nc)
    din = lambda n, s: nc.dram_tensor(n, list(s), F32, kind="ExternalInput").ap()
    K.d = d = {}
    for n, s in [("x", (T, D)), ("mem", (MEM, D)), ("gains", (128, 14, 8)),
                 ("w_in_even", (2, D, 2848)), ("w_out_even", (2, D, D)), ("w_in_odd", (2, D, 3072)),
                 ("w_out_odd", (2, D, D)), ("xa_wq", (4, D, 512)), ("xa_wkv", (4, D, 1024)),
                 ("xa_wo", (4, 512, D)), ("mlp_w1", (4, D, 4096)), ("mlp_w2", (4, 4096, D)),
                 ("b_forget", (2, 8)), ("cmp_w1_k", (2, 32, 64, 64)), ("cmp_w1_v", (2, 32, 64, 64)),
                 ("cmp_w2_k", (2, 64, 64)), ("cmp_w2_v", (2, 64, 64)),
                 ("cmp_peT_k", (2, 64, 32)), ("cmp_peT_v", (2, 64, 32)),
                 ("bf_rep", (2, 128, 128)), ("c_ident", (128, 128)), ("c_caus", (128, 128)), ("c_strict", (128, 128)),
                 ("c_negu", (128, 128)), ("c_usum", (128, 128)), ("c_expb", (32, 16, 128)),
                 ("c_ovl", (127, 33)), ("c_kb", (128, 64)), ("c_ab", (128, 64)), ("c_sel12", (12, 6, 128)),
                 ("t_tb", (128, 8, 2, 128)), ("t_wb", (128, 8, 384)), ("t_cb", (8, 127, T)), ("t_ch", (128, 8))]:
        d[n] = din(n, s)
    d["out"] = nc.dram_tensor("out", [T, D], F32, kind="ExternalOutput").ap()
    K.xT = P.sbuf("xT", [128, NCK, T], F32)
    K.hT = P.sbuf("hT", [128, NCK, T], BF16)
    K.KTf = P.sbuf("KT", [128, 2 * T], BF16)
    K.VTf = P.sbuf("VT", [128, 2 * T], BF16)
    K.qT = P.sbuf("qT", [128, 2, CW], BF16)
    K.oT = P.sbuf("oT", [128, 4, CW], BF16)
    K.WL = [P.sbuf("WL%d" % i, [128, 2048], BF16) for i in range(3)]
    K.WS = [P.sbuf("WS%d" % i, [128, 2048], BF16) for i in range(3)]
    K.PT = [P.sbuf("PT%d" % i, [128, CW], BF16) for i in range(8)]
    K.FW = [P.sbuf("FW%d" % i, [128, CW], F32) for i in range(6)]
    K.rr = {}
    K.gains = P.sbuf("gains", [128, 14, 8], F32)
    K.identf = P.sbuf("identf", [128, 128], F32)
    K.ident = P.sbuf("ident", [128, 128], BF16)
    K.ones = P.sbuf("ones", [128, 128], BF16)
    K.memT = P.sbuf("memT", [128, NCK, MEM], BF16)
    K.eps_col = P.sbuf("eps_col", [128, 1], F32)
    K.ps = [P.psum("ps%d" % i, [128, CW]) for i in range(8)]
    K.ps3 = [Buf(b.h.rearrange("p (a b) -> p a b", a=4), b.name, [128, 4, 128], F32) for b in K.ps]
    K.roles = {"Y": [0, 1], "O": [2, 3], "Dn": [4, 5], "M": [6, 7]}
    return K


def set_roles(K, **kw):
    K.roles = dict(kw)


def run_interleaved(gens, offsets=None):
    gens = list(gens)
    if offsets:
        for g_, o in zip(list(gens), offsets):
            for _ in range(o):
                try:
                    next(g_)
                except StopIteration:
                    break
    while gens:
        for g_ in list(gens):
            try:
                next(g_)
            except StopIteration:
                gens.remove(g_)


ROLES_PROJ = dict(Y=[0, 1], O=[2, 3], Dn=[4, 5], M=[6, 7])
ROLES_ATT = dict(Y=[0, 1, 2], O=[3, 4], Dn=[5, 6], M=[7])
ROLES_SB = dict(Y=[0, 1, 2, 3], O=[4, 5], Dn=[6, 7], M=[7, 6])


def rot(K, key, lst):
    i = K.rr.get(key, 0)
    K.rr[key] = i + 1
    return lst[i % len(lst)]


def bank(K, role):
    return K.ps[rot(K, "ps" + role, K.roles[role])]


def pt(K):
    return rot(K, "PT", K.PT)


def fw(K):
    return rot(K, "FW", K.FW)


def ws(K):
    return rot(K, "WS", K.WS)


def load_consts(K):
    P, d = K.P, K.d
    P.dma("sp", K.gains.full(), V_(d["gains"]))
    P.dma("sp", K.identf.full(), V_(d["c_ident"]))
    P.dma("pool", K.ident.full(), V_(d["c_ident"]))
    P.memset("dve", K.ones.full(), 1.0)
    P.memset("dve", K.eps_col.full(), EPS)


def load_T(K, src_d, dst, ntok, dst_is_bf16_norm=False):
    P = K.P
    for tt in range(ntok // 128):
        for half in range(2):
            st = fw(K)
            P.dma("sp", st.full(), V_(src_d[tt * 128:(tt + 1) * 128, half * 512:(half + 1) * 512]))
            bi = rot(K, "psM", K.roles["M"])
            for q in range(4):
                P.transpose(K.ps[bi][:, q * 128:(q + 1) * 128], st[:, q * 128:(q + 1) * 128], K.identf.full())
            eng = "dve" if (tt + half) % 2 == 0 else "act"
            P.copy(eng, dst[:, half * 4:(half + 1) * 4, tt * 128:(tt + 1) * 128], K.ps3[bi].full())


def store_T(K, src, dst_d):
    P = K.P
    for tt in range(T // 128):
        for half in range(2):
            bi = rot(K, "psM", K.roles["M"])
            for q in range(4):
                P.transpose(K.ps[bi][:, q * 128:(q + 1) * 128], src[:, half * 4 + q, tt * 128:(tt + 1) * 128],
                            K.identf.full())
            st = fw(K)
            eng = "dve" if (tt + half) % 2 == 0 else "act"
            P.copy(eng, st.full(), K.ps[bi].full())
            P.dma("sp", V_(dst_d[tt * 128:(tt + 1) * 128, half * 512:(half + 1) * 512]), st.full())


def rmsnorm(K, src, gidx, dst, ntok):
    P = K.P
    W = min(CW, ntok)
    for ch in range(ntok // W):
        cs = slice(ch * W, (ch + 1) * W)
        b = bank(K, "M")
        for c in range(NCK):
            sq = pt(K)
            P.act(sq[:, :W], src[:, c, cs], AF.Square)
            P.mm(b[:, :W], K.ones.full(), sq[:, :W], start=(c == 0), stop=(c == NCK - 1))
        r = fw(K)
        P.act(r[:, :W], b[:, :W], AF.Sqrt, bias=K.eps_col[:, 0:1], scale=1.0 / D)
        P.recip(r[:, :W], r[:, :W])
        for c in range(NCK):
            P.stt("dve", dst[:, c, cs], src[:, c, cs], K.gains[:, gidx, c:c + 1], r[:, :W], ALU.mult, ALU.mult)


def wload(K, buf, dram_ap, pattern=None, **kw):
    ap = dram_ap if pattern is None else dram_ap.rearrange(pattern, **kw)
    K.P.dma("pool", buf, V_(ap))


def resid_add(K, n, cs, psv):
    K.P.tt("dve", K.xT[:, n, cs], K.xT[:, n, cs], psv, ALU.add)


def mlp_phase(K, layer):
    P, d = K.P, K.d
    set_roles(K, **ROLES_PROJ)
    rmsnorm(K, K.xT, 8 + layer, K.hT, T)
    set_roles(K, Y=[0, 1, 2, 3], O=[4, 5, 6, 7], Dn=[4, 5], M=[6, 7])
    U = [K.KTf.sub(0, 2 * T, "p (a b) -> p a b", a=2), K.VTf.sub(0, 2 * T, "p (a b) -> p a b", a=2)]
    NG = 16

    def stage1(g):
        w1 = ws(K).sub(0, 2048, "p (c n) -> p c n", c=8)
        wload(K, w1.full(), d["mlp_w1"][layer, :, g * 256:(g + 1) * 256], "(c p) n -> p c n", p=128)
        u = U[g % 2]
        for fi in range(2):
            for ch in range(NCH):
                cs = slice(ch * CW, (ch + 1) * CW)
                b = bank(K, "Y")
                for c in range(NCK):
                    P.mm(b.full(), w1[:, c, fi * 128:(fi + 1) * 128], K.hT[:, c, cs], start=(c == 0), stop=(c == NCK - 1))
                r = fw(K)
                P.act(r.full(), b.full(), AF.Relu)
                P.act(u[:, fi, cs], r.full(), AF.Square)

    def stage2(g):
        w2 = ws(K).sub(0, 2048, "p (a n) -> p a n", a=2)
        wload(K, w2.full(), d["mlp_w2"][layer, g * 256:(g + 1) * 256, :], "(a p) n -> p a n", p=128)
        u = U[g % 2]
        for n in range(NCK):
            for ch in range(NCH):
                cs = slice(ch * CW, (ch + 1) * CW)
                b = bank(K, "O")
                for fi in range(2):
                    P.mm(b.full(), w2[:, fi, n * 128:(n + 1) * 128], u[:, fi, cs], start=(fi == 0), stop=(fi == 1))
                resid_add(K, n, cs, b.full())

    stage1(0)
    for g in range(NG):
        if g + 1 < NG:
            stage1(g + 1)
        stage2(g)
    set_roles(K, **ROLES_PROJ)


def xattn_phase(K, layer):
    P, d = K.P, K.d
    rmsnorm(K, K.xT, 4 + layer, K.hT, T)
    kx = K.WL[0].sub(0, 1024, "p (a m) -> p a m", a=4)
    vx = K.WL[0].sub(1024, 2048, "p (a n) -> p a n", a=2)
    for a in range(4):
        w = ws(K).sub(0, 1024, "p (c n) -> p c n", c=8)
        wload(K, w.full(), d["xa_wkv"][layer, :, a * 128:(a + 1) * 128], "(c p) n -> p c n", p=128)
        b = bank(K, "M")
        for c in range(NCK):
            P.mm(b[:, :MEM], w[:, c, :], K.memT[:, c, :], start=(c == 0), stop=(c == NCK - 1))
        P.copy("dve", kx[:, a, :], b[:, :MEM])
    for hv in range(2):
        w = ws(K).sub(0, 2048, "p (c n) -> p c n", c=8)
        wload(K, w.full(), d["xa_wkv"][layer, :, 512 + hv * 256:512 + (hv + 1) * 256], "(c p) n -> p c n", p=128)
        for mt in range(2):
            b = bank(K, "M")
            for c in range(NCK):
                P.mm(b[:, :256], K.memT[:, c, mt * 128:(mt + 1) * 128], w[:, c, :], start=(c == 0), stop=(c == NCK - 1))
            P.copy("dve", vx[:, mt, hv * 256:(hv + 1) * 256], b[:, :256])
    qx = [K.KTf.sub(0, 2 * T, "p (a b) -> p a b", a=2), K.VTf.sub(0, 2 * T, "p (a b) -> p a b", a=2)]
    for hp in range(2):
        w = ws(K).sub(0, 2048, "p (c n) -> p c n", c=8)
        wload(K, w.full(), d["xa_wq"][layer, :, hp * 256:(hp + 1) * 256], "(c p) n -> p c n", p=128)
        for a in range(2):
            for ch in range(NCH):
                cs = slice(ch * CW, (ch + 1) * CW)
                b = bank(K, "M")
                for c in range(NCK):
                    P.mm(b.full(), w[:, c, a * 128:(a + 1) * 128], K.hT[:, c, cs], start=(c == 0), stop=(c == NCK - 1))
                P.copy("act" if ch % 2 else "dve", qx[hp][:, a, cs], b.full())
    wo = [K.WL[1].sub(0, 2048, "p (a n) -> p a n", a=2), K.WL[2].sub(0, 2048, "p (a n) -> p a n", a=2)]
    for i in range(2):
        wload(K, wo[i].full(), d["xa_wo"][layer, i * 256:(i + 1) * 256, :], "(a p) n -> p a n", p=128)
    sc = 128.0 ** -0.5
    for ch in range(NCH):
        cs = slice(ch * CW, (ch + 1) * CW)
        for a in range(4):
            ob = bank(K, "O")
            db = bank(K, "Dn")
            for mt in range(2):
                y = bank(K, "Y")
                P.mm(y.full(), kx[:, a, mt * 128:(mt + 1) * 128], qx[a // 2][:, a % 2, cs])
                p_ = pt(K)
                P.act(p_.full(), y.full(), AF.Exp, scale=sc)
                P.mm(ob.full(), vx[:, mt, a * 128:(a + 1) * 128], p_.full(), start=(mt == 0), stop=(mt == 1))
                P.mm(db.full(), K.ones.full(), p_.full(), start=(mt == 0), stop=(mt == 1))
            rd = fw(K)
            P.recip(rd.full(), db.full())
            P.tt("dve", K.oT[:, a, :], ob.full(), rd.full(), ALU.mult)
        for n in range(NCK):
            b = bank(K, "M")
            for a in range(4):
                P.mm(b.full(), wo[a // 2][:, a % 2, n * 128:(n + 1) * 128], K.oT[:, a, :], start=(a == 0), stop=(a == 3))
            resid_add(K, n, cs, b.full())


def prep_mem(K):
    P = K.P
    memf = Buf(K.xT.h[:, :, 0:MEM], "xT", [128, NCK, MEM], F32, region="xT")
    memf.strides = [1, T, 1]
    load_T(K, K.d["mem"], memf, MEM)
    rmsnorm(K, memf, 13, K.memT, MEM)


def setup_mixer(K):
    P = K.P
    K.caus = P.sbuf("caus", [128, 128], BF16)
    K.strict = P.sbuf("strict", [128, 128], BF16)
    K.negu = P.sbuf("negu", [128, 128], BF16)
    K.zeros = P.sbuf("zeros", [128, 128], BF16)
    K.negones = P.sbuf("negones", [128, 128], BF16)
    K.LACC = [P.sbuf("lacc%d" % i, [128, CW], BF16) for i in range(4)]
    K.ACC = [P.sbuf("acc%d" % i, [128, CW], F32) for i in range(2)]
    wload(K, K.caus.full(), K.d["c_caus"])
    wload(K, K.strict.full(), K.d["c_strict"])
    wload(K, K.negu.full(), K.d["c_negu"])
    P.memset("dve", K.zeros.full(), 0.0)
    P.memset("dve", K.negones.full(), -1.0)


def proj_fm(K, w, ncols_tile, dst_fn, scale=None):
    P = K.P
    for ch in range(NCH):
        cs = slice(ch * CW, (ch + 1) * CW)
        b = bank(K, "M")
        for c in range(NCK):
            P.mm(b.full(), w[:, c, ncols_tile], K.hT[:, c, cs], start=(c == 0), stop=(c == NCK - 1))
        P.copy("act" if ch % 2 else "dve", dst_fn(cs), b.full())


def proj_tm(K, w, ncols, VT):
    P = K.P
    for tt in range(T // 128):
        b = bank(K, "M")
        for c in range(NCK):
            P.mm(b[:, :ncols], K.hT[:, c, tt * 128:(tt + 1) * 128], w[:, c, :], start=(c == 0), stop=(c == NCK - 1))
        P.copy("act" if tt % 2 else "dve", VT[:, tt, 0:ncols], b[:, :ncols])


def q_proj(K, wq, cs):
    P = K.P
    for jp in range(2):
        b = bank(K, "M")
        for c in range(NCK):
            P.mm(b.full(), wq[:, c, jp * 128:(jp + 1) * 128], K.hT[:, c, cs], start=(c == 0), stop=(c == NCK - 1))
        P.ts("dve", K.qT[:, jp, :], b.full(), 0.125, ALU.mult)


def out_proj(K, wout, cs):
    P = K.P
    for n in range(NCK):
        b = bank(K, "M")
        for jp in range(2):
            P.mm(b.full(), wout[:, jp, n * 128:(n + 1) * 128], K.oT[:, jp, :], start=(jp == 0), stop=(jp == 1))
        resid_add(K, n, cs, b.full())


def w_tile(K, dram_cols):
    w = ws(K).sub(0, 1024, "p (c n) -> p c n", c=8)
    wload(K, w.full(), dram_cols, "(c p) n -> p c n", p=128)
    return w


def sb_pass(K, o, pg):
    P, d = K.P, K.d
    win, wo_d = d["w_in_odd"], d["w_out_odd"]
    KT = K.KTf.sub(0, 2 * T, "p (a b) -> p a b", a=2)
    VT = K.VTf.sub(0, 2 * T, "p (a b) -> p a b", a=16)
    for jp in range(2):
        c0 = 1024 + (2 * pg + jp) * 128
        w = w_tile(K, win[o, :, c0:c0 + 128])
        proj_fm(K, w, slice(0, 128), lambda cs, jp=jp: KT[:, jp, cs])
    wv = K.WL[0].sub(0, 2048, "p (c n) -> p c n", c=8)
    wload(K, wv.full(), win[o, :, 2048 + pg * 256:2048 + (pg + 1) * 256], "(c p) n -> p c n", p=128)
    proj_tm(K, wv, 256, VT)
    wq = K.WL[1].sub(0, 2048, "p (c n) -> p c n", c=8)
    wload(K, wq.full(), win[o, :, pg * 256:(pg + 1) * 256], "(c p) n -> p c n", p=128)
    wout = K.WL[2].sub(0, 2048, "p (a n) -> p a n", a=2)
    wload(K, wout.full(), wo_d[o, pg * 256:(pg + 1) * 256, :], "(a p) n -> p a n", p=128)
    for ch in range(NCH):
        cs = slice(ch * CW, (ch + 1) * CW)
        set_roles(K, **ROLES_PROJ)
        q_proj(K, wq, cs)
        set_roles(K, **ROLES_SB)
        obs = [bank(K, "O"), bank(K, "O")]
        for jp in range(2):
            P.mm(obs[jp].full(), K.zeros.full(), K.qT[:, jp, :], start=True, stop=False)

        def stream(jp, half):
            hs = slice(half * 64, half * 64 + 64)
            ob = obs[jp]
            la = K.LACC[2 * jp + half]
            P.memset("pool", la.full(), 0.0)
            ktop = 4 * ch + 3
            first = True
            for kt in range(ktop, -1, -1):
                n0 = max(0, kt - 4 * ch) * 128
                N = CW - n0
                diag = kt >= 4 * ch
                y = K.ps[2 * jp + half]
                P.mm(y[:, :N], KT[hs, jp, kt * 128:(kt + 1) * 128], K.qT[hs, jp, n0:CW], start=True, stop=not diag)
                if diag:
                    P.mm(y[:, 0:128], K.ident.full(), K.strict.full(), start=False, stop=True)
                E = fw(K)
                P.act(E[:, :N], y[:, :N], AF.Exp)
                Lb = pt(K)
                P.act(Lb[:, :N], E[:, :N], AF.Ln, bias=1.0)
                yield
                P.mm(y[:, :N], K.negu.full(), Lb[:, :N], start=False, stop=first)
                if not first:
                    P.mm(y[:, :N], K.negones.full(), la[:, n0:CW], start=False, stop=True)
                A = pt(K)
                P.act(A[:, :N], y[:, :N], AF.Exp)
                yield
                P.mm(ob[hs, n0:CW], VT[:, kt, (2 * jp + half) * 64:(2 * jp + half + 1) * 64], A[:, :N],
                     start=False, stop=(kt == 0))
                if kt > 0:
                    P.tt("pool", la[:, n0:CW], la[:, n0:CW], Lb[:, :N], ALU.add)
                first = False
                yield

        run_interleaved([stream(jp, half) for jp in range(2) for half in range(2)], offsets=[0, 1, 2, 1])
        for jp in range(2):
            P.copy("dve", K.oT[:, jp, :], obs[jp].full())
        set_roles(K, **ROLES_PROJ)
        out_proj(K, wout, cs)


def fox_prep(K, e):
    P, d = K.P, K.d
    if not hasattr(K, "Ccol"):
        K.Ccol = P.sbuf("Ccol", [128, 128], F32)
        K.Coff = P.sbuf("Coff", [128, 128], F32)
        K.LF = P.sbuf("LF", [128, 128], F32)
        K.Stot = P.sbuf("Stot", [128, 128], F32)
        K.bfrep = P.sbuf("bfrep", [128, 128], F32)
        K.usum = P.sbuf("usum", [128, 128], F32)
        K.onesf = P.sbuf("onesf", [128, 128], F32)
        K.biasF = P.sbuf("biasF", [128, 128], F32)
        P.dma("sp", K.usum.full(), V_(d["c_usum"]))
        P.memset("dve", K.onesf.full(), 1.0)
    P.dma("sp", K.bfrep.full(), V_(d["bf_rep"][e]))
    wff = ws(K).sub(0, 64, "p (c n) -> p c n", c=8)
    wload(K, wff.full(), d["w_in_even"][e, :, 1536:1544], "(c p) n -> p c n", p=128)
    b = bank(K, "M")
    for tt in range(16):
        for c in range(NCK):
            P.mm(b[:, tt * 8:(tt + 1) * 8], K.hT[:, c, tt * 128:(tt + 1) * 128], wff[:, c, :],
                 start=(c == 0), stop=(c == NCK - 1))
    z = fw(K)
    P.tt("dve", z[:, 0:128], b[:, 0:128], K.bfrep.full(), ALU.add)
    P.act(z[:, 0:128], z[:, 0:128], AF.Exp, scale=-1.0)
    P.act(K.LF.full(), z[:, 0:128], AF.Ln, bias=1.0)
    b1 = bank(K, "M")
    P.mm(b1[:, 0:128], K.usum.full(), K.LF.full())
    b2 = bank(K, "M")
    P.mm(b2[:, 0:128], K.onesf.full(), K.LF.full())
    P.copy("dve", K.Stot.full(), b2[:, 0:128])
    P.memset("dve", K.Coff[:, 0:8], 0.0)
    for tt in range(1, 16):
        P.tt("dve", K.Coff[:, tt * 8:(tt + 1) * 8], K.Coff[:, (tt - 1) * 8:tt * 8], K.Stot[:, (tt - 1) * 8:tt * 8], ALU.add)
    P.tt("dve", K.Ccol.full(), b1[:, 0:128], K.Coff.full(), ALU.add)


def fox_pass(K, e, pg):
    P, d = K.P, K.d
    win, wo_d = d["w_in_even"], d["w_out_even"]
    KT = K.KTf.sub(0, 2 * T, "p (a b) -> p a b", a=2)
    VT = K.VTf.sub(0, 2 * T, "p (a b) -> p a b", a=16)
    for jp in range(2):
        c0 = 512 + (2 * pg + jp) * 128
        w = w_tile(K, win[e, :, c0:c0 + 128])
        proj_fm(K, w, slice(0, 128), lambda cs, jp=jp: KT[:, jp, cs])
    wv = K.WL[0].sub(0, 2048, "p (c n) -> p c n", c=8)
    wload(K, wv.full(), win[e, :, 1024 + pg * 256:1024 + (pg + 1) * 256], "(c p) n -> p c n", p=128)
    proj_tm(K, wv, 256, VT)
    wq = K.WL[1].sub(0, 2048, "p (c n) -> p c n", c=8)
    wload(K, wq.full(), win[e, :, pg * 256:(pg + 1) * 256], "(c p) n -> p c n", p=128)
    wout = K.WL[2].sub(0, 2048, "p (a n) -> p a n", a=2)
    wload(K, wout.full(), wo_d[e, pg * 256:(pg + 1) * 256, :], "(a p) n -> p a n", p=128)
    for ch in range(NCH):
        cs = slice(ch * CW, (ch + 1) * CW)
        set_roles(K, **ROLES_PROJ)
        q_proj(K, wq, cs)
        ktop = 4 * ch + 3
        for kt in range(ktop + 1):
            P.tt("dve", K.biasF[:, kt * 8 + 4 * pg:kt * 8 + 4 * pg + 4], K.Ccol[:, kt * 8 + 4 * pg:kt * 8 + 4 * pg + 4],
                 K.Coff[:, 4 * ch * 8 + 4 * pg:4 * ch * 8 + 4 * pg + 4], ALU.subtract)
        set_roles(K, **ROLES_ATT)
        obs = [bank(K, "O"), bank(K, "O")]
        dbs = [bank(K, "Dn"), bank(K, "Dn")]

        def stream(jp, half):
            hs = slice(half * 64, half * 64 + 64)
            h = 4 * pg + 2 * jp + half
            ob, db = obs[jp], dbs[jp]
            for kt in range(ktop + 1):
                n0 = max(0, kt - 4 * ch) * 128
                N = CW - n0
                diag = kt >= 4 * ch
                y = bank(K, "Y")
                P.mm(y[:, :N], KT[hs, jp, kt * 128:(kt + 1) * 128], K.qT[hs, jp, n0:CW], start=True, stop=not diag)
                if diag:
                    P.mm(y[:, 0:128], K.ident.full(), K.caus.full(), start=False, stop=True)
                A = pt(K)
                P.act(A[:, :N], y[:, :N], AF.Exp, bias=K.biasF[:, kt * 8 + h:kt * 8 + h + 1])
                yield
                P.mm(ob[hs, n0:CW], VT[:, kt, (2 * jp + half) * 64:(2 * jp + half + 1) * 64], A[:, :N],
                     start=(kt == 0), stop=(kt == ktop))
                P.mm(db[hs, n0:CW], K.ones[:, 0:64], A[:, :N], start=(kt == 0), stop=(kt == ktop))
                yield

        run_interleaved([stream(jp, half) for jp in range(2) for half in range(2)], offsets=[0, 1, 0, 1])
        for jp in range(2):
            rd = fw(K)
            P.recip(rd.full(), dbs[jp].full())
            P.tt("dve", K.oT[:, jp, :], obs[jp].full(), rd.full(), ALU.mult)
        set_roles(K, **ROLES_PROJ)
        out_proj(K, wout, cs)


def nsa_setup(K):
    P, d = K.P, K.d
    K.expb = P.sbuf("expb", [32, 16, 128], BF16)
    K.ovl = P.sbuf("ovl", [127, 33], BF16)
    K.kb = P.sbuf("kb", [128, 64], F32)
    K.ab = P.sbuf("ab", [128, 64], F32)
    K.sel12 = P.sbuf("sel12", [12, 6, 128], BF16)
    K.chcol = P.sbuf("chcol", [128, 8], F32)
    K.tbm = P.sbuf("tbm", [128, 8, 256], BF16)
    K.wbb = P.sbuf("wbb", [128, 8, 384], BF16)
    K.w2c = P.sbuf("w2c", [128, 64], BF16)
    K.peT = P.sbuf("peT", [128, 32], BF16)
    K.cKV = P.sbuf("cKV", [128, 1], F32)
    K.xg = P.sbuf("xg", [128, 127], F32)
    K.x2 = P.sbuf("x2", [128, 127], F32)
    K.hg = P.sbuf("hg", [128, 127], BF16)
    K.kcd = P.sbuf("kcd", [128, 127], BF16)
    K.vcs = P.sbuf("vcs", [127, 64], BF16)
    K.sacc = P.sbuf("sacc", [128, 4, 32], F32)
    K.fin = P.sbuf("fin", [128, 32], F32)
    K.top8 = P.sbuf("top8", [128, 8], F32)
    K.negm = P.sbuf("negm", [128, 4, 32], F32)
    K.rden4 = P.sbuf("rden4", [128, 4], F32)
    K.NEGT = [P.sbuf("negT%d" % i, [32, CW], BF16) for i in range(1)]
    K.CB = [P.sbuf("cb%d" % i, [127, CW], BF16) for i in range(2)]
    K.Gf = P.sbuf("Gf", [12, CW], F32)
    K.Ghi = P.sbuf("Ghi", [12, CW], BF16)
    K.Glo = P.sbuf("Glo", [12, CW], BF16)
    wload(K, K.expb.full(), d["c_expb"])
    wload(K, K.ovl.full(), d["c_ovl"])
    wload(K, K.sel12.full(), d["c_sel12"])
    wload(K, K.wbb.full(), d["t_wb"])
    P.dma("sp", K.kb.full(), V_(d["c_kb"]))
    P.dma("sp", K.ab.full(), V_(d["c_ab"]))
    P.dma("sp", K.chcol.full(), V_(d["t_ch"]))
    tbf = d["t_tb"].rearrange("p h a b -> p (h a b)")
    for hp in range(4):
        tmp = fw(K)
        P.dma("sp", tmp.full(), V_(tbf[:, hp * 512:(hp + 1) * 512]))
        for i in range(2):
            h = 2 * hp + i
            P.ts("dve", K.tbm[:, h, :], tmp[:, i * 256:(i + 1) * 256], K.chcol[:, h:h + 1], ALU.subtract)


def nsa_combine(K, jp, br, ob, db, guard):
    P = K.P
    gb = bank(K, "M")
    P.mm(gb.full(), K.sel12[:, jp * 3 + br, :], K.Ghi.full(), start=True, stop=False)
    P.mm(gb.full(), K.sel12[:, jp * 3 + br, :], K.Glo.full(), start=False, stop=True)
    rd = fw(K)
    if guard:
        P.ts("dve", rd.full(), db.full(), 1e-30, ALU.max)
        P.recip(rd.full(), rd.full())
    else:
        P.recip(rd.full(), db.full())
    P.tt("dve", rd.full(), gb.full(), rd.full(), ALU.mult)
    if br == 0:
        P.tt("dve", K.ACC[jp].full(), ob.full(), rd.full(), ALU.mult)
    else:
        tmp = fw(K)
        P.tt("dve", tmp.full(), ob.full(), rd.full(), ALU.mult)
        dst = K.oT[:, jp, :] if br == 2 else K.ACC[jp].full()
        P.tt("dve", dst, K.ACC[jp].full(), tmp.full(), ALU.add)


def nsa_pass(K, e, g):
    P, d = K.P, K.d
    if not hasattr(K, "expb"):
        nsa_setup(K)
    win, wo_d = d["w_in_even"], d["w_out_even"]
    KT = K.KTf.sub(0, 2 * T, "p (a b) -> p a b", a=2)
    VT = K.VTf.sub(0, 2048, "p (a b) -> p a b", a=16)
    CR = K.VTf.sub(2048, 4096)

    def dup_tile(ca, cb_):
        w = ws(K).sub(0, 1024, "p (c n) -> p c n", c=8)
        wload(K, w[:, :, 0:64], win[e, :, ca:ca + 64], "(c p) n -> p c n", p=128)
        wload(K, w[:, :, 64:128], win[e, :, cb_:cb_ + 64], "(c p) n -> p c n", p=128)
        return w

    ksl0, kwn0, kc0, vc0 = 2312 + g * 64, 2568 + g * 64, 2056 + g * 64, 2184 + g * 64
    vsl0, vwn0 = 2440 + g * 64, 2696 + g * 64
    proj_fm(K, dup_tile(ksl0, ksl0), slice(0, 128), lambda cs: KT[:, 0, cs])
    proj_fm(K, dup_tile(kwn0, kwn0), slice(0, 128), lambda cs: KT[:, 1, cs])
    proj_fm(K, dup_tile(kc0, vc0), slice(0, 128), lambda cs: CR[:, cs])
    proj_tm(K, dup_tile(vsl0, vwn0), 128, VT)
    W1 = ws(K).sub(0, 2048, "p (l e) -> p l e", l=32)
    wload(K, W1[0:64], d["cmp_w1_k"][e], "l d e -> d l e")
    wload(K, W1[64:128], d["cmp_w1_v"][e], "l d e -> d l e")
    wload(K, K.w2c[0:64, :], d["cmp_w2_k"][e])
    wload(K, K.w2c[64:128, :], d["cmp_w2_v"][e])
    wload(K, K.peT[0:64, :], d["cmp_peT_k"][e])
    wload(K, K.peT[64:128, :], d["cmp_peT_v"][e])
    hb = bank(K, "M")
    for half in range(2):
        hs = slice(half * 64, half * 64 + 64)
        for l in range(32):
            P.mm(hb[hs, 0:127], W1[hs, l, :], CR[hs, l:l + 16 * 126 + 1:16], start=(l == 0), stop=(l == 31))
        for l in range(32):
            P.mm(hb[hs, 128:129], W1[hs, l, :], K.peT[hs, l:l + 1], start=(l == 0), stop=(l == 31))
    P.copy("dve", K.cKV.full(), hb[:, 128:129])
    P.act(K.xg.full(), hb[:, 0:127], AF.Identity, bias=K.cKV[:, 0:1])
    P.tt("dve", K.x2.full(), K.xg.full(), K.xg.full(), ALU.mult)
    P.ts("dve", K.x2.full(), K.x2.full(), 0.044715, ALU.mult, 1.0, ALU.add)
    P.tt("dve", K.x2.full(), K.x2.full(), K.xg.full(), ALU.mult)
    P.act(K.x2.full(), K.x2.full(), AF.Sigmoid, scale=1.5957691216057308)
    P.tt("dve", K.hg.full(), K.x2.full(), K.xg.full(), ALU.mult)
    kb_ = bank(K, "M")
    P.mm(kb_[0:64, 0:127], K.w2c[0:64, :], K.hg[0:64, :])
    P.mm(kb_[64:128, 0:127], K.w2c[0:64, :], K.hg[0:64, :])
    P.copy("dve", K.kcd.full(), kb_[:, 0:127])
    vb_ = bank(K, "M")
    P.mm(vb_[0:127, 0:64], K.hg[64:128, :], K.w2c[64:128, :])
    P.copy("dve", K.vcs.full(), vb_[0:127, 0:64])
    wg = K.WL[0].sub(0, 96, "p (c n) -> p c n", c=8)
    wload(K, wg.full(), win[e, :, 2824 + g * 12:2824 + (g + 1) * 12], "(c p) n -> p c n", p=128)
    wq = K.WL[1].sub(0, 2048, "p (c n) -> p c n", c=8)
    wload(K, wq.full(), win[e, :, 1544 + g * 256:1544 + (g + 1) * 256], "(c p) n -> p c n", p=128)
    wout = K.WL[2].sub(0, 2048, "p (a n) -> p a n", a=2)
    wload(K, wout.full(), wo_d[e, 512 + g * 256:512 + (g + 1) * 256, :], "(a p) n -> p a n", p=128)
    for ch in range(NCH):
        cs = slice(ch * CW, (ch + 1) * CW)
        ktop = 4 * ch + 3
        set_roles(K, **ROLES_PROJ)
        q_proj(K, wq, cs)
        gb_ = bank(K, "M")
        for c in range(NCK):
            P.mm(gb_[0:12, :], wg[:, c, :], K.hT[:, c, cs], start=(c == 0), stop=(c == NCK - 1))
        P.act(K.Gf.full(), gb_[0:12, :], AF.Sigmoid)
        P.copy("dve", K.Ghi.full(), K.Gf.full())
        P.tt("dve", K.Glo.full(), K.Gf.full(), K.Ghi.full(), ALU.subtract)
        set_roles(K, **ROLES_ATT)
        obs = [bank(K, "O"), bank(K, "O")]
        dbs = [bank(K, "Dn"), bank(K, "Dn")]
        for jp in range(2):
            for half in range(2):
                hs = slice(half * 64, half * 64 + 64)
                hh = 4 * g + 2 * jp + half
                ob, db = obs[jp], dbs[jp]
                cb = rot(K, "CB", K.CB)
                wload(K, cb.full(), d["t_cb"][hh, :, cs])
                y = bank(K, "Y")
                P.mm(y[0:127, :], K.kcd[hs, :], K.qT[hs, jp, :], start=True, stop=False)
                P.mm(y[0:127, :], K.ident[0:127, 0:127], cb.full(), start=False, stop=True)
                Pc = pt(K)
                P.act(Pc[0:127, :], y[0:127, :], AF.Exp)
                P.mm(ob[hs, :], K.vcs.full(), Pc[0:127, :])
                P.mm(db[hs, :], K.ones[0:127, 0:64], Pc[0:127, :])
                s4 = bank(K, "M")
                for qi in range(4):
                    P.mm(s4[:, qi * 33:(qi + 1) * 33], Pc[0:127, qi * 128:(qi + 1) * 128], K.ovl.full())
                s4v = Buf(s4.h[:, 0:132].rearrange("p (a b) -> p a b", a=4), s4.name, [128, 4, 33], F32, region=s4.region)
                P.ts("dve", K.rden4.full(), s4v[:, :, 32], 1e-30, ALU.max)
                P.recip(K.rden4.full(), K.rden4.full())
                for qi in range(4):
                    if jp == 0 and half == 0:
                        P.ts("dve", K.sacc[:, qi, :], s4[:, qi * 33:qi * 33 + 32], K.rden4[:, qi:qi + 1], ALU.mult)
                    else:
                        P.stt("dve", K.sacc[:, qi, :], s4[:, qi * 33:qi * 33 + 32], K.rden4[:, qi:qi + 1],
                              K.sacc[:, qi, :], ALU.mult, ALU.add)
        for jp in range(2):
            nsa_combine(K, jp, 0, obs[jp], dbs[jp], guard=True)
        tps = bank(K, "M")
        for qi in range(4):
            lo = 32 - 2 * (4 * ch + qi)
            P.tt("dve", K.fin.full(), K.sacc[:, qi, :], K.kb[:, lo:lo + 32], ALU.mult)
            P.tt("dve", K.fin.full(), K.fin.full(), K.ab[:, lo:lo + 32], ALU.add)
            P.memset("dve", K.fin[:, 0:1], 1e4)
            P.add("dve", lambda e_: e_.max(out=K.top8.full().ap, in_=K.fin.full().ap), reads=[K.fin.full()],
                  writes=[K.top8.full()])
            P.ts("dve", K.negm[:, qi, :], K.fin.full(), K.top8[:, 7:8], ALU.is_lt, NEGM, ALU.mult)
            P.transpose(tps[0:32, qi * 128:(qi + 1) * 128], K.negm[:, qi, :], K.identf.full())
        negT = rot(K, "NEGT", K.NEGT)
        P.copy("dve", negT.full(), tps[0:32, :])
        obs = [bank(K, "O"), bank(K, "O")]
        dbs = [bank(K, "Dn"), bank(K, "Dn")]

        def slc_stream(jp, half):
            hs = slice(half * 64, half * 64 + 64)
            hh = 4 * g + 2 * jp + half
            ob, db = obs[jp], dbs[jp]
            for kt in range(ktop + 1):
                n0 = max(0, kt - 4 * ch) * 128
                N = CW - n0
                y = bank(K, "Y")
                P.mm(y[:, :N], KT[hs, 0, kt * 128:(kt + 1) * 128], K.qT[hs, jp, n0:CW], start=True, stop=False)
                if kt >= 4 * ch:
                    P.mm(y[:, 0:128], K.ident.full(), K.tbm[:, hh, 0:128], start=False, stop=False)
                if 4 * ch <= kt + 1 <= ktop:
                    o1 = (kt + 1 - 4 * ch) * 128 - n0
                    P.mm(y[:, o1:o1 + 128], K.ident.full(), K.tbm[:, hh, 128:256], start=False, stop=False)
                P.mm(y[:, :N], K.expb[:, kt, :], negT[:, n0:CW], start=False, stop=True)
                A = pt(K)
                P.act(A[:, :N], y[:, :N], AF.Exp, bias=K.chcol[:, hh:hh + 1])
                yield
                P.mm(ob[hs, n0:CW], VT[:, kt, 0:64], A[:, :N], start=(kt == 0), stop=(kt == ktop))
                P.mm(db[hs, n0:CW], K.ones[:, 0:64], A[:, :N], start=(kt == 0), stop=(kt == ktop))
                yield

        run_interleaved([slc_stream(jp, half) for jp in range(2) for half in range(2)])
        for jp in range(2):
            nsa_combine(K, jp, 1, obs[jp], dbs[jp], guard=False)
        obs = [bank(K, "O"), bank(K, "O")]
        dbs = [bank(K, "Dn"), bank(K, "Dn")]
        for jp in range(2):
            P.mm(obs[jp].full(), K.zeros.full(), K.qT[:, jp, :], start=True, stop=False)
            P.mm(dbs[jp].full(), K.zeros.full(), K.qT[:, jp, :], start=True, stop=False)

        def win_stream(jp, half):
            hs = slice(half * 64, half * 64 + 64)
            hh = 4 * g + 2 * jp + half
            ob, db = obs[jp], dbs[jp]
            kts = list(range(max(0, 4 * ch - 2), ktop + 1))
            for kt in kts:
                blo, bhi = max(kt, 4 * ch), min(kt + 2, ktop)
                c0, c1 = (blo - 4 * ch) * 128, (bhi - 4 * ch + 1) * 128
                Nw = c1 - c0
                y = bank(K, "Y")
                P.mm(y[:, :Nw], KT[hs, 1, kt * 128:(kt + 1) * 128], K.qT[hs, jp, c0:c1], start=True, stop=False)
                P.mm(y[:, :Nw], K.ident.full(), K.wbb[:, hh, (blo - kt) * 128:(bhi - kt + 1) * 128], start=False, stop=True)
                A = pt(K)
                P.act(A[:, :Nw], y[:, :Nw], AF.Exp)
                yield
                last = kt == kts[-1]
                P.mm(ob[hs, c0:c1], VT[:, kt, 64:128], A[:, :Nw], start=False, stop=last)
                P.mm(db[hs, c0:c1], K.ones[:, 0:64], A[:, :Nw], start=False, stop=last)
                yield

        run_interleaved([win_stream(jp, half) for jp in range(2) for half in range(2)])
        for jp in range(2):
            nsa_combine(K, jp, 2, obs[jp], dbs[jp], guard=False)
        set_roles(K, **ROLES_PROJ)
        out_proj(K, wout, cs)


def mixer_phase(K, layer):
    rmsnorm(K, K.xT, layer, K.hT, T)
    if layer % 2 == 1:
        for pg in range(4):
            sb_pass(K, layer // 2, pg)
    else:
        e = layer // 2
        fox_prep(K, e)
        for pg in range(2):
            fox_pass(K, e, pg)
        for g in range(2):
            nsa_pass(K, e, g)


def build(phases, final_norm=True, with_mem=True):
    nc = bass.Bass("TRN2", target_bir_lowering=False)
    K = setup(nc)
    load_consts(K)
    setup_mixer(K)
    if with_mem:
        prep_mem(K)
    load_T(K, K.d["x"], K.xT, T)
    for kind, layer in phases:
        if kind == "mixer":
            mixer_phase(K, layer)
        elif kind == "fox":
            rmsnorm(K, K.xT, layer, K.hT, T)
            fox_prep(K, layer // 2)
            for pg in range(2):
                fox_pass(K, layer // 2, pg)
        elif kind == "nsa":
            rmsnorm(K, K.xT, layer, K.hT, T)
            for g in range(2):
                nsa_pass(K, layer // 2, g)
        elif kind == "xattn":
            xattn_phase(K, layer)
        elif kind == "mlp":
            mlp_phase(K, layer)
    if final_norm:
        rmsnorm(K, K.xT, 12, K.xT, T)
    store_T(K, K.xT, K.d["out"])
    K.P.finish()
    return nc, K


def t5_bucket_np(dist):
    dist = np.maximum(dist, 0)
    d_f = np.maximum(dist, 1).astype(np.float32)
    large = 16 + (np.log(d_f / np.float32(16)) / np.float32(np.log(128 / 16)) * np.float32(16)).astype(np.int32)
    large = np.minimum(large, 31)
    return np.where(dist < 16, dist, large)


def host_consts(inp):
    f = np.float32
    c = {}
    gl = [inp["norm_mix_g"][i] for i in range(4)] + [inp["norm_xattn_g"][i] for i in range(4)] + \
         [inp["norm_mlp_g"][i] for i in range(4)] + [inp["final_norm_g"], inp["mem_norm_g"]]
    c["gains"] = np.ascontiguousarray(np.stack(gl, 0).reshape(14, 8, 128).transpose(2, 0, 1)).astype(f)
    i = np.arange(128)
    c["c_ident"] = np.eye(128, dtype=f)
    c["c_caus"] = np.where(i[None, :] >= i[:, None], 0.0, NEGM).astype(f)
    c["c_strict"] = np.where(i[None, :] > i[:, None], 0.0, NEGM).astype(f)
    c["c_negu"] = np.where(i[:, None] >= i[None, :], -1.0, 0.0).astype(f)
    c["c_usum"] = np.where(i[:, None] <= i[None, :], 1.0, 0.0).astype(f)
    n = np.arange(32)
    kt = np.arange(16)
    s_ = np.arange(128)
    c["c_expb"] = (n[:, None, None] == (kt[None, :, None] * 128 + s_[None, None, :]) // 64).astype(f)
    cc = np.arange(127)
    cs0 = cc * 16
    ce = cs0 + 31
    ss0 = n * 64
    se = ss0 + 63
    ovl = ((cs0[:, None] <= se[None, :]) & (ce[:, None] >= ss0[None, :])).astype(f)
    c["c_ovl"] = np.concatenate([ovl, np.ones((127, 1), f)], 1)
    q = np.arange(128)
    hi = (q >= 64).astype(np.int64)[:, None]
    m = np.arange(64)[None, :]
    forced = (m == 32 + hi) | (m == 31 + hi)
    future = m > 32 + hi
    c["c_kb"] = np.where(forced | future, 0.0, 1.0).astype(f)
    c["c_ab"] = np.where(forced, 1e4, np.where(future, -1.0, 0.0)).astype(f)
    sel = np.zeros((12, 6, 128), f)
    for jp in range(2):
        for half in range(2):
            for br in range(3):
                sel[(2 * jp + half) * 3 + br, jp * 3 + br, half * 64:(half + 1) * 64] = 1.0
    c["c_sel12"] = sel
    rb = inp["rel_bias"].astype(f)
    tb = np.zeros((128, 8, 2, 128), f)
    for dl in range(2):
        dist = dl * 128 + s_[None, :] - s_[:, None]
        val = rb[t5_bucket_np(dist)]
        val = np.where((dist >= 0)[:, :, None], val, NEGM)
        tb[:, :, dl, :] = val.transpose(0, 2, 1)
    c["t_tb"] = tb
    wb = np.zeros((128, 8, 384), f)
    for dl in range(3):
        dist = dl * 128 + s_[None, :] - s_[:, None]
        val = rb[t5_bucket_np(dist)]
        ok = (dist >= 0) & (dist < 256)
        val = np.where(ok[:, :, None], val, NEGM)
        wb[:, :, dl * 128:(dl + 1) * 128] = val.transpose(0, 2, 1)
    c["t_wb"] = wb
    tq = np.arange(T)
    dist = tq[None, :] - ce[:, None]
    val = rb[t5_bucket_np(dist)]
    val = np.where((dist >= 0)[:, :, None], val, NEGM)
    c["t_cb"] = np.ascontiguousarray(val.transpose(2, 0, 1)).astype(f)
    c["t_ch"] = np.ascontiguousarray(np.broadcast_to(rb[31][None, :], (128, 8))).astype(f)
    c["bf_rep"] = np.ascontiguousarray(np.broadcast_to(np.tile(inp["b_forget"].astype(f), (1, 16))[:, None, :], (2, 128, 128)))
    c["cmp_peT_k"] = np.ascontiguousarray(inp["cmp_pe_k"].transpose(0, 2, 1)).astype(f)
    c["cmp_peT_v"] = np.ascontiguousarray(inp["cmp_pe_v"].transpose(0, 2, 1)).astype(f)
    return c


PASS_KEYS = ["w_in_even", "w_out_even", "w_in_odd", "w_out_odd", "xa_wq", "xa_wkv", "xa_wo", "mlp_w1", "mlp_w2",
             "b_forget", "cmp_w1_k", "cmp_w1_v", "cmp_w2_k", "cmp_w2_v"]


def make_in_maps(inp, xs, ncores):
    c = host_consts(inp)
    base = {k: np.ascontiguousarray(inp[k], dtype=np.float32) for k in PASS_KEYS}
    base.update(c)
    maps = []
    for b in range(ncores):
        m = dict(base)
        m["x"] = np.ascontiguousarray(xs[b], dtype=np.float32)
        m["mem"] = np.ascontiguousarray(inp["mem"][b], dtype=np.float32)
        maps.append(m)
    return maps


_CACHE = {}


def kernel(**inputs):
    phases = []
    for layer in range(4):
        phases += [("mixer", layer), ("xattn", layer), ("mlp", layer)]
    if "nc" not in _CACHE:
        _CACHE["nc"] = build(phases, final_norm=True)[0]
    nc = _CACHE["nc"]
    n = 8
    maps = make_in_maps(inputs, inputs["x"], n)
    res = run_bass_kernel_spmd(nc, maps, core_ids=list(range(n)))
    return np.stack([np.asarray(r["out"], dtype=np.float32) for r in res.results], 0)
```
